# Optimizing a Trainium2 kernel written in Bass

```python
import jax, jax.numpy as jnp
from jax import lax
import numpy as np

D_MODEL = 1024
BATCH = 8
SEQ = 2048
DEPTH = 4

ATT_HEADS = 16
ATT_KV_HEADS = 4
ATT_GROUP = ATT_HEADS // ATT_KV_HEADS
ATT_HEAD_DIM = 64
WINDOW = 128
DN_HEADS = 8
DN_HEAD_DIM = 128
CONV_K = 4
CHUNK = 64
D_FF = 4 * D_MODEL
ATT_Q_W = ATT_HEADS * ATT_HEAD_DIM
ATT_KV_W = ATT_KV_HEADS * ATT_HEAD_DIM
DN_W = DN_HEADS * DN_HEAD_DIM
IN_SPLITS = (ATT_Q_W, ATT_KV_W, ATT_KV_W, DN_W, DN_W, DN_W, DN_W, DN_HEADS, DN_HEADS, D_MODEL, D_MODEL)
D_IN = ATT_Q_W + 2 * ATT_KV_W + 4 * DN_W + 2 * DN_HEADS + 2 * D_MODEL
ALPHA = (2 * DEPTH) ** 0.25
BETA_INIT = (8 * DEPTH) ** -0.25
LN_EPS = 1e-5
RMS_EPS = 1e-6
ADA_SCALE = 0.2

kernel_name = "hybrid_swa_sink_gdn_parallel_deepnorm_adaln"


def _split_in(p):
    outs = []
    off = 0
    for w in IN_SPLITS:
        outs.append(p[..., off:off + w])
        off += w
    return outs


def _layer_norm(x, g, b):
    xf = x.astype(jnp.float32)
    mu = jnp.mean(xf, axis=-1, keepdims=True)
    var = jnp.mean(jnp.square(xf - mu), axis=-1, keepdims=True)
    return ((xf - mu) * lax.rsqrt(var + LN_EPS) * g.astype(jnp.float32) + b.astype(jnp.float32)).astype(x.dtype)


def _l2norm(t):
    return t * lax.rsqrt(jnp.sum(jnp.square(t), axis=-1, keepdims=True) + RMS_EPS)


def _causal_conv_silu(x, w):
    s = x.shape[1]
    xp = jnp.pad(x, ((0, 0), (CONV_K - 1, 0), (0, 0)))
    y = sum(xp[:, j:j + s] * w[j] for j in range(CONV_K))
    return jax.nn.silu(y)


def _sliding_window_attention(q, k, v, sinks):
    b, s, _ = q.shape
    nb = s // WINDOW
    qb = q.reshape(b, nb, WINDOW, ATT_KV_HEADS, ATT_GROUP, ATT_HEAD_DIM)

    def band(t):
        tp = jnp.pad(t, ((0, 0), (WINDOW, 0), (0, 0)))
        tb = tp.reshape(b, nb + 1, WINDOW, ATT_KV_HEADS, ATT_HEAD_DIM)
        return jnp.concatenate([tb[:, :-1], tb[:, 1:]], axis=2)

    kb, vb = band(k), band(v)
    scores = jnp.einsum('bnqhgd,bnshd->bnhgqs', qb, kb).astype(jnp.float32) * (ATT_HEAD_DIM ** -0.5)
    qi = jnp.arange(WINDOW)[:, None]
    si = jnp.arange(2 * WINDOW)[None, :]
    diff = qi + WINDOW - si
    blk = jnp.arange(nb)[:, None, None]
    valid = (diff >= 0) & (diff < WINDOW) & (blk * WINDOW + si - WINDOW >= 0)
    scores = jnp.where(valid[None, :, None, None], scores, -jnp.inf)
    sink = sinks.astype(jnp.float32).reshape(1, 1, ATT_KV_HEADS, ATT_GROUP, 1, 1)
    m = jnp.maximum(jnp.max(scores, axis=-1, keepdims=True), sink)
    p = jnp.exp(scores - m)
    denom = jnp.sum(p, axis=-1, keepdims=True) + jnp.exp(sink - m)
    probs = (p / denom).astype(v.dtype)
    o = jnp.einsum('bnhgqs,bnshd->bnqhgd', probs, vb)
    return o.reshape(b, s, ATT_Q_W)


def _gated_delta_rule(q, k, v, beta, g):
    b, s, h, d = q.shape
    n = s // CHUNK

    def chunks(t):
        t = t.reshape((b, n, CHUNK, h) + t.shape[3:])
        return jnp.moveaxis(t, 3, 1)

    q, k, v, beta, g = (chunks(t) for t in (q, k, v, beta, g))
    g_cum = jnp.cumsum(g, axis=-1)
    causal = jnp.tril(jnp.ones((CHUNK, CHUNK), dtype=bool))
    strict = jnp.tril(jnp.ones((CHUNK, CHUNK), dtype=bool), -1)
    decay = jnp.exp(jnp.where(causal, g_cum[..., :, None] - g_cum[..., None, :], -jnp.inf))
    kb = k * beta[..., None]
    vb = v * beta[..., None]
    l_mat = jnp.where(strict, jnp.einsum('bhncd,bhnsd->bhncs', kb, k) * decay, 0.0)
    a_mat = l_mat + jnp.eye(CHUNK, dtype=l_mat.dtype)
    rhs = jnp.concatenate([vb, kb * jnp.exp(g_cum)[..., None]], axis=-1)
    sol = lax.linalg.triangular_solve(a_mat, rhs, left_side=True, lower=True, unit_diagonal=True)
    u, w = sol[..., :d], sol[..., d:]
    intra = jnp.einsum('bhncd,bhnsd->bhncs', q, k) * decay
    q_dec = q * jnp.exp(g_cum)[..., None]
    k_dec = k * jnp.exp(g_cum[..., -1:] - g_cum)[..., None]
    last = jnp.exp(g_cum[..., -1])
    xs = tuple(jnp.moveaxis(t, 2, 0) for t in (u, w, intra, q_dec, k_dec, last))

    def step(state, inp):
        u_c, w_c, intra_c, q_c, k_c, last_c = inp
        v_new = u_c - jnp.einsum('bhck,bhkv->bhcv', w_c, state)
        o_c = jnp.einsum('bhck,bhkv->bhcv', q_c, state) + jnp.einsum('bhcs,bhsv->bhcv', intra_c, v_new)
        state = state * last_c[..., None, None] + jnp.einsum('bhck,bhcv->bhkv', k_c, v_new)
        return state, o_c

    s0 = jnp.zeros((b, h, d, d), jnp.float32)
    _, o = lax.scan(step, s0, xs)
    return jnp.transpose(o, (1, 0, 3, 2, 4)).reshape(b, s, h, d)


def _gated_deltanet(dq, dk, dv, z, b_raw, a_raw, conv_w, a_log, dt_bias, norm_w):
    bsz, s, _ = dq.shape
    qkv = _causal_conv_silu(jnp.concatenate([dq, dk, dv], axis=-1), conv_w)
    shp = (bsz, s, DN_HEADS, DN_HEAD_DIM)
    q = _l2norm(qkv[..., :DN_W].reshape(shp).astype(jnp.float32)) * (DN_HEAD_DIM ** -0.5)
    k = _l2norm(qkv[..., DN_W:2 * DN_W].reshape(shp).astype(jnp.float32))
    v = qkv[..., 2 * DN_W:].reshape(shp).astype(jnp.float32)
    beta = jax.nn.sigmoid(b_raw.astype(jnp.float32))
    g = -jnp.exp(a_log.astype(jnp.float32)) * jax.nn.softplus(a_raw.astype(jnp.float32) + dt_bias.astype(jnp.float32))
    o = _gated_delta_rule(q, k, v, beta, g)
    o = o * lax.rsqrt(jnp.mean(jnp.square(o), axis=-1, keepdims=True) + RMS_EPS) * norm_w.astype(jnp.float32)
    o = o * jax.nn.silu(z.reshape(shp).astype(jnp.float32))
    return o.reshape(bsz, s, DN_W).astype(dq.dtype)


def setup_inputs(seed: int = 0) -> dict:
    key = jax.random.key(seed)
    ks = jax.random.split(key, 24)
    nrm = jax.random.normal
    L, D = DEPTH, D_MODEL
    dt = jnp.exp(jax.random.uniform(ks[8], (L, DN_HEADS), minval=np.log(1e-3), maxval=np.log(1e-1)))
    return {
        "x": nrm(ks[0], (BATCH, SEQ, D), jnp.float32),
        "c": nrm(ks[1], (BATCH, D), jnp.float32),
        "w_ada": nrm(ks[2], (L, D, 6 * D), jnp.float32) * (ADA_SCALE * D ** -0.5),
        "b_ada": nrm(ks[3], (L, 6 * D), jnp.float32) * 0.01,
        "w_in": nrm(ks[4], (L, D, D_IN), jnp.float32) * D ** -0.5,
        "conv_w": nrm(ks[5], (L, CONV_K, 3 * DN_W), jnp.float32) * CONV_K ** -0.5,
        "a_log": jnp.log(jax.random.uniform(ks[6], (L, DN_HEADS), minval=1.0, maxval=16.0)),
        "dt_bias": dt + jnp.log(-jnp.expm1(-dt)),
        "sinks": nrm(ks[7], (L, ATT_HEADS), jnp.float32),
        "dn_norm_w": 1.0 + 0.02 * nrm(ks[9], (L, DN_HEAD_DIM), jnp.float32),
        "w_oa": nrm(ks[10], (L, ATT_Q_W, D), jnp.float32) * ATT_Q_W ** -0.5,
        "w_ob": nrm(ks[11], (L, DN_W, D), jnp.float32) * DN_W ** -0.5,
        "w_out": nrm(ks[12], (L, D, D), jnp.float32) * (BETA_INIT * D ** -0.5),
        "ln1_g": 1.0 + 0.02 * nrm(ks[13], (L, D), jnp.float32),
        "ln1_b": 0.02 * nrm(ks[14], (L, D), jnp.float32),
        "w_ff1": nrm(ks[15], (L, D, D_FF), jnp.float32) * D ** -0.5,
        "b_ff1": 0.02 * nrm(ks[16], (L, D_FF), jnp.float32),
        "w_ff2": nrm(ks[17], (L, D_FF, D), jnp.float32) * (BETA_INIT * D_FF ** -0.5),
        "b_ff2": 0.02 * nrm(ks[18], (L, D), jnp.float32),
        "ln2_g": 1.0 + 0.02 * nrm(ks[19], (L, D), jnp.float32),
        "ln2_b": 0.02 * nrm(ks[20], (L, D), jnp.float32),
    }


def reference(x, c, w_ada, b_ada, w_in, conv_w, a_log, dt_bias, sinks, dn_norm_w, w_oa, w_ob, w_out,
              ln1_g, ln1_b, w_ff1, b_ff1, w_ff2, b_ff2, ln2_g, ln2_b):
    c_act = jax.nn.silu(c)
    for l in range(DEPTH):
        mod = c_act @ w_ada[l] + b_ada[l]
        sh1, sc1, gt1, sh2, sc2, gt2 = jnp.split(mod[:, None, :], 6, axis=-1)
        u = x * (1.0 + sc1) + sh1
        proj = u @ w_in[l]
        qa, ka, va, dq, dk, dv, z, b_raw, a_raw, g_a, g_b = _split_in(proj)
        y_a = _sliding_window_attention(qa, ka, va, sinks[l]) @ w_oa[l]
        y_b = _gated_deltanet(dq, dk, dv, z, b_raw, a_raw, conv_w[l], a_log[l], dt_bias[l], dn_norm_w[l]) @ w_ob[l]
        mixed = (jax.nn.sigmoid(g_a) * y_a + jax.nn.sigmoid(g_b) * y_b) @ w_out[l]
        x = _layer_norm(ALPHA * x + (1.0 + gt1) * mixed, ln1_g[l], ln1_b[l])
        u2 = x * (1.0 + sc2) + sh2
        h = jnp.square(jax.nn.relu(u2 @ w_ff1[l] + b_ff1[l]))
        x = _layer_norm(ALPHA * x + (1.0 + gt2) * (h @ w_ff2[l] + b_ff2[l]), ln2_g[l], ln2_b[l])
    return x
```

```python
import contextlib
import os
import numpy as np
CUT = int(os.environ.get('K_CUT', '99'))
import concourse.bass as bass
import concourse.mybir as mybir
from concourse.bass_utils import run_bass_kernel_spmd

F32 = mybir.dt.float32
BF16 = mybir.dt.bfloat16
ALU = mybir.AluOpType
AF = mybir.ActivationFunctionType

ALL_ENG = ("sync", "tensor", "vector", "scalar", "gpsimd")
D = 1024
SEQ = 2048
DEPTH = 4
ALPHA = (2 * DEPTH) ** 0.25
LN_EPS = 1e-5
RMS_EPS = 1e-6
NEG = -30000.0


class Buf:
    __slots__ = ("name", "w", "r", "excl")

    def __init__(self, name="", excl=False):
        self.name = name
        self.w = None
        self.r = []
        self.excl = excl


class Prog:
    def __init__(self, nc, same_engine_sync=True):
        self.nc = nc
        self.ops = {e: [] for e in ALL_ENG}
        self.dsem_count = {}
        self.same = same_engine_sync
        self.es = contextlib.ExitStack()
        self.dsem_names = []

    def sb(self, name, shape, dt):
        return self.es.enter_context(self.nc.sbuf_tensor("sb_" + name, list(shape), dt))

    def ps(self, name, shape, dt):
        return self.es.enter_context(self.nc.psum_tensor("ps_" + name, list(shape), dt))

    def dsem(self, name):
        self.dsem_count[name] = 0
        self.dsem_names.append(name)
        return name

    def _deps(self, eng, reads, writes):
        deps = []
        for b in reads:
            if b.w is not None:
                deps.append(b.w)
        for b in writes:
            if b.w is not None:
                deps.append(b.w)
            deps.extend(b.r)
        out = []
        for d in deps:
            if d[0] == "e" and d[1] == eng and (eng in ("tensor", "sync") or not self.same):
                continue
            if d not in out:
                out.append(d)
        return out

    def op(self, eng, fn, reads=(), writes=()):
        writes = list(writes) + [b for b in reads if b.excl]
        reads = [b for b in reads if not b.excl]
        deps = self._deps(eng, reads, writes)
        idx = len(self.ops[eng])
        self.ops[eng].append(dict(fn=fn, deps=deps, sig=False, dma=None))
        ev = ("e", eng, idx)
        for b in reads:
            b.r.append(ev)
        for b in writes:
            b.w = ev
            b.r = []
        return ev

    def dma(self, eng, fn, dsem, reads=(), writes=()):
        deps = self._deps(eng, reads, writes)
        self.dsem_count[dsem] += 16
        ev = ("d", dsem, self.dsem_count[dsem])
        self.ops[eng].append(dict(fn=fn, deps=deps, sig=False, dma=dsem))
        for b in reads:
            b.r.append(ev)
        for b in writes:
            b.w = ev
            b.r = []
        return ev

    def wait_events(self, eng, evs):
        self.ops[eng].append(dict(fn=None, deps=list(evs), sig=False, dma=None))

    def barrier(self, bufs=()):
        evs = []
        for eng in ("tensor", "vector", "scalar", "gpsimd"):
            n = len(self.ops[eng])
            i = n - 1
            while i >= 0 and (self.ops[eng][i]["fn"] is None or self.ops[eng][i]["dma"] is not None):
                i -= 1
            if i >= 0:
                evs.append(("e", eng, i))
        for eng in ("tensor", "vector", "scalar", "gpsimd"):
            self.wait_events(eng, [e for e in evs if e[1] != eng])
        for b in bufs:
            b.r = list(b.r) + evs
        return evs

    def emit(self):
        nc = self.nc
        for eng in ALL_ENG:
            for rec in self.ops[eng]:
                for d in rec["deps"]:
                    if d[0] == "e":
                        self.ops[d[1]][d[2]]["sig"] = True
        cnt = {}
        for eng in ALL_ENG:
            c = 0
            for i, rec in enumerate(self.ops[eng]):
                if rec["sig"]:
                    c += 1
                    cnt[(eng, i)] = c
        sems = {}
        for eng in ALL_ENG:
            sems[eng] = self.es.enter_context(nc.semaphore("s_" + eng))
        dsems = {}
        for n in self.dsem_names:
            dsems[n] = self.es.enter_context(nc.semaphore("d_" + n))
        self.nwaits = 0

        def run(eng, e):
            known = {}
            for rec in self.ops[eng]:
                for d in rec["deps"]:
                    if d[0] == "e":
                        key = ("e", d[1])
                        val = cnt[(d[1], d[2])]
                        sem = sems[d[1]]
                    else:
                        key = ("d", d[1])
                        val = d[2]
                        sem = dsems[d[1]]
                    if known.get(key, 0) >= val:
                        continue
                    known[key] = val
                    e.wait_ge(sem, val)
                    self.nwaits += 1
                if rec["fn"] is None:
                    continue
                ins = rec["fn"](e)
                if rec["dma"] is not None:
                    ins.then_inc(dsems[rec["dma"]], 16)
                elif rec["sig"]:
                    ins.then_inc(sems[eng], 1)

        with nc.Block() as block:
            @block.sync
            def _(e):
                run("sync", e)

            @block.tensor
            def _(e):
                run("tensor", e)

            @block.vector
            def _(e):
                run("vector", e)

            @block.scalar
            def _(e):
                run("scalar", e)

            @block.gpsimd
            def _(e):
                run("gpsimd", e)
        self.es.close()


class Carver:
    def __init__(self, regions):
        self.regions = regions
        self.off = [0] * len(regions)

    def take(self, nelem, dt):
        nb = nelem * (4 if dt == F32 else 2)
        nb = (nb + 63) // 64 * 64
        for i, r in enumerate(self.regions):
            cap = r.shape[1] * 2
            if self.off[i] + nb <= cap:
                o = self.off[i]
                self.off[i] += nb
                ap = r[:, o // 2:(o + nb) // 2]
                if dt == F32:
                    ap = ap.bitcast(F32)
                return ap[:, 0:nelem]
        raise MemoryError(f"carver out of space for {nelem} {dt}")


NPP_L = 48 + 8 * 4 + 32 + 8 + 96 + 16 + 8 + 8 + 128


def pp_off(L):
    o = {}
    p = 8
    for name, n in (("b_ada", 48), ("ln1_g", 8), ("ln1_b", 8), ("ln2_g", 8), ("ln2_b", 8), ("b_ff1", 32),
                    ("b_ff2", 8), ("convw", 96), ("sinks", 16), ("a_log", 8), ("dt_bias", 8), ("normw", 128)):
        o[name] = (p, n)
        p += n * L
    return o, p


def build(L=DEPTH, stop=None, dbg=False):
    nc = bass.Bass("TRN2", target_bir_lowering=False)
    P = Prog(nc, same_engine_sync=not os.environ.get("K_NOSAME"))
    PO, NPP = pp_off(L)

    xT_d = nc.dram_tensor("xT", [128, 8 * SEQ], F32, kind="ExternalInput").ap()
    pp_d = nc.dram_tensor("pp", [128, NPP], F32, kind="ExternalInput").ap()
    cst_d = nc.dram_tensor("cst", [128, 6 * 128], F32, kind="ExternalInput").ap()
    wada_d = nc.dram_tensor("wada", [L, 8, 128, 6144], F32, kind="ExternalInput").ap()
    wA_d = nc.dram_tensor("wA", [L, 4, 128, 8 * 448], F32, kind="ExternalInput").ap()
    wD_d = nc.dram_tensor("wD", [L, 8, 128, 8 * 512], F32, kind="ExternalInput").ap()
    wBA_d = nc.dram_tensor("wBA", [L, 128, 128], F32, kind="ExternalInput").ap()
    wM_d = nc.dram_tensor("wM", [L, 5, 128, 8192], F32, kind="ExternalInput").ap()
    wF_d = nc.dram_tensor("wF", [L, 8, 128, 8192], F32, kind="ExternalInput").ap()
    yT_d = nc.dram_tensor("yT", [128, 8 * SEQ], F32, kind="ExternalOutput").ap()
    if dbg:
        dbg_d = nc.dram_tensor("dbg", [128, 8192], BF16, kind="ExternalOutput").ap()

    xT = P.sb("xTs", [128, 8 * SEQ], F32)
    xB = [[Buf(f"x{c}_{t}") for t in range(4)] for c in range(8)]
    uT = P.sb("uTs", [128, 8192], BF16)
    uB = [[Buf(f"u{c}_{t}") for t in range(2)] for c in range(8)]
    oT = P.sb("oTs", [128, 8192], BF16)
    oB = [[Buf(f"o{c}_{t}") for t in range(2)] for c in range(8)]
    wB = [P.sb(f"wB{i}", [128, 8192], BF16) for i in range(3)]
    wBB = [Buf(f"wB{i}") for i in range(3)]
    wBsem = [P.dsem(f"wB{i}") for i in range(3)]
    wAs = P.sb("wAs", [128, 8192], BF16)
    wAB = [Buf("wA0"), Buf("wA1")]
    wAsem = [P.dsem("wA0"), P.dsem("wA1")]
    wba = P.sb("wba", [128, 128], BF16)
    wbaB = Buf("wba")
    wbasem = P.dsem("wba")
    ppt = P.sb("ppt", [128, NPP], F32)
    ppB = Buf("pp")
    cst = P.sb("cst", [128, 768], F32)
    cstB = Buf("cst")
    cbf = P.sb("cbf", [128, 4 * 128], BF16)
    cbfB = Buf("cbf")
    negones = P.sb("negones", [128, 128], F32)
    mod = P.sb("mod", [128, L * 48], F32)
    sb2 = P.sb("sb2", [128, L * 8], F32)
    epsT = P.sb("epsT", [128, 4], F32)
    cact = P.sb("cact", [128, 8], F32)
    cactb = P.sb("cactb", [128, 8], BF16)
    cactB = Buf("cact")
    kcar = P.sb("kcar", [128, 4 * 128], BF16)
    vcar = P.sb("vcar", [128, 4 * 128], BF16)
    carB = [Buf(f"car{g}") for g in range(4)]
    ccar = P.sb("ccar", [128, 8 * 9], F32)
    ccarB = [Buf(f"cc{h}") for h in range(8)]
    S_f = P.sb("S_f", [128, 8 * 128], F32)
    S_fB = [Buf(f"S{h}") for h in range(8)]
    bet = P.sb("bet", [128, 64], F32)
    nbet = P.sb("nbet", [128, 64], F32)
    gtk = P.sb("gtk", [128, 64], F32)
    gtmp = P.sb("gtmp", [128, 64], F32)
    nega = P.sb("nega", [128, 8], F32)
    sexp = P.sb("sexp", [128, 16], F32)
    bgB = Buf("bg")
    lyrB = Buf("lyr")
    free_ar = P.sb("free_ar", [128, 14336], BF16)

    ident_f = cst[:, 0:128]
    ones_f = cst[:, 128:256]
    triinc = cst[:, 256:384]
    mls = cst[:, 384:512]
    mupi = cst[:, 512:640]
    ident_b = cbf[:, 0:128]
    ones_b = cbf[:, 128:256]
    mcur_b = cbf[:, 256:384]
    mprev_b = cbf[:, 384:512]

    pbank = [P.ps(f"pb{i}", [128, 512], F32) for i in range(8)]
    pbankB = [Buf(f"pb{i}", excl=True) for i in range(8)]
    st = dict(i=0)

    def big():
        lo, hi = st.get("rng", (0, 8))
        key = ("i", lo, hi)
        i = st.get(key, lo)
        st[key] = lo + (i + 1 - lo) % (hi - lo)
        return pbank[i], pbankB[i]

    def quarter():
        t, b = big()
        return t[:, 0:128], b

    def tslot():
        t, b = big()
        return t[:, :].bitcast(BF16)[:, 0:128], b

    def tslot4():
        t, b = big()
        return t[:, :].bitcast(BF16)[:, 0:512], [b]

    def T(fn, r, w):
        return P.op("tensor", fn, r, w)

    def V(fn, r, w):
        return P.op("vector", fn, r, w)

    def A(fn, r, w):
        return P.op("scalar", fn, r, w)

    def G(fn, r, w):
        return P.op("gpsimd", fn, r, w)

    def mm(out, lhsT, rhs, start, stop, r, w):
        return T(lambda e: e.matmul(out, lhsT, rhs, start=start, stop=stop), r, w)

    def act(out, in_, func, r, w, **kw):
        return A(lambda e: e.activation(out=out, in_=in_, func=func, **kw), r, w)

    def u_(k, a, b):
        return uT[:, k * 1024 + a:k * 1024 + b]

    def x_(c, a, b):
        return xT[:, c * SEQ + a:c * SEQ + b]

    def ppc(name, l, j=0, n=1):
        o, per = PO[name]
        return ppt[:, o + l * per + j:o + l * per + j + n]

    def modc(l, w, c=0, n=8):
        o = l * 48 + w * 8 + c
        return mod[:, o:o + n]

    P.dma("sync", lambda e: e.dma_start(out=ppt[:], in_=pp_d), P.dsem("pp"), writes=[ppB])
    P.dma("sync", lambda e: e.dma_start(out=cst[:], in_=cst_d), P.dsem("cst"), writes=[cstB])
    xsem = P.dsem("x")
    for c in range(8):
        P.dma("sync", lambda e, c=c: e.dma_start(out=xT[:, c * SEQ:(c + 1) * SEQ], in_=xT_d[:, c * SEQ:(c + 1) * SEQ]),
              xsem, writes=xB[c])
    for c in range(8):
        for t_ in range(4):
            xB[c][t_].w = ("d", xsem, 128)
    V(lambda e: e.tensor_copy(out=cbf[:, 0:384], in_=cst[:, 0:384]), [cstB], [cbfB])
    V(lambda e: e.tensor_copy(out=cbf[:, 384:512], in_=cst[:, 640:768]), [cstB], [cbfB])
    V(lambda e: e.memset(negones[:], -1.0), [], [cbfB])
    V(lambda e: e.memset(epsT[:, 0:1], LN_EPS / (ALPHA * ALPHA)), [], [cbfB])
    V(lambda e: e.memset(epsT[:, 1:2], RMS_EPS), [], [cbfB])
    V(lambda e: e.memset(epsT[:, 2:3], 1.0), [], [cbfB])
    V(lambda e: e.memset(epsT[:, 3:4], 0.0), [], [cbfB])
    eps_ln = epsT[:, 0:1]
    eps_rms = epsT[:, 1:2]
    one_c = epsT[:, 2:3]
    act(cact[:], ppt[:, 0:8], AF.Silu, [ppB], [cactB])
    V(lambda e: e.tensor_copy(out=cactb[:], in_=cact[:]), [cactB], [cactB])
    modBs = [Buf(f"mod{l}") for l in range(L)]
    modst = dict(cnt=0)
    o_b, _ = PO["b_ada"]
    o_f2, _ = PO["b_ff2"]

    def mod_dma(l, piece):
        s_ = (l * 8 + piece) % 3
        P.dma("gpsimd", lambda e: e.dma_start(out=wB[s_][:, 0:6144], in_=wada_d[l, piece]), wBsem[s_], writes=[wBB[s_]])

    def mod_piece(l, piece):
        s_ = (l * 8 + piece) % 3
        psm, psmB = big()
        for m in range(6):
            for k in range(8):
                mm(psm[:, m:m + 1], wB[s_][:, k * 768 + m * 128:k * 768 + (m + 1) * 128], cactb[:, k:k + 1],
                   k == 0, k == 7, [wBB[s_], cactB], [psmB])
        c0_ = l * 48 + piece * 6
        V(lambda e: e.tensor_tensor(out=mod[:, c0_:c0_ + 6], in0=psm[:, 0:6], in1=ppt[:, o_b + c0_:o_b + c0_ + 6], op=ALU.add),
          [psmB, ppB], [modBs[l]])

    def mod_final(l):
        for w_ in (1, 4):
            V(lambda e, w_=w_: e.tensor_scalar_add(out=modc(l, w_), in0=modc(l, w_), scalar1=1.0), [modBs[l]], [modBs[l]])
        for w_ in (2, 5):
            V(lambda e, w_=w_: e.tensor_scalar(out=modc(l, w_), in0=modc(l, w_), scalar1=1.0, scalar2=1.0 / ALPHA,
                                               op0=ALU.add, op1=ALU.mult), [modBs[l]], [modBs[l]])
        V(lambda e: e.tensor_tensor(out=sb2[:, l * 8:(l + 1) * 8], in0=modc(l, 5), in1=ppt[:, o_f2 + l * 8:o_f2 + (l + 1) * 8], op=ALU.mult),
          [modBs[l], ppB], [modBs[l]])

    for piece in range(8):
        mod_dma(0, piece)
        mod_piece(0, piece)
    mod_final(0)
    allW = wBB + wAB
    P.barrier(allW)

    def phase_u(l, hf, which):
        scw, shw = (1, 0) if which == 0 else (4, 3)
        i = 0
        for c in range(8):
            for t in range(2):
                T0 = hf * 1024 + t * 512
                if i % 2 == 0:
                    act(u_(c, t * 512, (t + 1) * 512), x_(c, T0, T0 + 512), AF.Identity, [xB[c][hf * 2 + t], modBs[l]],
                        [uB[c][t]], scale=modc(l, scw, c, 1), bias=modc(l, shw, c, 1))
                else:
                    V(lambda e, c=c, t=t, T0=T0: e.tensor_scalar(out=u_(c, t * 512, (t + 1) * 512), in0=x_(c, T0, T0 + 512),
                                                                 scalar1=modc(l, scw, c, 1), scalar2=modc(l, shw, c, 1),
                                                                 op0=ALU.mult, op1=ALU.add),
                      [xB[c][hf * 2 + t], modBs[l]], [uB[c][t]])
                i += 1

    def load_w(eng, dst_ap, src_ap, sem, buf):
        P.dma(eng, lambda e: e.dma_start(out=dst_ap, in_=src_ap), sem, writes=[buf])

    def prefetch_merge(l, X):
        load_w("gpsimd", wB[0][:], wM_d[l, 2 * X], wBsem[0], wBB[0])
        load_w("gpsimd", wB[1][:], wM_d[l, 2 * X + 1], wBsem[1], wBB[1])
        load_w("gpsimd", wB[2][:], wM_d[l, 4], wBsem[2], wBB[2])

    def phase_attn(l, hf):
        cv = Carver([free_ar[:]])
        do_mod = (hf == 0 and l + 1 < L)
        qT = [cv.take(1024, BF16) for _ in range(4)]
        kT2 = cv.take(1152, BF16)
        vaug = cv.take(9 * 128, BF16)
        Et = [[cv.take(512, BF16) for _ in range(2)] for _ in range(2)]
        rec = cv.take(512, F32)
        qB = [Buf(), Buf(), Buf(), Buf()]
        kB = Buf()
        vB = Buf()
        EB = [[Buf(), Buf()], [Buf(), Buf()]]
        recB = Buf()
        act(sexp[:], ppc("sinks", l, 0, 16), AF.Exp, [ppB], [lyrB])
        V(lambda e: e.memset(vaug[:].rearrange("p (b c) -> p b c", c=128)[:, :, 64:128], 1.0), [], [vB])
        for h in range(4):
            V(lambda e, h=h: e.memset(qT[h][:, :], 0.0), [], [qB[h]])
        load_w("gpsimd", wAs[:, 0:3584], wA_d[l, 0], wAsem[0], wAB[0])
        for g in range(4):
            s = g % 2
            if g + 1 < 4:
                load_w("gpsimd", wAs[:, (1 - s) * 4096:(1 - s) * 4096 + 3584], wA_d[l, g + 1], wAsem[1 - s], wAB[1 - s])
            if do_mod and g == 0:
                for pc in range(3):
                    mod_dma(l + 1, pc)
            Wb = s * 4096

            def W(k, a, b):
                return wAs[:, Wb + k * 448 + a:Wb + k * 448 + b]
            for jp in range(2):
                for t in range(2):
                    ps, pb = big()
                    for k in range(8):
                        mm(ps[:, :], W(k, jp * 128, (jp + 1) * 128), u_(k, t * 512, (t + 1) * 512), k == 0, k == 7,
                           [wAB[s], uB[k][t]], [pb])
                    act(qT[2 * jp][0:64, t * 512:(t + 1) * 512], ps[0:64, :], AF.Copy, [pb], [qB[2 * jp]], scale=0.125)
                    act(qT[2 * jp + 1][64:128, t * 512:(t + 1) * 512], ps[64:128, :], AF.Copy, [pb], [qB[2 * jp + 1]], scale=0.125)
            for t in range(2):
                ps, pb = big()
                for k in range(8):
                    mm(ps[:, :], W(k, 256, 384), u_(k, t * 512, (t + 1) * 512), k == 0, k == 7, [wAB[s], uB[k][t]], [pb])
                V(lambda e, ps=ps, t=t: e.tensor_copy(out=kT2[:, 128 + t * 512:128 + (t + 1) * 512], in_=ps[:, :]), [pb], [kB])
            ps, pb = big()
            for blk in range(8):
                for k in range(8):
                    mm(ps[:, blk * 64:(blk + 1) * 64], u_(k, blk * 128, (blk + 1) * 128), W(k, 384, 448), k == 0, k == 7,
                       [wAB[s], uB[k][blk // 4]], [pb])
            V(lambda e, ps=ps: e.tensor_copy(out=vaug[:].rearrange("p (b c) -> p b c", c=128)[:, 1:9, 0:64],
                                             in_=ps[:, :].rearrange("p (b c) -> p b c", c=64)), [pb], [vB])
            if CUT <= 1:
                return
            if hf == 1:
                V(lambda e, g=g: e.tensor_copy(out=kT2[:, 0:128], in_=kcar[:, g * 128:(g + 1) * 128]), [carB[g]], [kB])
                V(lambda e, g=g: e.tensor_copy(out=vaug[:, 0:64], in_=vcar[:, g * 128:g * 128 + 64]), [carB[g]], [vB])
            def stage1(n, g=g):
                N = hf * 8 + n
                js = [0, 1] if N > 0 else [1]
                par = n % 2
                for j in js:
                    ps, pb = big()
                    for h in range(4):
                        mm(ps[:, h * 128:(h + 1) * 128], kT2[:, (n + j) * 128:(n + j + 1) * 128],
                           qT[h][:, n * 128:(n + 1) * 128], True, True, [kB, qB[h]], [pb])
                    E = Et[j][par]
                    act(E[:, :], ps[:, :], AF.Exp, [pb], [EB[j][par]])
                    mk = mcur_b if j == 1 else mprev_b
                    V(lambda e, E=E, mk=mk: e.tensor_tensor(out=E.rearrange("p (h q) -> p h q", h=4),
                                                            in0=E.rearrange("p (h q) -> p h q", h=4),
                                                            in1=mk.unsqueeze(1).to_broadcast([128, 4, 128]), op=ALU.mult),
                      [EB[j][par], cbfB], [EB[j][par]])
                return js

            def stage2(n, js, g=g):
                par = n % 2
                ps, pb = big()
                for idx, j in enumerate(js):
                    mm(ps[:, :], vaug[:, (n + j) * 128:(n + j + 1) * 128], Et[j][par][:, :], idx == 0, idx == len(js) - 1,
                       [vB, EB[j][par]], [pb])
                V(lambda e, ps=ps: e.tensor_tensor(out=rec[0:64, :].rearrange("p (h q) -> p h q", h=4),
                                                   in0=ps[64:128, :].rearrange("p (h q) -> p h q", h=4),
                                                   in1=sexp[64:128, g * 4:(g + 1) * 4].unsqueeze(2).to_broadcast([64, 4, 128]),
                                                   op=ALU.add), [pb, lyrB], [recB])
                act(rec[0:64, :], rec[0:64, :], AF.Ln, [recB], [recB])
                act(rec[0:64, :], rec[0:64, :], AF.Exp, [recB], [recB], scale=-1.0)
                psv = ps[0:64, :].rearrange("p (a b q) -> p a b q", a=2, b=2)
                rcv = rec[0:64, :].rearrange("p (a b q) -> p a b q", a=2, b=2)
                for odd in range(2):
                    c0 = 2 * g
                    dst = oT[odd * 64:odd * 64 + 64, :].rearrange("p (c t) -> p c t", c=8)[:, c0:c0 + 2, n * 128:(n + 1) * 128]
                    V(lambda e, dst=dst, odd=odd, psv=psv, rcv=rcv: e.tensor_tensor(out=dst, in0=psv[:, :, odd, :], in1=rcv[:, :, odd, :],
                                                                                    op=ALU.mult),
                      [pb, recB], [oB[c0][n // 4], oB[c0 + 1][n // 4]])
            js_cur = stage1(0)
            for n in range(8):
                js_nxt = stage1(n + 1) if n + 1 < 8 else None
                stage2(n, js_cur)
                js_cur = js_nxt
            if hf == 0:
                V(lambda e, g=g: e.tensor_copy(out=kcar[:, g * 128:(g + 1) * 128], in_=kT2[:, 1024:1152]), [kB], [carB[g]])
                V(lambda e, g=g: e.tensor_copy(out=vcar[:, g * 128:g * 128 + 64], in_=vaug[:, 8 * 128:8 * 128 + 64]), [vB], [carB[g]])
            if do_mod:
                for pc in (2 * g, 2 * g + 1):
                    mod_piece(l + 1, pc)
                    if pc + 3 < 8:
                        mod_dma(l + 1, pc + 3)
        if do_mod:
            mod_final(l + 1)
        prefetch_merge(l, 0)

    def ln_alloc(cv):
        return dict(sq=cv.take(512, F32), sacc=cv.take(512, F32), qacc=cv.take(512, F32), mu=cv.take(512, F32),
                    rstd=cv.take(512, F32), tmp=[cv.take(512, F32) for _ in range(2)],
                    B=[Buf() for _ in range(5)], tB=[Buf(), Buf()])

    def layer_norm(l, Tg_, gname, bname, tm):
        sq, sacc, qacc, mu, rstd, tmp = tm["sq"], tm["sacc"], tm["qacc"], tm["mu"], tm["rstd"], tm["tmp"]
        sqB, saB, qaB, muB, rsB = tm["B"]
        tB = tm["tB"]
        a0 = Tg_ * 512
        for c in range(8):
            xs = x_(c, a0, a0 + 512)
            if c == 0:
                V(lambda e, xs=xs: e.tensor_copy(out=sacc, in_=xs), [xB[c][Tg_]], [saB])
                act(qacc, xs, AF.Square, [xB[c][Tg_]], [qaB])
            else:
                V(lambda e, xs=xs: e.tensor_tensor(out=sacc, in0=sacc, in1=xs, op=ALU.add), [xB[c][Tg_], saB], [saB])
                act(sq, xs, AF.Square, [xB[c][Tg_]], [sqB])
                V(lambda e: e.tensor_tensor(out=qacc, in0=qacc, in1=sq, op=ALU.add), [sqB, qaB], [qaB])
        p1, p1B = big()
        mm(p1[:, :], ones_f, sacc, True, True, [saB, cstB], [p1B])
        p2, p2B = big()
        mm(p2[:, :], ones_f, qacc, True, True, [qaB, cstB], [p2B])
        act(mu, p1[:, :], AF.Copy, [p1B], [muB], scale=1.0 / D)
        V(lambda e: e.tensor_tensor(out=sq, in0=mu, in1=mu, op=ALU.mult), [muB, sqB], [sqB])
        V(lambda e: e.scalar_tensor_tensor(out=rstd, in0=p2[:, :], scalar=1.0 / D, in1=sq, op0=ALU.mult, op1=ALU.subtract),
          [p2B, sqB], [rsB])
        act(rstd, rstd, AF.Ln, [rsB], [rsB], bias=eps_ln, scale=1.0)
        act(rstd, rstd, AF.Exp, [rsB], [rsB], scale=-0.5)
        for c in range(8):
            xs = x_(c, a0, a0 + 512)
            tt = tmp[c % 2]
            V(lambda e, xs=xs, tt=tt: e.tensor_tensor(out=tt, in0=xs, in1=mu, op=ALU.subtract), [xB[c][Tg_], muB], [tB[c % 2]])
            V(lambda e, tt=tt: e.tensor_tensor(out=tt, in0=tt, in1=rstd, op=ALU.mult), [tB[c % 2], rsB], [tB[c % 2]])
            act(xs, tt, AF.Identity, [tB[c % 2], ppB], [xB[c][Tg_]], scale=ppc(gname, l, c, 1), bias=ppc(bname, l, c, 1))

    def phase_merge(l, hf, X, do_ln):
        if X == 1:
            prefetch_merge(l, X)
        cv = Carver([free_ar[:], wAs[:]])
        sg = [cv.take(512, F32) for _ in range(2)]
        mg = cv.take(8 * 512, BF16)
        sgB = [Buf(), Buf()]
        mgB = [Buf() for _ in range(8)]
        lnt = ln_alloc(cv) if do_ln else None
        for t in range(2):
            Tg_ = hf * 2 + t
            for c in range(8):
                py, pyB = big()
                for k in range(8):
                    mm(py[:, :], wB[0][:, k * 1024 + c * 128:k * 1024 + (c + 1) * 128], oT[:, k * 1024 + t * 512:k * 1024 + (t + 1) * 512],
                       k == 0, k == 7, [wBB[0], oB[k][t]], [pyB])
                pg, pgB = big()
                for k in range(8):
                    mm(pg[:, :], wB[1][:, k * 1024 + c * 128:k * 1024 + (c + 1) * 128], u_(k, t * 512, (t + 1) * 512),
                       k == 0, k == 7, [wBB[1], uB[k][t]], [pgB])
                act(sg[c % 2], pg[:, :], AF.Sigmoid, [pgB], [sgB[c % 2]])
                V(lambda e, c=c, py=py: e.tensor_tensor(out=mg[:, c * 512:(c + 1) * 512], in0=py[:, :], in1=sg[c % 2], op=ALU.mult),
                  [pyB, sgB[c % 2]], [mgB[c]])
            for c2 in range(8):
                pm, pmB = big()
                for c in range(8):
                    mm(pm[:, :], wB[2][:, c * 1024 + c2 * 128:c * 1024 + (c2 + 1) * 128], mg[:, c * 512:(c + 1) * 512],
                       c == 0, c == 7, [wBB[2], mgB[c]], [pmB])
                xs = x_(c2, Tg_ * 512, Tg_ * 512 + 512)
                V(lambda e, pm=pm, xs=xs, c2=c2: e.scalar_tensor_tensor(out=xs, in0=pm[:, :], scalar=modc(l, 2, c2, 1), in1=xs,
                                                                         op0=ALU.mult, op1=ALU.add),
                  [pmB, modBs[l], xB[c2][Tg_]], [xB[c2][Tg_]])
            if do_ln and t == 1:
                load_w("gpsimd", wB[0][:], wF_d[l, 0], wBsem[0], wBB[0])
                load_w("gpsimd", wB[1][:], wF_d[l, 1], wBsem[1], wBB[1])
            if do_ln:
                layer_norm(l, Tg_, "ln1_g", "ln1_b", lnt)

    def phase_ffn(l, hf):
        cv = Carver([free_ar[:], wAs[:]])
        rl = [cv.take(512, F32) for _ in range(2)]
        hT = [cv.take(4 * 512, BF16) for _ in range(2)]
        rlB = [Buf(), Buf()]
        hB = [[Buf() for _ in range(4)] for _ in range(2)]
        lnt = ln_alloc(cv)
        phase_u(l, hf, 1)
        for c in range(8):
            for t in range(2):
                Tg_ = hf * 2 + t
                xs = x_(c, Tg_ * 512, Tg_ * 512 + 512)
                V(lambda e, xs=xs, c=c: e.tensor_scalar(out=xs, in0=xs, scalar1=sb2[:, l * 8 + c:l * 8 + c + 1], scalar2=None, op0=ALU.add),
                  [xB[c][Tg_], modBs[l]], [xB[c][Tg_]])
        it = 0
        for F in range(8):
            s = F % 3
            if F + 2 < 8:
                s2 = (F + 2) % 3
                load_w("gpsimd", wB[s2][:], wF_d[l, F + 2], wBsem[s2], wBB[s2])
            for t in range(2):
                Tg_ = hf * 2 + t
                hp = it % 2
                it += 1
                for f in range(4):
                    ph, phB = big()
                    for k in range(8):
                        mm(ph[:, :], wB[s][:, k * 512 + f * 128:k * 512 + (f + 1) * 128], u_(k, t * 512, (t + 1) * 512),
                           k == 0, k == 7, [wBB[s], uB[k][t]], [phB])
                    r_ = rl[f % 2]
                    act(r_, ph[:, :], AF.Relu, [phB, ppB], [rlB[f % 2]], bias=ppc("b_ff1", l, F * 4 + f, 1), scale=1.0)
                    V(lambda e, r_=r_, hp=hp, f=f: e.tensor_tensor(out=hT[hp][:, f * 512:(f + 1) * 512], in0=r_, in1=r_, op=ALU.mult),
                      [rlB[f % 2]], [hB[hp][f]])
                for c2 in range(8):
                    po_, poB = big()
                    for f in range(4):
                        mm(po_[:, :], wB[s][:, 4096 + f * 1024 + c2 * 128:4096 + f * 1024 + (c2 + 1) * 128], hT[hp][:, f * 512:(f + 1) * 512],
                           f == 0, f == 3, [wBB[s], hB[hp][f]], [poB])
                    xs = x_(c2, Tg_ * 512, Tg_ * 512 + 512)
                    V(lambda e, po_=po_, xs=xs, c2=c2: e.scalar_tensor_tensor(out=xs, in0=po_[:, :], scalar=modc(l, 5, c2, 1), in1=xs,
                                                                               op0=ALU.mult, op1=ALU.add),
                      [poB, modBs[l], xB[c2][Tg_]], [xB[c2][Tg_]])
        for t in range(2):
            layer_norm(l, hf * 2 + t, "ln2_g", "ln2_b", lnt)

    def interleave(*gens, rngs=None, reps=None):
        gens = [(g, (rngs[i] if rngs else (0, 8)), (reps[i] if reps else 1)) for i, g in enumerate(gens) if g is not None]
        while gens:
            for it in list(gens):
                st["rng"] = it[1]
                try:
                    for _ in range(it[2]):
                        next(it[0])
                except StopIteration:
                    gens.remove(it)
        st["rng"] = (0, 8)

    def chain(*gens):
        for g in gens:
            if g is not None:
                yield from g

    def phase_dn(l, hf):
        cv = Carver([free_ar[:], wB[0][:], wB[1][:], wB[2][:]])
        W5 = 512
        pre = cv.take(1028, BF16)
        dg = cv.take(512, BF16)
        dgB = Buf()
        acc = cv.take(1024, F32)
        sqb = cv.take(512, BF16)
        rn = cv.take(512, F32)
        qT = cv.take(1024, BF16)
        kT = cv.take(1024, BF16)
        vT = cv.take(1024, BF16)
        kk = cv.take(1024, BF16)
        vv = cv.take(1024, BF16)
        sz = [cv.take(1024, BF16) for _ in range(2)]
        Tg4 = cv.take(W5, F32)
        D4 = cv.take(W5, F32)
        DT4 = cv.take(W5, F32)
        eg4 = cv.take(W5, F32)
        Pab = [cv.take(W5, F32) for _ in range(2)]
        PTab = [cv.take(W5, F32) for _ in range(2)]
        PabB = [Buf(), Buf()]
        PTabB = [Buf(), Buf()]
        Za = cv.take(W5, F32)
        Zb4 = cv.take(W5, BF16)
        sm4 = cv.take(32, F32)
        ab = [dict(P0=cv.take(W5, F32), Rw=cv.take(W5, BF16), vb=cv.take(W5, BF16), B=Buf()) for _ in range(2)]
        bun = [dict(qdT=cv.take(W5, BF16), kdec=cv.take(W5, BF16), intraT=cv.take(W5, BF16), negWT=cv.take(W5, BF16),
                    U=cv.take(W5, F32), last=cv.take(4, F32), B=Buf(), B2=Buf()) for _ in range(3)]
        vnew = cv.take(128, BF16)
        S_b = cv.take(128, BF16)
        tt = cv.take(128, F32)
        junk = cv.take(128, F32)
        obt = cv.take(128, BF16)
        smc = cv.take(8, F32)
        preB, accB, sqbB, rnB, qTB, kTB, vTB, kkB, vvB = (Buf() for _ in range(9))
        szB = [Buf(), Buf()]
        TgB, DB, DTB, egB, PaB, PTaB, ZaB, ZbB, RwB, vbB, smB = (Buf() for _ in range(11))
        vnB, SbB, ttB, jkB, obB, scB = (Buf() for _ in range(6))
        gl4, t4, egp4, ekd4, bege4 = (sm4[:, i * 4:(i + 1) * 4] for i in range(5))
        ssq, rms = smc[:, 0:1], smc[:, 1:2]

        def v4(ap):
            return ap.rearrange("p (i f) -> p i f", i=4)

        def bc1(ap):
            return ap.unsqueeze(1).to_broadcast([128, 4, 128])

        def bc2(ap):
            return ap.unsqueeze(2).to_broadcast([128, 4, 128])

        load_w("gpsimd", wba[:], wBA_d[l], wbasem, wbaB)
        ps, pb = big()
        for blk in range(8):
            for k in range(8):
                mm(ps[:, blk * 16:(blk + 1) * 16], u_(k, blk * 128, (blk + 1) * 128), wba[:, k * 16:(k + 1) * 16], k == 0, k == 7,
                   [wbaB, uB[k][blk // 4]], [pb])
        psv = ps[:, 0:128].rearrange("p (b j) -> p b j", j=16)
        b3 = bet[:].rearrange("p (b j) -> p b j", j=8)
        nb3 = nbet[:].rearrange("p (b j) -> p b j", j=8)
        g3 = gtk[:].rearrange("p (b j) -> p b j", j=8)
        gt3 = gtmp[:].rearrange("p (b j) -> p b j", j=8)
        act(b3, psv[:, :, 0:8], AF.Sigmoid, [pb], [bgB])
        V(lambda e: e.tensor_scalar(out=nbet[:], in0=bet[:], scalar1=-1.0, scalar2=None, op0=ALU.mult), [bgB], [bgB])
        V(lambda e: e.tensor_tensor(out=gt3, in0=psv[:, :, 8:16], in1=ppc("dt_bias", l, 0, 8).unsqueeze(1).to_broadcast([128, 8, 8]),
                                    op=ALU.add), [pb, ppB], [bgB])
        act(gtmp[:], gtmp[:], AF.Exp, [bgB], [bgB])
        act(gtmp[:], gtmp[:], AF.Ln, [bgB], [bgB], bias=one_c, scale=1.0)
        act(nega[:], ppc("a_log", l, 0, 8), AF.Exp, [ppB], [bgB])
        V(lambda e: e.scalar_tensor_tensor(out=g3, in0=gt3, scalar=-1.0, in1=nega[:].unsqueeze(1).to_broadcast([128, 8, 8]),
                                           op0=ALU.mult, op1=ALU.mult), [bgB], [bgB])
        normw = ppc("normw", l, 0, 128)
        cwo, _ = PO["convw"]

        def preamble(h):
            s = h % 2
            if h == 0:
                load_w("gpsimd", wAs[:, 0:4096], wD_d[l, 0], wAsem[0], wAB[0])
            if h + 1 < 8:
                load_w("gpsimd", wAs[:, (1 - s) * 4096:(2 - s) * 4096], wD_d[l, h + 1], wAsem[1 - s], wAB[1 - s])
            Wb = s * 4096

            def W(k, a, b):
                return wAs[:, Wb + k * 512 + a:Wb + k * 512 + b]
            for X in range(3):
                cc = ccar[:, (h * 3 + X) * 3:(h * 3 + X) * 3 + 3]
                if hf == 0:
                    V(lambda e: e.memset(pre[:, 0:3], 0.0), [], [preB])
                else:
                    V(lambda e, cc=cc: e.tensor_copy(out=pre[:, 0:3], in_=cc), [ccarB[h]], [preB])

                def cw(j, X=X, h=h):
                    o = cwo + l * 96 + j * 24 + X * 8 + h
                    return ppt[:, o:o + 1]
                for j in range(4):
                    V(lambda e, j=j, cw=cw: e.tensor_scalar(out=dg[:, j * 128:(j + 1) * 128], in0=ident_b, scalar1=cw(j), scalar2=None, op0=ALU.mult),
                      [cbfB, ppB], [dgB])
                for t in range(2):
                    ps, pb = big()
                    for k in range(8):
                        mm(ps[:, :], W(k, X * 128, (X + 1) * 128), u_(k, t * 512, (t + 1) * 512), k == 0, k == 7, [wAB[s], uB[k][t]], [pb])
                    act(pre[:, 3 + t * 512:3 + (t + 1) * 512], ps[:, :], AF.Copy, [pb], [preB])
                    yield
                if hf == 0:
                    V(lambda e, cc=cc: e.tensor_copy(out=cc, in_=pre[:, 1024:1027]), [preB], [ccarB[h]])
                for t in range(2):
                    ps, pb = big()
                    for j in range(4):
                        mm(ps[:, :], dg[:, j * 128:(j + 1) * 128], pre[:, t * 512 + j:t * 512 + j + 512], j == 0, j == 3, [dgB, preB], [pb])
                    act(acc[:, t * 512:(t + 1) * 512], ps[:, :], AF.Silu, [pb], [accB])
                    yield
                if X < 2:
                    for t in range(2):
                        at_ = acc[:, t * 512:(t + 1) * 512]
                        act(sqb, at_, AF.Square, [accB], [sqbB])
                        ps, pb = big()
                        mm(ps[:, :], ones_b, sqb, True, True, [sqbB, cbfB], [pb])
                        act(rn, ps[:, :], AF.Ln, [pb], [rnB], bias=eps_rms, scale=1.0)
                        act(rn, rn, AF.Exp, [rnB], [rnB], scale=-0.5)
                        if X == 0:
                            V(lambda e, at_=at_, t=t: e.scalar_tensor_tensor(out=qT[:, t * 512:(t + 1) * 512], in0=at_, scalar=128 ** -0.5, in1=rn,
                                                                             op0=ALU.mult, op1=ALU.mult), [accB, rnB], [qTB])
                        else:
                            V(lambda e, at_=at_, t=t: e.tensor_tensor(out=kT[:, t * 512:(t + 1) * 512], in0=at_, in1=rn, op=ALU.mult),
                              [accB, rnB], [kTB])
                        yield
                else:
                    for t in range(2):
                        G(lambda e, t=t: e.tensor_copy(out=vT[:, t * 512:(t + 1) * 512], in_=acc[:, t * 512:(t + 1) * 512]), [accB], [vTB])
                    yield
            for h2 in range(2):
                ps, pb = big()
                for b4 in range(4):
                    blk = h2 * 4 + b4
                    for k in range(8):
                        mm(ps[:, b4 * 128:(b4 + 1) * 128], u_(k, blk * 128, (blk + 1) * 128), W(k, 384, 512), k == 0, k == 7,
                           [wAB[s], uB[k][blk // 4]], [pb])
                act(sz[s][:, h2 * 512:(h2 + 1) * 512], ps[:, :], AF.Silu, [pb], [szB[s]])
                yield
            for src, srcB, dst, dstB in ((kT, kTB, kk, kkB), (vT, vTB, vv, vvB)):
                for h2 in range(2):
                    pt4, pt4B = tslot4()
                    for b4 in range(4):
                        blk = h2 * 4 + b4
                        T(lambda e, pt4=pt4, b4=b4, src=src, blk=blk: e.transpose(pt4[:, b4 * 128:(b4 + 1) * 128],
                                                                                  src[:, blk * 128:(blk + 1) * 128], ident_b),
                          [srcB, cbfB], pt4B)
                    act(dst[:, h2 * 512:(h2 + 1) * 512], pt4, AF.Copy, pt4B, [dstB])
                    yield

        def prescanA(h, t, gi):
            n0 = 4 * t
            a0 = t * 512
            bu = bun[gi % 3]
            abu = ab[gi % 2]
            abB = abu["B"]
            P0, Rw4, vb4 = abu["P0"], abu["Rw"], abu["vb"]
            gsel = g3[:, n0:n0 + 4, h]
            bsel = b3[:, n0:n0 + 4, h]
            nbsel = nb3[:, n0:n0 + 4, h]
            V(lambda e: e.tensor_tensor(out=v4(Tg4), in0=bc1(triinc), in1=bc2(gsel), op=ALU.mult), [cstB, bgB], [TgB])
            pgr, pgrB = big()
            mm(pgr[:, :], ones_f, Tg4, True, True, [cstB, TgB], [pgrB])
            pgd, pgdB = big()
            mm(pgd[:, :], negones[:], Tg4, True, False, [cbfB, TgB], [pgdB])
            for i in range(4):
                mm(pgd[:, i * 128:(i + 1) * 128], Tg4[:, i * 128:(i + 1) * 128], ones_f, False, i == 3, [cstB, TgB], [pgdB])
            yield
            V(lambda e: e.tensor_tensor(out=v4(D4), in0=v4(pgd[:, :]), in1=bc1(mls), op=ALU.add), [pgdB, cstB], [DB])
            act(D4, D4, AF.Exp, [DB], [DB])
            V(lambda e: e.tensor_tensor(out=v4(DT4), in0=bc1(mupi), in1=v4(pgd[:, :]), op=ALU.subtract), [pgdB, cstB], [DTB])
            act(DT4, DT4, AF.Exp, [DTB], [DTB])
            act(eg4, pgr[:, :], AF.Exp, [pgrB], [egB])
            yield
            pgr_l = v4(pgr[:, :])[:, :, 127]
            pgd_l = v4(pgd[:, :])[:, :, 127]
            act(gl4, pgr_l, AF.Copy, [pgrB], [smB])
            V(lambda e: e.tensor_tensor(out=t4, in0=pgd_l, in1=gl4, op=ALU.add), [pgdB, smB], [smB])
            act(egp4, t4, AF.Exp, [smB], [smB])
            act(ekd4, pgd_l, AF.Exp, [pgdB, smB], [smB], scale=-1.0)
            V(lambda e: e.tensor_tensor(out=bege4, in0=egp4, in1=bsel, op=ALU.mult), [smB, bgB], [smB])
            act(bu["last"], v4(eg4)[:, :, 127], AF.Copy, [egB], [bu["B"]])
            yield
            pkk, pkkB = big()
            for i in range(4):
                ks = kT[:, a0 + i * 128:a0 + (i + 1) * 128]
                mm(pkk[:, i * 128:(i + 1) * 128], ks, ks, True, True, [kTB], [pkkB])
            pqk, pqkB = big()
            for i in range(4):
                ks = kT[:, a0 + i * 128:a0 + (i + 1) * 128]
                mm(pqk[:, i * 128:(i + 1) * 128], ks, qT[:, a0 + i * 128:a0 + (i + 1) * 128], True, True, [kTB, qTB], [pqkB])
            yield
            V(lambda e: e.tensor_tensor(out=P0, in0=pkk[:, :], in1=D4, op=ALU.mult), [pkkB, DB], [abB])
            V(lambda e: e.tensor_tensor(out=v4(P0), in0=v4(P0), in1=bc2(nbsel), op=ALU.mult), [abB, bgB], [abB])
            V(lambda e: e.tensor_tensor(out=bu["intraT"], in0=pqk[:, :], in1=DT4, op=ALU.mult), [pqkB, DTB], [bu["B"]])
            yield
            G(lambda e: e.tensor_tensor(out=bu["qdT"], in0=qT[:, a0:a0 + 512], in1=eg4, op=ALU.mult), [qTB, egB], [bu["B"]])
            G(lambda e: e.tensor_tensor(out=v4(bu["kdec"]), in0=v4(kk[:, a0:a0 + 512]), in1=bc2(ekd4), op=ALU.mult), [kkB, smB], [bu["B"]])
            G(lambda e: e.tensor_tensor(out=v4(Rw4), in0=v4(kk[:, a0:a0 + 512]), in1=bc2(bege4), op=ALU.mult), [kkB, smB], [abB])
            G(lambda e: e.tensor_tensor(out=v4(vb4), in0=v4(vv[:, a0:a0 + 512]), in1=bc2(bsel), op=ALU.mult), [vvB, bgB], [abB])
            yield

        def prescanB(h, t, gi):
            bu = bun[gi % 3]
            abu = ab[gi % 2]
            abB = abu["B"]
            P0, Rw4, vb4 = abu["P0"], abu["Rw"], abu["vb"]
            ptp, ptpB = big()
            for i in range(4):
                T(lambda e, i=i: e.transpose(ptp[:, i * 128:(i + 1) * 128], P0[:, i * 128:(i + 1) * 128], ident_f), [abB, cstB], [ptpB])
            act(PTab[0], ptp[:, :], AF.Copy, [ptpB], [PTabB[0]])
            V(lambda e: e.tensor_tensor(out=v4(Za), in0=v4(ptp[:, :]), in1=bc1(ident_f), op=ALU.add), [ptpB, cstB], [ZaB])
            yield
            pend = None
            for s_ in range(1, 7):
                cur, prv = s_ % 2, (s_ - 1) % 2
                Pin, PinB = (P0, abB) if s_ == 1 else (Pab[prv], PabB[prv])
                PTin, PTinB = PTab[prv], PTabB[prv]
                pp_, ppB_ = big()
                for i in range(4):
                    sl = slice(i * 128, (i + 1) * 128)
                    mm(pp_[:, sl], PTin[:, sl], Pin[:, sl], True, True, [PTinB, PinB], [ppB_])
                if os.environ.get("K_FINE", "0") == "1":
                    yield
                if s_ < 6:
                    pq_, pqB_ = big()
                    for i in range(4):
                        sl = slice(i * 128, (i + 1) * 128)
                        mm(pq_[:, sl], Pin[:, sl], PTin[:, sl], True, True, [PTinB, PinB], [pqB_])
                if os.environ.get("K_FINE", "0") == "1":
                    yield
                if pend is not None:
                    pend()
                    if os.environ.get("K_FINE", "0") == "1":
                        yield
                act(Pab[cur], pp_[:, :], AF.Copy, [ppB_], [PabB[cur]])
                if s_ < 6:
                    V(lambda e, pq_=pq_, cur=cur: e.tensor_copy(out=PTab[cur], in_=pq_[:, :]), [pqB_], [PTabB[cur]])
                yield

                def zupd(cur=cur):
                    pz_, pzB_ = big()
                    for i in range(4):
                        sl = slice(i * 128, (i + 1) * 128)
                        mm(pz_[:, sl], Pab[cur][:, sl], Za[:, sl], True, True, [PabB[cur], ZaB], [pzB_])
                    V(lambda e, pz_=pz_: e.tensor_tensor(out=Za, in0=pz_[:, :], in1=Za, op=ALU.add), [pzB_, ZaB], [ZaB])
                pend = zupd
            pend()
            yield
            V(lambda e: e.tensor_copy(out=Zb4, in_=Za), [ZaB], [ZbB])
            pu, puB = big()
            for i in range(4):
                sl = slice(i * 128, (i + 1) * 128)
                mm(pu[:, sl], Zb4[:, sl], vb4[:, sl], True, True, [ZbB, abB], [puB])
            act(bu["U"], pu[:, :], AF.Copy, [puB], [bu["B2"]])
            pw, pwB = big()
            for i in range(4):
                sl = slice(i * 128, (i + 1) * 128)
                mm(pw[:, sl], Rw4[:, sl], Zb4[:, sl], True, True, [ZbB, abB], [pwB])
            act(bu["negWT"], pw[:, :], AF.Copy, [pwB], [bu["B2"]], scale=-1.0)
            yield

        def scan(h, t, gi):
            bu = bun[gi % 3]
            bB = bu["B"]
            b2 = bu["B2"]
            s = h % 2
            Sh = S_f[:, h * 128:(h + 1) * 128]
            if t == 0:
                if hf == 0:
                    V(lambda e: e.memset(Sh, 0.0), [], [S_fB[h]])
                V(lambda e: e.tensor_copy(out=S_b, in_=Sh), [S_fB[h]], [SbB])
            for i in range(4):
                n = 4 * t + i
                c0 = n * 128
                sl = slice(i * 128, (i + 1) * 128)
                p1, p1B = quarter()
                mm(p1, bu["negWT"][:, sl], S_b, True, True, [b2, SbB], [p1B])
                V(lambda e, p1=p1, sl=sl: e.tensor_tensor(out=vnew, in0=p1, in1=bu["U"][:, sl], op=ALU.add), [p1B, b2], [vnB])
                yield
                po_, poB = quarter()
                mm(po_, bu["qdT"][:, sl], S_b, True, False, [bB, SbB], [poB])
                mm(po_, bu["intraT"][:, sl], vnew, False, True, [bB, vnB], [poB])
                pS, pSB = quarter()
                mm(pS, bu["kdec"][:, sl], vnew, True, True, [bB, vnB], [pSB])
                V(lambda e, pS=pS, i=i: e.scalar_tensor_tensor(out=Sh, in0=Sh, scalar=bu["last"][:, i:i + 1], in1=pS,
                                                               op0=ALU.mult, op1=ALU.add), [S_fB[h], pSB, bB], [S_fB[h]])
                V(lambda e: e.tensor_copy(out=S_b, in_=Sh), [S_fB[h]], [SbB])
                yield
                V(lambda e: e.memset(ssq, 0.0), [], [scB])
                act(junk, po_, AF.Square, [poB, scB], [jkB, scB], accum_out=ssq)
                act(rms, ssq, AF.Ln, [scB], [scB], bias=eps_rms, scale=1.0 / 128)
                act(rms, rms, AF.Exp, [scB], [scB], scale=-0.5)
                V(lambda e, po_=po_: e.scalar_tensor_tensor(out=tt, in0=po_, scalar=rms, in1=normw, op0=ALU.mult, op1=ALU.mult),
                  [poB, scB, ppB], [ttB])
                yield
                G(lambda e, c0=c0: e.tensor_tensor(out=obt, in0=tt, in1=sz[s][:, c0:c0 + 128], op=ALU.mult), [ttB, szB[s]], [obB])
                pt2, pt2B = tslot()
                T(lambda e, pt2=pt2: e.transpose(pt2, obt, ident_b), [obB, cbfB], [pt2B])
                act(oT[:, h * 1024 + c0:h * 1024 + c0 + 128], pt2, AF.Copy, [pt2B], [oB[h][n // 4]])
                yield

        groups = [(h, t) for h in range(8) for t in range(2)]
        NG = len(groups)

        def a_stream(gi):
            if gi >= NG:
                return None
            h, t = groups[gi]
            if t == 0:
                return chain(preamble(h), prescanA(h, t, gi))
            return prescanA(h, t, gi)
        interleave(a_stream(0))
        SR = int(os.environ.get("K_SR", "1"))
        for gi in range(NG + 1):
            gens, rngs, reps = [], [], []
            if gi >= 1:
                gens.append(scan(groups[gi - 1][0], groups[gi - 1][1], gi - 1))
                rngs.append((6, 8))
                reps.append(SR)
            if gi < NG:
                gens.append(prescanB(groups[gi][0], groups[gi][1], gi))
                rngs.append((3, 6))
                reps.append(int(os.environ.get("K_BR", "1")))
            nxt = a_stream(gi + 1)
            if nxt is not None:
                gens.append(nxt)
                rngs.append((0, 3))
                reps.append(int(os.environ.get("K_AR", "1")))
            interleave(*gens, rngs=rngs, reps=reps)

    for l in range(L):
        if stop == "p0":
            break
        for hf in range(2):
            phase_u(l, hf, 0)
            if stop == "u":
                break
            phase_attn(l, hf)
            P.barrier(allW)
            if stop == "attn":
                break
            phase_merge(l, hf, 0, False)
            P.barrier(allW)
            if stop == "ma":
                break
            phase_dn(l, hf)
            P.barrier(allW)
            if stop == "dn":
                break
            phase_merge(l, hf, 1, True)
            P.barrier(allW)
            if stop == "ln1":
                continue
            phase_ffn(l, hf)
            P.barrier(allW)
        if stop is not None and stop != "ln1":
            break

    osem = P.dsem("out")
    evs = []
    for c in range(8):
        evs.append(P.dma("sync", lambda e, c=c: e.dma_start(out=yT_d[:, c * SEQ:(c + 1) * SEQ], in_=xT[:, c * SEQ:(c + 1) * SEQ]),
                         osem, reads=xB[c]))
    if dbg:
        dsm = P.dsem("dbg")
        evs.append(P.dma("sync", lambda e: e.dma_start(out=dbg_d, in_=oT[:]), dsm, reads=[b for row in oB for b in row]))
    P.wait_events("sync", evs)
    P.emit()
    return nc


def _kp(w):
    n = w.shape[1]
    return np.ascontiguousarray(w.reshape(8, 128, n).transpose(1, 0, 2))


def _colT(v, nch):
    return np.ascontiguousarray(v.reshape(nch, 128).T)


def pack_shared(inp, L):
    PO, NPP = pp_off(L)
    pp = np.zeros((128, NPP), np.float32)

    def put(name, l, arr):
        o, per = PO[name]
        pp[:, o + l * per:o + (l + 1) * per] = arr
    wada = np.zeros((L, 8, 128, 6144), np.float32)
    wA = np.zeros((L, 4, 128, 8 * 448), np.float32)
    wD = np.zeros((L, 8, 128, 8 * 512), np.float32)
    wBA = np.zeros((L, 128, 128), np.float32)
    wM = np.zeros((L, 5, 128, 8192), np.float32)
    wF = np.zeros((L, 8, 128, 8192), np.float32)
    for l in range(L):
        put("b_ada", l, _colT(inp["b_ada"][l], 48))
        for nm in ("ln1_g", "ln1_b", "ln2_g", "ln2_b", "b_ff2"):
            put(nm, l, _colT(inp[nm][l], 8))
        put("b_ff1", l, _colT(inp["b_ff1"][l], 32))
        cw = inp["conv_w"][l]
        put("convw", l, cw.reshape(4, 24, 128).transpose(2, 0, 1).reshape(128, 96))
        put("sinks", l, np.broadcast_to(inp["sinks"][l][None, :], (128, 16)))
        put("a_log", l, np.broadcast_to(inp["a_log"][l][None, :], (128, 8)))
        put("dt_bias", l, np.broadcast_to(inp["dt_bias"][l][None, :], (128, 8)))
        put("normw", l, np.broadcast_to(inp["dn_norm_w"][l][None, :], (128, 128)))
        wa = _kp(inp["w_ada"][l])
        for piece in range(8):
            wada[l, piece] = wa[:, :, piece * 768:(piece + 1) * 768].reshape(128, 6144)
        wi = _kp(inp["w_in"][l])
        for g in range(4):
            q = wi[:, :, g * 256:(g + 1) * 256]
            k = wi[:, :, 1024 + g * 64:1024 + (g + 1) * 64]
            v = wi[:, :, 1280 + g * 64:1280 + (g + 1) * 64]
            wA[l, g] = np.concatenate([q, k, k, v], axis=2).reshape(128, 8 * 448)
        for h in range(8):
            parts = [wi[:, :, 1536 + X * 1024 + h * 128:1536 + X * 1024 + (h + 1) * 128] for X in range(4)]
            wD[l, h] = np.concatenate(parts, axis=2).reshape(128, 8 * 512)
        wBA[l] = wi[:, :, 5632:5648].reshape(128, 128)
        wM[l, 0] = _kp(inp["w_oa"][l]).reshape(128, 8192)
        wM[l, 1] = wi[:, :, 5648:6672].reshape(128, 8192)
        wM[l, 2] = _kp(inp["w_ob"][l]).reshape(128, 8192)
        wM[l, 3] = wi[:, :, 6672:7696].reshape(128, 8192)
        wM[l, 4] = _kp(inp["w_out"][l]).reshape(128, 8192)
        w1 = _kp(inp["w_ff1"][l])
        w2 = inp["w_ff2"][l]
        for F in range(8):
            wF[l, F, :, 0:4096] = w1[:, :, F * 512:(F + 1) * 512].reshape(128, 4096)
            blk = w2[F * 512:(F + 1) * 512, :].reshape(4, 128, 1024).transpose(1, 0, 2)
            wF[l, F, :, 4096:8192] = blk.reshape(128, 4096)
    p = np.arange(128)[:, None]
    f = np.arange(128)[None, :]
    cst = np.concatenate([
        (p == f).astype(np.float32), np.ones((128, 128), np.float32), (p <= f).astype(np.float32),
        np.where(p > f, 0.0, NEG).astype(np.float32), np.where(f >= p, 0.0, NEG).astype(np.float32),
        (p > f).astype(np.float32)], axis=1)
    return dict(pp=pp, cst=cst, wada=wada, wA=wA, wD=wD, wBA=wBA, wM=wM, wF=wF)


def make_in_maps(inp, L, ncores):
    shared = pack_shared(inp, L)
    maps = []
    for b in range(ncores):
        m = dict(shared)
        pp = shared["pp"].copy()
        pp[:, 0:8] = _colT(inp["c"][b], 8)
        m["pp"] = pp
        m["xT"] = np.ascontiguousarray(inp["x"][b].T.reshape(8, 128, SEQ).transpose(1, 0, 2)).reshape(128, 8 * SEQ)
        maps.append(m)
    return maps


def unpack_out(yT):
    return np.ascontiguousarray(yT.reshape(128, 8, SEQ).transpose(2, 1, 0).reshape(SEQ, D))


_NC_CACHE = {}


def kernel(**inputs):
    inp = {k: np.asarray(v, dtype=np.float32) for k, v in inputs.items()}
    ncores = 8
    if "nc" not in _NC_CACHE:
        _NC_CACHE["nc"] = build(DEPTH)
    nc = _NC_CACHE["nc"]
    maps = make_in_maps(inp, DEPTH, ncores)
    res = run_bass_kernel_spmd(nc, maps, core_ids=list(range(ncores)))
    out = np.stack([unpack_out(res.results[b]["yT"]) for b in range(ncores)], axis=0)
    return out.astype(np.float32)
```

```python
import contextlib
import os
import numpy as np
CUT = int(os.environ.get('K_CUT', '99'))
import concourse.bass as bass
import concourse.mybir as mybir
from concourse.bass_utils import run_bass_kernel_spmd

F32 = mybir.dt.float32
BF16 = mybir.dt.bfloat16
ALU = mybir.AluOpType
AF = mybir.ActivationFunctionType

ALL_ENG = ("sync", "tensor", "vector", "scalar", "gpsimd")
D = 1024
SEQ = 2048
DEPTH = 4
ALPHA = (2 * DEPTH) ** 0.25
LN_EPS = 1e-5
RMS_EPS = 1e-6
NEG = -30000.0


class Buf:
    __slots__ = ("name", "w", "r", "excl")

    def __init__(self, name="", excl=False):
        self.name = name
        self.w = None
        self.r = []
        self.excl = excl


class Prog:
    def __init__(self, nc, same_engine_sync=True):
        self.nc = nc
        self.ops = {e: [] for e in ALL_ENG}
        self.dsem_count = {}
        self.same = same_engine_sync
        self.es = contextlib.ExitStack()
        self.dsem_names = []

    def sb(self, name, shape, dt):
        return self.es.enter_context(self.nc.sbuf_tensor("sb_" + name, list(shape), dt))

    def ps(self, name, shape, dt):
        return self.es.enter_context(self.nc.psum_tensor("ps_" + name, list(shape), dt))

    def dsem(self, name):
        self.dsem_count[name] = 0
        self.dsem_names.append(name)
        return name

    def _deps(self, eng, reads, writes):
        deps = []
        for b in reads:
            if b.w is not None:
                deps.append(b.w)
        for b in writes:
            if b.w is not None:
                deps.append(b.w)
            deps.extend(b.r)
        out = []
        for d in deps:
            if d[0] == "e" and d[1] == eng and (eng in ("tensor", "sync") or not self.same):
                continue
            if d not in out:
                out.append(d)
        return out

    def op(self, eng, fn, reads=(), writes=()):
        writes = list(writes) + [b for b in reads if b.excl]
        reads = [b for b in reads if not b.excl]
        deps = self._deps(eng, reads, writes)
        idx = len(self.ops[eng])
        self.ops[eng].append(dict(fn=fn, deps=deps, sig=False, dma=None))
        ev = ("e", eng, idx)
        for b in reads:
            b.r.append(ev)
        for b in writes:
            b.w = ev
            b.r = []
        return ev

    def dma(self, eng, fn, dsem, reads=(), writes=()):
        deps = self._deps(eng, reads, writes)
        self.dsem_count[dsem] += 16
        ev = ("d", dsem, self.dsem_count[dsem])
        self.ops[eng].append(dict(fn=fn, deps=deps, sig=False, dma=dsem))
        for b in reads:
            b.r.append(ev)
        for b in writes:
            b.w = ev
            b.r = []
        return ev

    def wait_events(self, eng, evs):
        self.ops[eng].append(dict(fn=None, deps=list(evs), sig=False, dma=None))

    def barrier(self, bufs=()):
        evs = []
        for eng in ("tensor", "vector", "scalar", "gpsimd"):
            n = len(self.ops[eng])
            i = n - 1
            while i >= 0 and (self.ops[eng][i]["fn"] is None or self.ops[eng][i]["dma"] is not None):
                i -= 1
            if i >= 0:
                evs.append(("e", eng, i))
        for eng in ("tensor", "vector", "scalar", "gpsimd"):
            self.wait_events(eng, [e for e in evs if e[1] != eng])
        for b in bufs:
            b.r = list(b.r) + evs
        return evs

    def emit(self):
        nc = self.nc
        for eng in ALL_ENG:
            for rec in self.ops[eng]:
                for d in rec["deps"]:
                    if d[0] == "e":
                        self.ops[d[1]][d[2]]["sig"] = True
        cnt = {}
        for eng in ALL_ENG:
            c = 0
            for i, rec in enumerate(self.ops[eng]):
                if rec["sig"]:
                    c += 1
                    cnt[(eng, i)] = c
        sems = {}
        for eng in ALL_ENG:
            sems[eng] = self.es.enter_context(nc.semaphore("s_" + eng))
        dsems = {}
        for n in self.dsem_names:
            dsems[n] = self.es.enter_context(nc.semaphore("d_" + n))
        self.nwaits = 0

        def run(eng, e):
            known = {}
            for rec in self.ops[eng]:
                for d in rec["deps"]:
                    if d[0] == "e":
                        key = ("e", d[1])
                        val = cnt[(d[1], d[2])]
                        sem = sems[d[1]]
                    else:
                        key = ("d", d[1])
                        val = d[2]
                        sem = dsems[d[1]]
                    if known.get(key, 0) >= val:
                        continue
                    known[key] = val
                    e.wait_ge(sem, val)
                    self.nwaits += 1
                if rec["fn"] is None:
                    continue
                ins = rec["fn"](e)
                if rec["dma"] is not None:
                    ins.then_inc(dsems[rec["dma"]], 16)
                elif rec["sig"]:
                    ins.then_inc(sems[eng], 1)

        with nc.Block() as block:
            @block.sync
            def _(e):
                run("sync", e)

            @block.tensor
            def _(e):
                run("tensor", e)

            @block.vector
            def _(e):
                run("vector", e)

            @block.scalar
            def _(e):
                run("scalar", e)

            @block.gpsimd
            def _(e):
                run("gpsimd", e)
        self.es.close()


class Carver:
    def __init__(self, regions):
        self.regions = regions
        self.off = [0] * len(regions)

    def take(self, nelem, dt):
        nb = nelem * (4 if dt == F32 else 2)
        nb = (nb + 63) // 64 * 64
        for i, r in enumerate(self.regions):
            cap = r.shape[1] * 2
            if self.off[i] + nb <= cap:
                o = self.off[i]
                self.off[i] += nb
                ap = r[:, o // 2:(o + nb) // 2]
                if dt == F32:
                    ap = ap.bitcast(F32)
                return ap[:, 0:nelem]
        raise MemoryError(f"carver out of space for {nelem} {dt}")


NPP_L = 48 + 8 * 4 + 32 + 8 + 96 + 16 + 8 + 8 + 128


def pp_off(L):
    o = {}
    p = 8
    for name, n in (("b_ada", 48), ("ln1_g", 8), ("ln1_b", 8), ("ln2_g", 8), ("ln2_b", 8), ("b_ff1", 32),
                    ("b_ff2", 8), ("convw", 96), ("sinks", 16), ("a_log", 8), ("dt_bias", 8), ("normw", 128)):
        o[name] = (p, n)
        p += n * L
    return o, p


def build(L=DEPTH, stop=None, dbg=False):
    nc = bass.Bass("TRN2", target_bir_lowering=False)
    P = Prog(nc, same_engine_sync=not os.environ.get("K_NOSAME"))
    PO, NPP = pp_off(L)

    xT_d = nc.dram_tensor("xT", [128, 8 * SEQ], F32, kind="ExternalInput").ap()
    pp_d = nc.dram_tensor("pp", [128, NPP], F32, kind="ExternalInput").ap()
    cst_d = nc.dram_tensor("cst", [128, 6 * 128], F32, kind="ExternalInput").ap()
    wada_d = nc.dram_tensor("wada", [L, 8, 128, 6144], F32, kind="ExternalInput").ap()
    wA_d = nc.dram_tensor("wA", [L, 4, 128, 8 * 448], F32, kind="ExternalInput").ap()
    wD_d = nc.dram_tensor("wD", [L, 8, 128, 8 * 512], F32, kind="ExternalInput").ap()
    wBA_d = nc.dram_tensor("wBA", [L, 128, 128], F32, kind="ExternalInput").ap()
    wM_d = nc.dram_tensor("wM", [L, 5, 128, 8192], F32, kind="ExternalInput").ap()
    wF_d = nc.dram_tensor("wF", [L, 8, 128, 8192], F32, kind="ExternalInput").ap()
    yT_d = nc.dram_tensor("yT", [128, 8 * SEQ], F32, kind="ExternalOutput").ap()
    if dbg:
        dbg_d = nc.dram_tensor("dbg", [128, 8192], BF16, kind="ExternalOutput").ap()

    xT = P.sb("xTs", [128, 8 * SEQ], F32)
    xB = [[Buf(f"x{c}_{t}") for t in range(4)] for c in range(8)]
    uT = P.sb("uTs", [128, 8192], BF16)
    uB = [[Buf(f"u{c}_{t}") for t in range(2)] for c in range(8)]
    oT = P.sb("oTs", [128, 8192], BF16)
    oB = [[Buf(f"o{c}_{t}") for t in range(2)] for c in range(8)]
    wB = [P.sb(f"wB{i}", [128, 8192], BF16) for i in range(3)]
    wBB = [Buf(f"wB{i}") for i in range(3)]
    wBsem = [P.dsem(f"wB{i}") for i in range(3)]
    wAs = P.sb("wAs", [128, 8192], BF16)
    wAB = [Buf("wA0"), Buf("wA1")]
    wAsem = [P.dsem("wA0"), P.dsem("wA1")]
    wba = P.sb("wba", [128, 128], BF16)
    wbaB = Buf("wba")
    wbasem = P.dsem("wba")
    ppt = P.sb("ppt", [128, NPP], F32)
    ppB = Buf("pp")
    cst = P.sb("cst", [128, 768], F32)
    cstB = Buf("cst")
    cbf = P.sb("cbf", [128, 4 * 128], BF16)
    cbfB = Buf("cbf")
    negones = P.sb("negones", [128, 128], F32)
    mod = P.sb("mod", [128, L * 48], F32)
    sb2 = P.sb("sb2", [128, L * 8], F32)
    epsT = P.sb("epsT", [128, 4], F32)
    cact = P.sb("cact", [128, 8], F32)
    cactb = P.sb("cactb", [128, 8], BF16)
    cactB = Buf("cact")
    kcar = P.sb("kcar", [128, 4 * 128], BF16)
    vcar = P.sb("vcar", [128, 4 * 128], BF16)
    carB = [Buf(f"car{g}") for g in range(4)]
    ccar = P.sb("ccar", [128, 8 * 9], F32)
    ccarB = [Buf(f"cc{h}") for h in range(8)]
    S_f = P.sb("S_f", [128, 8 * 128], F32)
    S_fB = [Buf(f"S{h}") for h in range(8)]
    bet = P.sb("bet", [128, 64], F32)
    nbet = P.sb("nbet", [128, 64], F32)
    gtk = P.sb("gtk", [128, 64], F32)
    gtmp = P.sb("gtmp", [128, 64], F32)
    nega = P.sb("nega", [128, 8], F32)
    sexp = P.sb("sexp", [128, 16], F32)
    bgB = Buf("bg")
    lyrB = Buf("lyr")
    free_ar = P.sb("free_ar", [128, 14336], BF16)

    ident_f = cst[:, 0:128]
    ones_f = cst[:, 128:256]
    triinc = cst[:, 256:384]
    mls = cst[:, 384:512]
    mupi = cst[:, 512:640]
    ident_b = cbf[:, 0:128]
    ones_b = cbf[:, 128:256]
    mcur_b = cbf[:, 256:384]
    mprev_b = cbf[:, 384:512]

    pbank = [P.ps(f"pb{i}", [128, 512], F32) for i in range(8)]
    pbankB = [Buf(f"pb{i}", excl=True) for i in range(8)]
    st = dict(i=0)

    def big():
        lo, hi = st.get("rng", (0, 8))
        key = ("i", lo, hi)
        i = st.get(key, lo)
        st[key] = lo + (i + 1 - lo) % (hi - lo)
        return pbank[i], pbankB[i]

    def quarter():
        t, b = big()
        return t[:, 0:128], b

    def tslot():
        t, b = big()
        return t[:, :].bitcast(BF16)[:, 0:128], b

    def tslot4():
        t, b = big()
        return t[:, :].bitcast(BF16)[:, 0:512], [b]

    def T(fn, r, w):
        return P.op("tensor", fn, r, w)

    def V(fn, r, w):
        return P.op("vector", fn, r, w)

    def A(fn, r, w):
        return P.op("scalar", fn, r, w)

    def G(fn, r, w):
        return P.op("gpsimd", fn, r, w)

    def mm(out, lhsT, rhs, start, stop, r, w):
        return T(lambda e: e.matmul(out, lhsT, rhs, start=start, stop=stop), r, w)

    def act(out, in_, func, r, w, **kw):
        return A(lambda e: e.activation(out=out, in_=in_, func=func, **kw), r, w)

    def u_(k, a, b):
        return uT[:, k * 1024 + a:k * 1024 + b]

    def x_(c, a, b):
        return xT[:, c * SEQ + a:c * SEQ + b]

    def ppc(name, l, j=0, n=1):
        o, per = PO[name]
        return ppt[:, o + l * per + j:o + l * per + j + n]

    def modc(l, w, c=0, n=8):
        o = l * 48 + w * 8 + c
        return mod[:, o:o + n]

    P.dma("sync", lambda e: e.dma_start(out=ppt[:], in_=pp_d), P.dsem("pp"), writes=[ppB])
    P.dma("sync", lambda e: e.dma_start(out=cst[:], in_=cst_d), P.dsem("cst"), writes=[cstB])
    xsem = P.dsem("x")
    for c in range(8):
        P.dma("sync", lambda e, c=c: e.dma_start(out=xT[:, c * SEQ:(c + 1) * SEQ], in_=xT_d[:, c * SEQ:(c + 1) * SEQ]),
              xsem, writes=xB[c])
    for c in range(8):
        for t_ in range(4):
            xB[c][t_].w = ("d", xsem, 128)
    V(lambda e: e.tensor_copy(out=cbf[:, 0:384], in_=cst[:, 0:384]), [cstB], [cbfB])
    V(lambda e: e.tensor_copy(out=cbf[:, 384:512], in_=cst[:, 640:768]), [cstB], [cbfB])
    V(lambda e: e.memset(negones[:], -1.0), [], [cbfB])
    V(lambda e: e.memset(epsT[:, 0:1], LN_EPS / (ALPHA * ALPHA)), [], [cbfB])
    V(lambda e: e.memset(epsT[:, 1:2], RMS_EPS), [], [cbfB])
    V(lambda e: e.memset(epsT[:, 2:3], 1.0), [], [cbfB])
    V(lambda e: e.memset(epsT[:, 3:4], 0.0), [], [cbfB])
    eps_ln = epsT[:, 0:1]
    eps_rms = epsT[:, 1:2]
    one_c = epsT[:, 2:3]
    act(cact[:], ppt[:, 0:8], AF.Silu, [ppB], [cactB])
    V(lambda e: e.tensor_copy(out=cactb[:], in_=cact[:]), [cactB], [cactB])
    modBs = [Buf(f"mod{l}") for l in range(L)]
    modst = dict(cnt=0)
    o_b, _ = PO["b_ada"]
    o_f2, _ = PO["b_ff2"]

    def mod_dma(l, piece):
        s_ = (l * 8 + piece) % 3
        P.dma("gpsimd", lambda e: e.dma_start(out=wB[s_][:, 0:6144], in_=wada_d[l, piece]), wBsem[s_], writes=[wBB[s_]])

    def mod_piece(l, piece):
        s_ = (l * 8 + piece) % 3
        psm, psmB = big()
        for m in range(6):
            for k in range(8):
                mm(psm[:, m:m + 1], wB[s_][:, k * 768 + m * 128:k * 768 + (m + 1) * 128], cactb[:, k:k + 1],
                   k == 0, k == 7, [wBB[s_], cactB], [psmB])
        c0_ = l * 48 + piece * 6
        V(lambda e: e.tensor_tensor(out=mod[:, c0_:c0_ + 6], in0=psm[:, 0:6], in1=ppt[:, o_b + c0_:o_b + c0_ + 6], op=ALU.add),
          [psmB, ppB], [modBs[l]])

    def mod_final(l):
        for w_ in (1, 4):
            V(lambda e, w_=w_: e.tensor_scalar_add(out=modc(l, w_), in0=modc(l, w_), scalar1=1.0), [modBs[l]], [modBs[l]])
        for w_ in (2, 5):
            V(lambda e, w_=w_: e.tensor_scalar(out=modc(l, w_), in0=modc(l, w_), scalar1=1.0, scalar2=1.0 / ALPHA,
                                               op0=ALU.add, op1=ALU.mult), [modBs[l]], [modBs[l]])
        V(lambda e: e.tensor_tensor(out=sb2[:, l * 8:(l + 1) * 8], in0=modc(l, 5), in1=ppt[:, o_f2 + l * 8:o_f2 + (l + 1) * 8], op=ALU.mult),
          [modBs[l], ppB], [modBs[l]])

    for piece in range(8):
        mod_dma(0, piece)
        mod_piece(0, piece)
    mod_final(0)
    allW = wBB + wAB
    P.barrier(allW)

    def phase_u(l, hf, which):
        scw, shw = (1, 0) if which == 0 else (4, 3)
        i = 0
        for c in range(8):
            for t in range(2):
                T0 = hf * 1024 + t * 512
                if i % 2 == 0:
                    act(u_(c, t * 512, (t + 1) * 512), x_(c, T0, T0 + 512), AF.Identity, [xB[c][hf * 2 + t], modBs[l]],
                        [uB[c][t]], scale=modc(l, scw, c, 1), bias=modc(l, shw, c, 1))
                else:
                    V(lambda e, c=c, t=t, T0=T0: e.tensor_scalar(out=u_(c, t * 512, (t + 1) * 512), in0=x_(c, T0, T0 + 512),
                                                                 scalar1=modc(l, scw, c, 1), scalar2=modc(l, shw, c, 1),
                                                                 op0=ALU.mult, op1=ALU.add),
                      [xB[c][hf * 2 + t], modBs[l]], [uB[c][t]])
                i += 1

    def load_w(eng, dst_ap, src_ap, sem, buf):
        P.dma(eng, lambda e: e.dma_start(out=dst_ap, in_=src_ap), sem, writes=[buf])

    def prefetch_merge(l, X):
        load_w("gpsimd", wB[0][:], wM_d[l, 2 * X], wBsem[0], wBB[0])
        load_w("gpsimd", wB[1][:], wM_d[l, 2 * X + 1], wBsem[1], wBB[1])
        load_w("gpsimd", wB[2][:], wM_d[l, 4], wBsem[2], wBB[2])

    def phase_attn(l, hf):
        cv = Carver([free_ar[:]])
        do_mod = (hf == 0 and l + 1 < L)
        qT = [cv.take(1024, BF16) for _ in range(4)]
        kT2 = cv.take(1152, BF16)
        vaug = cv.take(9 * 128, BF16)
        Et = [[cv.take(512, BF16) for _ in range(2)] for _ in range(2)]
        rec = cv.take(512, F32)
        qB = [Buf(), Buf(), Buf(), Buf()]
        kB = Buf()
        vB = Buf()
        EB = [[Buf(), Buf()], [Buf(), Buf()]]
        recB = Buf()
        act(sexp[:], ppc("sinks", l, 0, 16), AF.Exp, [ppB], [lyrB])
        V(lambda e: e.memset(vaug[:].rearrange("p (b c) -> p b c", c=128)[:, :, 64:128], 1.0), [], [vB])
        for h in range(4):
            V(lambda e, h=h: e.memset(qT[h][:, :], 0.0), [], [qB[h]])
        load_w("gpsimd", wAs[:, 0:3584], wA_d[l, 0], wAsem[0], wAB[0])
        for g in range(4):
            s = g % 2
            if g + 1 < 4:
                load_w("gpsimd", wAs[:, (1 - s) * 4096:(1 - s) * 4096 + 3584], wA_d[l, g + 1], wAsem[1 - s], wAB[1 - s])
            if do_mod and g == 0:
                for pc in range(3):
                    mod_dma(l + 1, pc)
            Wb = s * 4096

            def W(k, a, b):
                return wAs[:, Wb + k * 448 + a:Wb + k * 448 + b]
            for jp in range(2):
                for t in range(2):
                    ps, pb = big()
                    for k in range(8):
                        mm(ps[:, :], W(k, jp * 128, (jp + 1) * 128), u_(k, t * 512, (t + 1) * 512), k == 0, k == 7,
                           [wAB[s], uB[k][t]], [pb])
                    act(qT[2 * jp][0:64, t * 512:(t + 1) * 512], ps[0:64, :], AF.Copy, [pb], [qB[2 * jp]], scale=0.125)
                    act(qT[2 * jp + 1][64:128, t * 512:(t + 1) * 512], ps[64:128, :], AF.Copy, [pb], [qB[2 * jp + 1]], scale=0.125)
            for t in range(2):
                ps, pb = big()
                for k in range(8):
                    mm(ps[:, :], W(k, 256, 384), u_(k, t * 512, (t + 1) * 512), k == 0, k == 7, [wAB[s], uB[k][t]], [pb])
                V(lambda e, ps=ps, t=t: e.tensor_copy(out=kT2[:, 128 + t * 512:128 + (t + 1) * 512], in_=ps[:, :]), [pb], [kB])
            ps, pb = big()
            for blk in range(8):
                for k in range(8):
                    mm(ps[:, blk * 64:(blk + 1) * 64], u_(k, blk * 128, (blk + 1) * 128), W(k, 384, 448), k == 0, k == 7,
                       [wAB[s], uB[k][blk // 4]], [pb])
            V(lambda e, ps=ps: e.tensor_copy(out=vaug[:].rearrange("p (b c) -> p b c", c=128)[:, 1:9, 0:64],
                                             in_=ps[:, :].rearrange("p (b c) -> p b c", c=64)), [pb], [vB])
            if CUT <= 1:
                return
            if hf == 1:
                V(lambda e, g=g: e.tensor_copy(out=kT2[:, 0:128], in_=kcar[:, g * 128:(g + 1) * 128]), [carB[g]], [kB])
                V(lambda e, g=g: e.tensor_copy(out=vaug[:, 0:64], in_=vcar[:, g * 128:g * 128 + 64]), [carB[g]], [vB])
            def stage1(n, g=g):
                N = hf * 8 + n
                js = [0, 1] if N > 0 else [1]
                par = n % 2
                for j in js:
                    ps, pb = big()
                    for h in range(4):
                        mm(ps[:, h * 128:(h + 1) * 128], kT2[:, (n + j) * 128:(n + j + 1) * 128],
                           qT[h][:, n * 128:(n + 1) * 128], True, True, [kB, qB[h]], [pb])
                    E = Et[j][par]
                    act(E[:, :], ps[:, :], AF.Exp, [pb], [EB[j][par]])
                    mk = mcur_b if j == 1 else mprev_b
                    V(lambda e, E=E, mk=mk: e.tensor_tensor(out=E.rearrange("p (h q) -> p h q", h=4),
                                                            in0=E.rearrange("p (h q) -> p h q", h=4),
                                                            in1=mk.unsqueeze(1).to_broadcast([128, 4, 128]), op=ALU.mult),
                      [EB[j][par], cbfB], [EB[j][par]])
                return js

            def stage2(n, js, g=g):
                par = n % 2
                ps, pb = big()
                for idx, j in enumerate(js):
                    mm(ps[:, :], vaug[:, (n + j) * 128:(n + j + 1) * 128], Et[j][par][:, :], idx == 0, idx == len(js) - 1,
                       [vB, EB[j][par]], [pb])
                V(lambda e, ps=ps: e.tensor_tensor(out=rec[0:64, :].rearrange("p (h q) -> p h q", h=4),
                                                   in0=ps[64:128, :].rearrange("p (h q) -> p h q", h=4),
                                                   in1=sexp[64:128, g * 4:(g + 1) * 4].unsqueeze(2).to_broadcast([64, 4, 128]),
                                                   op=ALU.add), [pb, lyrB], [recB])
                act(rec[0:64, :], rec[0:64, :], AF.Ln, [recB], [recB])
                act(rec[0:64, :], rec[0:64, :], AF.Exp, [recB], [recB], scale=-1.0)
                psv = ps[0:64, :].rearrange("p (a b q) -> p a b q", a=2, b=2)
                rcv = rec[0:64, :].rearrange("p (a b q) -> p a b q", a=2, b=2)
                for odd in range(2):
                    c0 = 2 * g
                    dst = oT[odd * 64:odd * 64 + 64, :].rearrange("p (c t) -> p c t", c=8)[:, c0:c0 + 2, n * 128:(n + 1) * 128]
                    V(lambda e, dst=dst, odd=odd, psv=psv, rcv=rcv: e.tensor_tensor(out=dst, in0=psv[:, :, odd, :], in1=rcv[:, :, odd, :],
                                                                                    op=ALU.mult),
                      [pb, recB], [oB[c0][n // 4], oB[c0 + 1][n // 4]])
            js_cur = stage1(0)
            for n in range(8):
                js_nxt = stage1(n + 1) if n + 1 < 8 else None
                stage2(n, js_cur)
                js_cur = js_nxt
            if hf == 0:
                V(lambda e, g=g: e.tensor_copy(out=kcar[:, g * 128:(g + 1) * 128], in_=kT2[:, 1024:1152]), [kB], [carB[g]])
                V(lambda e, g=g: e.tensor_copy(out=vcar[:, g * 128:g * 128 + 64], in_=vaug[:, 8 * 128:8 * 128 + 64]), [vB], [carB[g]])
            if do_mod:
                for pc in (2 * g, 2 * g + 1):
                    mod_piece(l + 1, pc)
                    if pc + 3 < 8:
                        mod_dma(l + 1, pc + 3)
        if do_mod:
            mod_final(l + 1)
        prefetch_merge(l, 0)

    def ln_alloc(cv):
        return dict(sq=cv.take(512, F32), sacc=cv.take(512, F32), qacc=cv.take(512, F32), mu=cv.take(512, F32),
                    rstd=cv.take(512, F32), tmp=[cv.take(512, F32) for _ in range(2)],
                    B=[Buf() for _ in range(5)], tB=[Buf(), Buf()])

    def layer_norm(l, Tg_, gname, bname, tm):
        sq, sacc, qacc, mu, rstd, tmp = tm["sq"], tm["sacc"], tm["qacc"], tm["mu"], tm["rstd"], tm["tmp"]
        sqB, saB, qaB, muB, rsB = tm["B"]
        tB = tm["tB"]
        a0 = Tg_ * 512
        for c in range(8):
            xs = x_(c, a0, a0 + 512)
            if c == 0:
                V(lambda e, xs=xs: e.tensor_copy(out=sacc, in_=xs), [xB[c][Tg_]], [saB])
                act(qacc, xs, AF.Square, [xB[c][Tg_]], [qaB])
            else:
                V(lambda e, xs=xs: e.tensor_tensor(out=sacc, in0=sacc, in1=xs, op=ALU.add), [xB[c][Tg_], saB], [saB])
                act(sq, xs, AF.Square, [xB[c][Tg_]], [sqB])
                V(lambda e: e.tensor_tensor(out=qacc, in0=qacc, in1=sq, op=ALU.add), [sqB, qaB], [qaB])
            yield
        p1, p1B = big()
        mm(p1[:, :], ones_f, sacc, True, True, [saB, cstB], [p1B])
        p2, p2B = big()
        mm(p2[:, :], ones_f, qacc, True, True, [qaB, cstB], [p2B])
        act(mu, p1[:, :], AF.Copy, [p1B], [muB], scale=1.0 / D)
        V(lambda e: e.tensor_tensor(out=sq, in0=mu, in1=mu, op=ALU.mult), [muB, sqB], [sqB])
        V(lambda e: e.scalar_tensor_tensor(out=rstd, in0=p2[:, :], scalar=1.0 / D, in1=sq, op0=ALU.mult, op1=ALU.subtract),
          [p2B, sqB], [rsB])
        act(rstd, rstd, AF.Ln, [rsB], [rsB], bias=eps_ln, scale=1.0)
        act(rstd, rstd, AF.Exp, [rsB], [rsB], scale=-0.5)
        yield
        for c in range(8):
            xs = x_(c, a0, a0 + 512)
            tt = tmp[c % 2]
            V(lambda e, xs=xs, tt=tt: e.tensor_tensor(out=tt, in0=xs, in1=mu, op=ALU.subtract), [xB[c][Tg_], muB], [tB[c % 2]])
            V(lambda e, tt=tt: e.tensor_tensor(out=tt, in0=tt, in1=rstd, op=ALU.mult), [tB[c % 2], rsB], [tB[c % 2]])
            act(xs, tt, AF.Identity, [tB[c % 2], ppB], [xB[c][Tg_]], scale=ppc(gname, l, c, 1), bias=ppc(bname, l, c, 1))
            yield

    def phase_merge(l, hf, X, do_ln):
        if X == 1:
            prefetch_merge(l, X)
        cv = Carver([free_ar[:], wAs[:]])
        sg = [cv.take(512, F32) for _ in range(2)]
        mg = cv.take(8 * 512, BF16)
        sgB = [Buf(), Buf()]
        mgB = [Buf() for _ in range(8)]
        lnt = ln_alloc(cv) if do_ln else None
        for t in range(2):
            Tg_ = hf * 2 + t
            for c in range(8):
                py, pyB = big()
                for k in range(8):
                    mm(py[:, :], wB[0][:, k * 1024 + c * 128:k * 1024 + (c + 1) * 128], oT[:, k * 1024 + t * 512:k * 1024 + (t + 1) * 512],
                       k == 0, k == 7, [wBB[0], oB[k][t]], [pyB])
                pg, pgB = big()
                for k in range(8):
                    mm(pg[:, :], wB[1][:, k * 1024 + c * 128:k * 1024 + (c + 1) * 128], u_(k, t * 512, (t + 1) * 512),
                       k == 0, k == 7, [wBB[1], uB[k][t]], [pgB])
                act(sg[c % 2], pg[:, :], AF.Sigmoid, [pgB], [sgB[c % 2]])
                V(lambda e, c=c, py=py: e.tensor_tensor(out=mg[:, c * 512:(c + 1) * 512], in0=py[:, :], in1=sg[c % 2], op=ALU.mult),
                  [pyB, sgB[c % 2]], [mgB[c]])
            for c2 in range(8):
                pm, pmB = big()
                for c in range(8):
                    mm(pm[:, :], wB[2][:, c * 1024 + c2 * 128:c * 1024 + (c2 + 1) * 128], mg[:, c * 512:(c + 1) * 512],
                       c == 0, c == 7, [wBB[2], mgB[c]], [pmB])
                xs = x_(c2, Tg_ * 512, Tg_ * 512 + 512)
                V(lambda e, pm=pm, xs=xs, c2=c2: e.scalar_tensor_tensor(out=xs, in0=pm[:, :], scalar=modc(l, 2, c2, 1), in1=xs,
                                                                         op0=ALU.mult, op1=ALU.add),
                  [pmB, modBs[l], xB[c2][Tg_]], [xB[c2][Tg_]])
            if do_ln and t == 1:
                load_w("gpsimd", wB[0][:], wF_d[l, 0], wBsem[0], wBB[0])
                load_w("gpsimd", wB[1][:], wF_d[l, 1], wBsem[1], wBB[1])
            if do_ln:
                for _ in layer_norm(l, Tg_, "ln1_g", "ln1_b", lnt):
                    pass

    def phase_ffn(l, hf):
        cv = Carver([free_ar[:], wAs[:]])
        rl = [cv.take(512, F32) for _ in range(2)]
        hT = [cv.take(4 * 512, BF16) for _ in range(2)]
        rlB = [Buf(), Buf()]
        hB = [[Buf() for _ in range(4)] for _ in range(2)]
        lnt = ln_alloc(cv)
        lnt2 = ln_alloc(cv)
        phase_u(l, hf, 1)
        for c in range(8):
            for t in range(2):
                Tg_ = hf * 2 + t
                xs = x_(c, Tg_ * 512, Tg_ * 512 + 512)
                V(lambda e, xs=xs, c=c: e.tensor_scalar(out=xs, in0=xs, scalar1=sb2[:, l * 8 + c:l * 8 + c + 1], scalar2=None, op0=ALU.add),
                  [xB[c][Tg_], modBs[l]], [xB[c][Tg_]])
        it = 0
        for F in range(8):
            s = F % 3
            if F + 2 < 8:
                s2 = (F + 2) % 3
                load_w("gpsimd", wB[s2][:], wF_d[l, F + 2], wBsem[s2], wBB[s2])
            for t in range(2):
                Tg_ = hf * 2 + t
                hp = it % 2
                it += 1
                for f in range(4):
                    ph, phB = big()
                    for k in range(8):
                        mm(ph[:, :], wB[s][:, k * 512 + f * 128:k * 512 + (f + 1) * 128], u_(k, t * 512, (t + 1) * 512),
                           k == 0, k == 7, [wBB[s], uB[k][t]], [phB])
                    r_ = rl[f % 2]
                    act(r_, ph[:, :], AF.Relu, [phB, ppB], [rlB[f % 2]], bias=ppc("b_ff1", l, F * 4 + f, 1), scale=1.0)
                    V(lambda e, r_=r_, hp=hp, f=f: e.tensor_tensor(out=hT[hp][:, f * 512:(f + 1) * 512], in0=r_, in1=r_, op=ALU.mult),
                      [rlB[f % 2]], [hB[hp][f]])
                for c2 in range(8):
                    po_, poB = big()
                    for f in range(4):
                        mm(po_[:, :], wB[s][:, 4096 + f * 1024 + c2 * 128:4096 + f * 1024 + (c2 + 1) * 128], hT[hp][:, f * 512:(f + 1) * 512],
                           f == 0, f == 3, [wBB[s], hB[hp][f]], [poB])
                    xs = x_(c2, Tg_ * 512, Tg_ * 512 + 512)
                    V(lambda e, po_=po_, xs=xs, c2=c2: e.scalar_tensor_tensor(out=xs, in0=po_[:, :], scalar=modc(l, 5, c2, 1), in1=xs,
                                                                               op0=ALU.mult, op1=ALU.add),
                      [poB, modBs[l], xB[c2][Tg_]], [xB[c2][Tg_]])
        gA = layer_norm(l, hf * 2, "ln2_g", "ln2_b", lnt)
        gB = layer_norm(l, hf * 2 + 1, "ln2_g", "ln2_b", lnt2)
        done = 0
        while done < 2:
            done = 0
            for g_ in (gA, gB):
                try:
                    next(g_)
                except StopIteration:
                    done += 1

    def interleave(*gens, rngs=None, reps=None):
        gens = [(g, (rngs[i] if rngs else (0, 8)), (reps[i] if reps else 1)) for i, g in enumerate(gens) if g is not None]
        while gens:
            for it in list(gens):
                st["rng"] = it[1]
                try:
                    for _ in range(it[2]):
                        next(it[0])
                except StopIteration:
                    gens.remove(it)
        st["rng"] = (0, 8)

    def chain(*gens):
        for g in gens:
            if g is not None:
                yield from g

    def phase_dn(l, hf):
        cv = Carver([free_ar[:], wB[0][:], wB[1][:], wB[2][:]])
        W5 = 512
        pre = cv.take(1028, BF16)
        dg = cv.take(512, BF16)
        dgB = Buf()
        acc = cv.take(1024, F32)
        sqb = cv.take(512, BF16)
        rn = cv.take(512, F32)
        qT = cv.take(1024, BF16)
        kT = cv.take(1024, BF16)
        vT = cv.take(1024, BF16)
        kk = cv.take(1024, BF16)
        vv = cv.take(1024, BF16)
        sz = [cv.take(1024, BF16) for _ in range(2)]
        Tg4 = cv.take(W5, F32)
        D4 = cv.take(W5, F32)
        DT4 = cv.take(W5, F32)
        eg4 = cv.take(W5, F32)
        Pab = [cv.take(W5, F32) for _ in range(2)]
        PTab = [cv.take(W5, F32) for _ in range(2)]
        PabB = [Buf(), Buf()]
        PTabB = [Buf(), Buf()]
        Za = cv.take(W5, F32)
        Zb4 = cv.take(W5, BF16)
        sm4 = cv.take(32, F32)
        ab = [dict(P0=cv.take(W5, F32), Rw=cv.take(W5, BF16), vb=cv.take(W5, BF16), B=Buf()) for _ in range(2)]
        bun = [dict(qdT=cv.take(W5, BF16), kdec=cv.take(W5, BF16), intraT=cv.take(W5, BF16), negWT=cv.take(W5, BF16),
                    U=cv.take(W5, F32), last=cv.take(4, F32), B=Buf(), B2=Buf()) for _ in range(3)]
        vnew = cv.take(128, BF16)
        S_b = cv.take(128, BF16)
        tt = cv.take(128, F32)
        junk = cv.take(128, F32)
        obt = cv.take(128, BF16)
        smc = cv.take(8, F32)
        preB, accB, sqbB, rnB, qTB, kTB, vTB, kkB, vvB = (Buf() for _ in range(9))
        szB = [Buf(), Buf()]
        TgB, DB, DTB, egB, PaB, PTaB, ZaB, ZbB, RwB, vbB, smB = (Buf() for _ in range(11))
        vnB, SbB, ttB, jkB, obB, scB = (Buf() for _ in range(6))
        gl4, t4, egp4, ekd4, bege4 = (sm4[:, i * 4:(i + 1) * 4] for i in range(5))
        ssq, rms = smc[:, 0:1], smc[:, 1:2]

        def v4(ap):
            return ap.rearrange("p (i f) -> p i f", i=4)

        def bc1(ap):
            return ap.unsqueeze(1).to_broadcast([128, 4, 128])

        def bc2(ap):
            return ap.unsqueeze(2).to_broadcast([128, 4, 128])

        load_w("gpsimd", wba[:], wBA_d[l], wbasem, wbaB)
        ps, pb = big()
        for blk in range(8):
            for k in range(8):
                mm(ps[:, blk * 16:(blk + 1) * 16], u_(k, blk * 128, (blk + 1) * 128), wba[:, k * 16:(k + 1) * 16], k == 0, k == 7,
                   [wbaB, uB[k][blk // 4]], [pb])
        psv = ps[:, 0:128].rearrange("p (b j) -> p b j", j=16)
        b3 = bet[:].rearrange("p (b j) -> p b j", j=8)
        nb3 = nbet[:].rearrange("p (b j) -> p b j", j=8)
        g3 = gtk[:].rearrange("p (b j) -> p b j", j=8)
        gt3 = gtmp[:].rearrange("p (b j) -> p b j", j=8)
        act(b3, psv[:, :, 0:8], AF.Sigmoid, [pb], [bgB])
        V(lambda e: e.tensor_scalar(out=nbet[:], in0=bet[:], scalar1=-1.0, scalar2=None, op0=ALU.mult), [bgB], [bgB])
        V(lambda e: e.tensor_tensor(out=gt3, in0=psv[:, :, 8:16], in1=ppc("dt_bias", l, 0, 8).unsqueeze(1).to_broadcast([128, 8, 8]),
                                    op=ALU.add), [pb, ppB], [bgB])
        act(gtmp[:], gtmp[:], AF.Exp, [bgB], [bgB])
        act(gtmp[:], gtmp[:], AF.Ln, [bgB], [bgB], bias=one_c, scale=1.0)
        act(nega[:], ppc("a_log", l, 0, 8), AF.Exp, [ppB], [bgB])
        V(lambda e: e.scalar_tensor_tensor(out=g3, in0=gt3, scalar=-1.0, in1=nega[:].unsqueeze(1).to_broadcast([128, 8, 8]),
                                           op0=ALU.mult, op1=ALU.mult), [bgB], [bgB])
        normw = ppc("normw", l, 0, 128)
        cwo, _ = PO["convw"]

        def preamble(h):
            s = h % 2
            if h == 0:
                load_w("gpsimd", wAs[:, 0:4096], wD_d[l, 0], wAsem[0], wAB[0])
            if h + 1 < 8:
                load_w("gpsimd", wAs[:, (1 - s) * 4096:(2 - s) * 4096], wD_d[l, h + 1], wAsem[1 - s], wAB[1 - s])
            Wb = s * 4096

            def W(k, a, b):
                return wAs[:, Wb + k * 512 + a:Wb + k * 512 + b]
            for X in range(3):
                cc = ccar[:, (h * 3 + X) * 3:(h * 3 + X) * 3 + 3]
                if hf == 0:
                    V(lambda e: e.memset(pre[:, 0:3], 0.0), [], [preB])
                else:
                    V(lambda e, cc=cc: e.tensor_copy(out=pre[:, 0:3], in_=cc), [ccarB[h]], [preB])

                def cw(j, X=X, h=h):
                    o = cwo + l * 96 + j * 24 + X * 8 + h
                    return ppt[:, o:o + 1]
                for j in range(4):
                    V(lambda e, j=j, cw=cw: e.tensor_scalar(out=dg[:, j * 128:(j + 1) * 128], in0=ident_b, scalar1=cw(j), scalar2=None, op0=ALU.mult),
                      [cbfB, ppB], [dgB])
                for t in range(2):
                    ps, pb = big()
                    for k in range(8):
                        mm(ps[:, :], W(k, X * 128, (X + 1) * 128), u_(k, t * 512, (t + 1) * 512), k == 0, k == 7, [wAB[s], uB[k][t]], [pb])
                    act(pre[:, 3 + t * 512:3 + (t + 1) * 512], ps[:, :], AF.Copy, [pb], [preB])
                    yield
                if hf == 0:
                    V(lambda e, cc=cc: e.tensor_copy(out=cc, in_=pre[:, 1024:1027]), [preB], [ccarB[h]])
                for t in range(2):
                    ps, pb = big()
                    for j in range(4):
                        mm(ps[:, :], dg[:, j * 128:(j + 1) * 128], pre[:, t * 512 + j:t * 512 + j + 512], j == 0, j == 3, [dgB, preB], [pb])
                    act(acc[:, t * 512:(t + 1) * 512], ps[:, :], AF.Silu, [pb], [accB])
                    yield
                if X < 2:
                    for t in range(2):
                        at_ = acc[:, t * 512:(t + 1) * 512]
                        act(sqb, at_, AF.Square, [accB], [sqbB])
                        ps, pb = big()
                        mm(ps[:, :], ones_b, sqb, True, True, [sqbB, cbfB], [pb])
                        act(rn, ps[:, :], AF.Ln, [pb], [rnB], bias=eps_rms, scale=1.0)
                        act(rn, rn, AF.Exp, [rnB], [rnB], scale=-0.5)
                        if X == 0:
                            V(lambda e, at_=at_, t=t: e.scalar_tensor_tensor(out=qT[:, t * 512:(t + 1) * 512], in0=at_, scalar=128 ** -0.5, in1=rn,
                                                                             op0=ALU.mult, op1=ALU.mult), [accB, rnB], [qTB])
                        else:
                            V(lambda e, at_=at_, t=t: e.tensor_tensor(out=kT[:, t * 512:(t + 1) * 512], in0=at_, in1=rn, op=ALU.mult),
                              [accB, rnB], [kTB])
                        yield
                else:
                    for t in range(2):
                        G(lambda e, t=t: e.tensor_copy(out=vT[:, t * 512:(t + 1) * 512], in_=acc[:, t * 512:(t + 1) * 512]), [accB], [vTB])
                    yield
            for h2 in range(2):
                ps, pb = big()
                for b4 in range(4):
                    blk = h2 * 4 + b4
                    for k in range(8):
                        mm(ps[:, b4 * 128:(b4 + 1) * 128], u_(k, blk * 128, (blk + 1) * 128), W(k, 384, 512), k == 0, k == 7,
                           [wAB[s], uB[k][blk // 4]], [pb])
                act(sz[s][:, h2 * 512:(h2 + 1) * 512], ps[:, :], AF.Silu, [pb], [szB[s]])
                yield
            for src, srcB, dst, dstB in ((kT, kTB, kk, kkB), (vT, vTB, vv, vvB)):
                for h2 in range(2):
                    pt4, pt4B = tslot4()
                    for b4 in range(4):
                        blk = h2 * 4 + b4
                        T(lambda e, pt4=pt4, b4=b4, src=src, blk=blk: e.transpose(pt4[:, b4 * 128:(b4 + 1) * 128],
                                                                                  src[:, blk * 128:(blk + 1) * 128], ident_b),
                          [srcB, cbfB], pt4B)
                    act(dst[:, h2 * 512:(h2 + 1) * 512], pt4, AF.Copy, pt4B, [dstB])
                    yield

        def prescanA(h, t, gi):
            n0 = 4 * t
            a0 = t * 512
            bu = bun[gi % 3]
            abu = ab[gi % 2]
            abB = abu["B"]
            P0, Rw4, vb4 = abu["P0"], abu["Rw"], abu["vb"]
            gsel = g3[:, n0:n0 + 4, h]
            bsel = b3[:, n0:n0 + 4, h]
            nbsel = nb3[:, n0:n0 + 4, h]
            V(lambda e: e.tensor_tensor(out=v4(Tg4), in0=bc1(triinc), in1=bc2(gsel), op=ALU.mult), [cstB, bgB], [TgB])
            pgr, pgrB = big()
            mm(pgr[:, :], ones_f, Tg4, True, True, [cstB, TgB], [pgrB])
            pgd, pgdB = big()
            mm(pgd[:, :], negones[:], Tg4, True, False, [cbfB, TgB], [pgdB])
            for i in range(4):
                mm(pgd[:, i * 128:(i + 1) * 128], Tg4[:, i * 128:(i + 1) * 128], ones_f, False, i == 3, [cstB, TgB], [pgdB])
            yield
            V(lambda e: e.tensor_tensor(out=v4(D4), in0=v4(pgd[:, :]), in1=bc1(mls), op=ALU.add), [pgdB, cstB], [DB])
            act(D4, D4, AF.Exp, [DB], [DB])
            V(lambda e: e.tensor_tensor(out=v4(DT4), in0=bc1(mupi), in1=v4(pgd[:, :]), op=ALU.subtract), [pgdB, cstB], [DTB])
            act(DT4, DT4, AF.Exp, [DTB], [DTB])
            act(eg4, pgr[:, :], AF.Exp, [pgrB], [egB])
            yield
            pgr_l = v4(pgr[:, :])[:, :, 127]
            pgd_l = v4(pgd[:, :])[:, :, 127]
            act(gl4, pgr_l, AF.Copy, [pgrB], [smB])
            V(lambda e: e.tensor_tensor(out=t4, in0=pgd_l, in1=gl4, op=ALU.add), [pgdB, smB], [smB])
            act(egp4, t4, AF.Exp, [smB], [smB])
            act(ekd4, pgd_l, AF.Exp, [pgdB, smB], [smB], scale=-1.0)
            V(lambda e: e.tensor_tensor(out=bege4, in0=egp4, in1=bsel, op=ALU.mult), [smB, bgB], [smB])
            act(bu["last"], v4(eg4)[:, :, 127], AF.Copy, [egB], [bu["B"]])
            yield
            pkk, pkkB = big()
            for i in range(4):
                ks = kT[:, a0 + i * 128:a0 + (i + 1) * 128]
                mm(pkk[:, i * 128:(i + 1) * 128], ks, ks, True, True, [kTB], [pkkB])
            pqk, pqkB = big()
            for i in range(4):
                ks = kT[:, a0 + i * 128:a0 + (i + 1) * 128]
                mm(pqk[:, i * 128:(i + 1) * 128], ks, qT[:, a0 + i * 128:a0 + (i + 1) * 128], True, True, [kTB, qTB], [pqkB])
            yield
            V(lambda e: e.tensor_tensor(out=P0, in0=pkk[:, :], in1=D4, op=ALU.mult), [pkkB, DB], [abB])
            V(lambda e: e.tensor_tensor(out=v4(P0), in0=v4(P0), in1=bc2(nbsel), op=ALU.mult), [abB, bgB], [abB])
            V(lambda e: e.tensor_tensor(out=bu["intraT"], in0=pqk[:, :], in1=DT4, op=ALU.mult), [pqkB, DTB], [bu["B"]])
            yield
            G(lambda e: e.tensor_tensor(out=bu["qdT"], in0=qT[:, a0:a0 + 512], in1=eg4, op=ALU.mult), [qTB, egB], [bu["B"]])
            G(lambda e: e.tensor_tensor(out=v4(bu["kdec"]), in0=v4(kk[:, a0:a0 + 512]), in1=bc2(ekd4), op=ALU.mult), [kkB, smB], [bu["B"]])
            G(lambda e: e.tensor_tensor(out=v4(Rw4), in0=v4(kk[:, a0:a0 + 512]), in1=bc2(bege4), op=ALU.mult), [kkB, smB], [abB])
            G(lambda e: e.tensor_tensor(out=v4(vb4), in0=v4(vv[:, a0:a0 + 512]), in1=bc2(bsel), op=ALU.mult), [vvB, bgB], [abB])
            yield

        def prescanB(h, t, gi):
            bu = bun[gi % 3]
            abu = ab[gi % 2]
            abB = abu["B"]
            P0, Rw4, vb4 = abu["P0"], abu["Rw"], abu["vb"]
            ptp, ptpB = big()
            for i in range(4):
                T(lambda e, i=i: e.transpose(ptp[:, i * 128:(i + 1) * 128], P0[:, i * 128:(i + 1) * 128], ident_f), [abB, cstB], [ptpB])
            act(PTab[0], ptp[:, :], AF.Copy, [ptpB], [PTabB[0]])
            V(lambda e: e.tensor_tensor(out=v4(Za), in0=v4(ptp[:, :]), in1=bc1(ident_f), op=ALU.add), [ptpB, cstB], [ZaB])
            yield
            pend = None
            for s_ in range(1, 7):
                cur, prv = s_ % 2, (s_ - 1) % 2
                Pin, PinB = (P0, abB) if s_ == 1 else (Pab[prv], PabB[prv])
                PTin, PTinB = PTab[prv], PTabB[prv]
                pp_, ppB_ = big()
                for i in range(4):
                    sl = slice(i * 128, (i + 1) * 128)
                    mm(pp_[:, sl], PTin[:, sl], Pin[:, sl], True, True, [PTinB, PinB], [ppB_])
                if os.environ.get("K_FINE", "0") == "1":
                    yield
                if s_ < 6:
                    pq_, pqB_ = big()
                    for i in range(4):
                        sl = slice(i * 128, (i + 1) * 128)
                        mm(pq_[:, sl], Pin[:, sl], PTin[:, sl], True, True, [PTinB, PinB], [pqB_])
                if os.environ.get("K_FINE", "0") == "1":
                    yield
                if pend is not None:
                    pend()
                    if os.environ.get("K_FINE", "0") == "1":
                        yield
                act(Pab[cur], pp_[:, :], AF.Copy, [ppB_], [PabB[cur]])
                if s_ < 6:
                    V(lambda e, pq_=pq_, cur=cur: e.tensor_copy(out=PTab[cur], in_=pq_[:, :]), [pqB_], [PTabB[cur]])
                yield

                def zupd(cur=cur):
                    pz_, pzB_ = big()
                    for i in range(4):
                        sl = slice(i * 128, (i + 1) * 128)
                        mm(pz_[:, sl], Pab[cur][:, sl], Za[:, sl], True, True, [PabB[cur], ZaB], [pzB_])
                    V(lambda e, pz_=pz_: e.tensor_tensor(out=Za, in0=pz_[:, :], in1=Za, op=ALU.add), [pzB_, ZaB], [ZaB])
                pend = zupd
            pend()
            yield
            V(lambda e: e.tensor_copy(out=Zb4, in_=Za), [ZaB], [ZbB])
            pu, puB = big()
            for i in range(4):
                sl = slice(i * 128, (i + 1) * 128)
                mm(pu[:, sl], Zb4[:, sl], vb4[:, sl], True, True, [ZbB, abB], [puB])
            act(bu["U"], pu[:, :], AF.Copy, [puB], [bu["B2"]])
            pw, pwB = big()
            for i in range(4):
                sl = slice(i * 128, (i + 1) * 128)
                mm(pw[:, sl], Rw4[:, sl], Zb4[:, sl], True, True, [ZbB, abB], [pwB])
            act(bu["negWT"], pw[:, :], AF.Copy, [pwB], [bu["B2"]], scale=-1.0)
            yield

        def scan(h, t, gi):
            bu = bun[gi % 3]
            bB = bu["B"]
            b2 = bu["B2"]
            s = h % 2
            Sh = S_f[:, h * 128:(h + 1) * 128]
            if t == 0:
                if hf == 0:
                    V(lambda e: e.memset(Sh, 0.0), [], [S_fB[h]])
                V(lambda e: e.tensor_copy(out=S_b, in_=Sh), [S_fB[h]], [SbB])
            for i in range(4):
                n = 4 * t + i
                c0 = n * 128
                sl = slice(i * 128, (i + 1) * 128)
                p1, p1B = quarter()
                mm(p1, bu["negWT"][:, sl], S_b, True, True, [b2, SbB], [p1B])
                V(lambda e, p1=p1, sl=sl: e.tensor_tensor(out=vnew, in0=p1, in1=bu["U"][:, sl], op=ALU.add), [p1B, b2], [vnB])
                yield
                po_, poB = quarter()
                mm(po_, bu["qdT"][:, sl], S_b, True, False, [bB, SbB], [poB])
                mm(po_, bu["intraT"][:, sl], vnew, False, True, [bB, vnB], [poB])
                pS, pSB = quarter()
                mm(pS, bu["kdec"][:, sl], vnew, True, True, [bB, vnB], [pSB])
                V(lambda e, pS=pS, i=i: e.scalar_tensor_tensor(out=Sh, in0=Sh, scalar=bu["last"][:, i:i + 1], in1=pS,
                                                               op0=ALU.mult, op1=ALU.add), [S_fB[h], pSB, bB], [S_fB[h]])
                V(lambda e: e.tensor_copy(out=S_b, in_=Sh), [S_fB[h]], [SbB])
                yield
                V(lambda e: e.memset(ssq, 0.0), [], [scB])
                act(junk, po_, AF.Square, [poB, scB], [jkB, scB], accum_out=ssq)
                act(rms, ssq, AF.Ln, [scB], [scB], bias=eps_rms, scale=1.0 / 128)
                act(rms, rms, AF.Exp, [scB], [scB], scale=-0.5)
                V(lambda e, po_=po_: e.scalar_tensor_tensor(out=tt, in0=po_, scalar=rms, in1=normw, op0=ALU.mult, op1=ALU.mult),
                  [poB, scB, ppB], [ttB])
                yield
                G(lambda e, c0=c0: e.tensor_tensor(out=obt, in0=tt, in1=sz[s][:, c0:c0 + 128], op=ALU.mult), [ttB, szB[s]], [obB])
                pt2, pt2B = tslot()
                T(lambda e, pt2=pt2: e.transpose(pt2, obt, ident_b), [obB, cbfB], [pt2B])
                act(oT[:, h * 1024 + c0:h * 1024 + c0 + 128], pt2, AF.Copy, [pt2B], [oB[h][n // 4]])
                yield

        groups = [(h, t) for h in range(8) for t in range(2)]
        NG = len(groups)

        def a_stream(gi):
            if gi >= NG:
                return None
            h, t = groups[gi]
            if t == 0:
                return chain(preamble(h), prescanA(h, t, gi))
            return prescanA(h, t, gi)
        interleave(a_stream(0))
        SR = int(os.environ.get("K_SR", "1"))
        for gi in range(NG + 1):
            gens, rngs, reps = [], [], []
            if gi >= 1:
                gens.append(scan(groups[gi - 1][0], groups[gi - 1][1], gi - 1))
                rngs.append((6, 8))
                reps.append(SR)
            if gi < NG:
                gens.append(prescanB(groups[gi][0], groups[gi][1], gi))
                rngs.append((3, 6))
                reps.append(int(os.environ.get("K_BR", "1")))
            nxt = a_stream(gi + 1)
            if nxt is not None:
                gens.append(nxt)
                rngs.append((0, 3))
                reps.append(int(os.environ.get("K_AR", "1")))
            interleave(*gens, rngs=rngs, reps=reps)

    for l in range(L):
        if stop == "p0":
            break
        for hf in range(2):
            phase_u(l, hf, 0)
            if stop == "u":
                break
            phase_attn(l, hf)
            P.barrier(allW)
            if stop == "attn":
                break
            phase_merge(l, hf, 0, False)
            P.barrier(allW)
            if stop == "ma":
                break
            phase_dn(l, hf)
            P.barrier(allW)
            if stop == "dn":
                break
            phase_merge(l, hf, 1, True)
            P.barrier(allW)
            if stop == "ln1":
                continue
            phase_ffn(l, hf)
            P.barrier(allW)
        if stop is not None and stop != "ln1":
            break

    osem = P.dsem("out")
    evs = []
    for c in range(8):
        evs.append(P.dma("sync", lambda e, c=c: e.dma_start(out=yT_d[:, c * SEQ:(c + 1) * SEQ], in_=xT[:, c * SEQ:(c + 1) * SEQ]),
                         osem, reads=xB[c]))
    if dbg:
        dsm = P.dsem("dbg")
        evs.append(P.dma("sync", lambda e: e.dma_start(out=dbg_d, in_=oT[:]), dsm, reads=[b for row in oB for b in row]))
    P.wait_events("sync", evs)
    P.emit()
    return nc


def _kp(w):
    n = w.shape[1]
    return np.ascontiguousarray(w.reshape(8, 128, n).transpose(1, 0, 2))


def _colT(v, nch):
    return np.ascontiguousarray(v.reshape(nch, 128).T)


def pack_shared(inp, L):
    PO, NPP = pp_off(L)
    pp = np.zeros((128, NPP), np.float32)

    def put(name, l, arr):
        o, per = PO[name]
        pp[:, o + l * per:o + (l + 1) * per] = arr
    wada = np.zeros((L, 8, 128, 6144), np.float32)
    wA = np.zeros((L, 4, 128, 8 * 448), np.float32)
    wD = np.zeros((L, 8, 128, 8 * 512), np.float32)
    wBA = np.zeros((L, 128, 128), np.float32)
    wM = np.zeros((L, 5, 128, 8192), np.float32)
    wF = np.zeros((L, 8, 128, 8192), np.float32)
    for l in range(L):
        put("b_ada", l, _colT(inp["b_ada"][l], 48))
        for nm in ("ln1_g", "ln1_b", "ln2_g", "ln2_b", "b_ff2"):
            put(nm, l, _colT(inp[nm][l], 8))
        put("b_ff1", l, _colT(inp["b_ff1"][l], 32))
        cw = inp["conv_w"][l]
        put("convw", l, cw.reshape(4, 24, 128).transpose(2, 0, 1).reshape(128, 96))
        put("sinks", l, np.broadcast_to(inp["sinks"][l][None, :], (128, 16)))
        put("a_log", l, np.broadcast_to(inp["a_log"][l][None, :], (128, 8)))
        put("dt_bias", l, np.broadcast_to(inp["dt_bias"][l][None, :], (128, 8)))
        put("normw", l, np.broadcast_to(inp["dn_norm_w"][l][None, :], (128, 128)))
        wa = _kp(inp["w_ada"][l])
        for piece in range(8):
            wada[l, piece] = wa[:, :, piece * 768:(piece + 1) * 768].reshape(128, 6144)
        wi = _kp(inp["w_in"][l])
        for g in range(4):
            q = wi[:, :, g * 256:(g + 1) * 256]
            k = wi[:, :, 1024 + g * 64:1024 + (g + 1) * 64]
            v = wi[:, :, 1280 + g * 64:1280 + (g + 1) * 64]
            wA[l, g] = np.concatenate([q, k, k, v], axis=2).reshape(128, 8 * 448)
        for h in range(8):
            parts = [wi[:, :, 1536 + X * 1024 + h * 128:1536 + X * 1024 + (h + 1) * 128] for X in range(4)]
            wD[l, h] = np.concatenate(parts, axis=2).reshape(128, 8 * 512)
        wBA[l] = wi[:, :, 5632:5648].reshape(128, 128)
        wM[l, 0] = _kp(inp["w_oa"][l]).reshape(128, 8192)
        wM[l, 1] = wi[:, :, 5648:6672].reshape(128, 8192)
        wM[l, 2] = _kp(inp["w_ob"][l]).reshape(128, 8192)
        wM[l, 3] = wi[:, :, 6672:7696].reshape(128, 8192)
        wM[l, 4] = _kp(inp["w_out"][l]).reshape(128, 8192)
        w1 = _kp(inp["w_ff1"][l])
        w2 = inp["w_ff2"][l]
        for F in range(8):
            wF[l, F, :, 0:4096] = w1[:, :, F * 512:(F + 1) * 512].reshape(128, 4096)
            blk = w2[F * 512:(F + 1) * 512, :].reshape(4, 128, 1024).transpose(1, 0, 2)
            wF[l, F, :, 4096:8192] = blk.reshape(128, 4096)
    p = np.arange(128)[:, None]
    f = np.arange(128)[None, :]
    cst = np.concatenate([
        (p == f).astype(np.float32), np.ones((128, 128), np.float32), (p <= f).astype(np.float32),
        np.where(p > f, 0.0, NEG).astype(np.float32), np.where(f >= p, 0.0, NEG).astype(np.float32),
        (p > f).astype(np.float32)], axis=1)
    return dict(pp=pp, cst=cst, wada=wada, wA=wA, wD=wD, wBA=wBA, wM=wM, wF=wF)


def make_in_maps(inp, L, ncores):
    shared = pack_shared(inp, L)
    maps = []
    for b in range(ncores):
        m = dict(shared)
        pp = shared["pp"].copy()
        pp[:, 0:8] = _colT(inp["c"][b], 8)
        m["pp"] = pp
        m["xT"] = np.ascontiguousarray(inp["x"][b].T.reshape(8, 128, SEQ).transpose(1, 0, 2)).reshape(128, 8 * SEQ)
        maps.append(m)
    return maps


def unpack_out(yT):
    return np.ascontiguousarray(yT.reshape(128, 8, SEQ).transpose(2, 1, 0).reshape(SEQ, D))


_NC_CACHE = {}


def kernel(**inputs):
    inp = {k: np.asarray(v, dtype=np.float32) for k, v in inputs.items()}
    ncores = 8
    if "nc" not in _NC_CACHE:
        _NC_CACHE["nc"] = build(DEPTH)
    nc = _NC_CACHE["nc"]
    maps = make_in_maps(inp, DEPTH, ncores)
    res = run_bass_kernel_spmd(nc, maps, core_ids=list(range(ncores)))
    out = np.stack([unpack_out(res.results[b]["yT"]) for b in range(ncores)], axis=0)
    return out.astype(np.float32)
```

```python
import contextlib
import os
import numpy as np
CUT = int(os.environ.get('K_CUT', '99'))
import concourse.bass as bass
import concourse.mybir as mybir
from concourse.bass_utils import run_bass_kernel_spmd

F32 = mybir.dt.float32
BF16 = mybir.dt.bfloat16
ALU = mybir.AluOpType
AF = mybir.ActivationFunctionType

ALL_ENG = ("sync", "tensor", "vector", "scalar", "gpsimd")
D = 1024
SEQ = 2048
DEPTH = 4
ALPHA = (2 * DEPTH) ** 0.25
LN_EPS = 1e-5
RMS_EPS = 1e-6
NEG = -30000.0


class Buf:
    __slots__ = ("name", "w", "r", "excl")

    def __init__(self, name="", excl=False):
        self.name = name
        self.w = None
        self.r = []
        self.excl = excl


class Prog:
    def __init__(self, nc, same_engine_sync=True):
        self.nc = nc
        self.ops = {e: [] for e in ALL_ENG}
        self.dsem_count = {}
        self.same = same_engine_sync
        self.es = contextlib.ExitStack()
        self.dsem_names = []

    def sb(self, name, shape, dt):
        return self.es.enter_context(self.nc.sbuf_tensor("sb_" + name, list(shape), dt))

    def ps(self, name, shape, dt):
        return self.es.enter_context(self.nc.psum_tensor("ps_" + name, list(shape), dt))

    def dsem(self, name):
        self.dsem_count[name] = 0
        self.dsem_names.append(name)
        return name

    def _deps(self, eng, reads, writes):
        deps = []
        for b in reads:
            if b.w is not None:
                deps.append(b.w)
        for b in writes:
            if b.w is not None:
                deps.append(b.w)
            deps.extend(b.r)
        out = []
        for d in deps:
            if d[0] == "e" and d[1] == eng and (eng in ("tensor", "sync") or not self.same):
                continue
            if d not in out:
                out.append(d)
        return out

    def op(self, eng, fn, reads=(), writes=()):
        writes = list(writes) + [b for b in reads if b.excl]
        reads = [b for b in reads if not b.excl]
        deps = self._deps(eng, reads, writes)
        idx = len(self.ops[eng])
        self.ops[eng].append(dict(fn=fn, deps=deps, sig=False, dma=None))
        ev = ("e", eng, idx)
        for b in reads:
            b.r.append(ev)
        for b in writes:
            b.w = ev
            b.r = []
        return ev

    def dma(self, eng, fn, dsem, reads=(), writes=()):
        deps = self._deps(eng, reads, writes)
        self.dsem_count[dsem] += 16
        ev = ("d", dsem, self.dsem_count[dsem])
        self.ops[eng].append(dict(fn=fn, deps=deps, sig=False, dma=dsem))
        for b in reads:
            b.r.append(ev)
        for b in writes:
            b.w = ev
            b.r = []
        return ev

    def wait_events(self, eng, evs):
        self.ops[eng].append(dict(fn=None, deps=list(evs), sig=False, dma=None))

    def barrier(self, bufs=()):
        evs = []
        for eng in ("tensor", "vector", "scalar", "gpsimd"):
            n = len(self.ops[eng])
            i = n - 1
            while i >= 0 and (self.ops[eng][i]["fn"] is None or self.ops[eng][i]["dma"] is not None):
                i -= 1
            if i >= 0:
                evs.append(("e", eng, i))
        for eng in ("tensor", "vector", "scalar", "gpsimd"):
            self.wait_events(eng, [e for e in evs if e[1] != eng])
        for b in bufs:
            b.r = list(b.r) + evs
        return evs

    def emit(self):
        nc = self.nc
        for eng in ALL_ENG:
            for rec in self.ops[eng]:
                for d in rec["deps"]:
                    if d[0] == "e":
                        self.ops[d[1]][d[2]]["sig"] = True
        cnt = {}
        for eng in ALL_ENG:
            c = 0
            for i, rec in enumerate(self.ops[eng]):
                if rec["sig"]:
                    c += 1
                    cnt[(eng, i)] = c
        sems = {}
        for eng in ALL_ENG:
            sems[eng] = self.es.enter_context(nc.semaphore("s_" + eng))
        dsems = {}
        for n in self.dsem_names:
            dsems[n] = self.es.enter_context(nc.semaphore("d_" + n))
        self.nwaits = 0

        def run(eng, e):
            known = {}
            for rec in self.ops[eng]:
                for d in rec["deps"]:
                    if d[0] == "e":
                        key = ("e", d[1])
                        val = cnt[(d[1], d[2])]
                        sem = sems[d[1]]
                    else:
                        key = ("d", d[1])
                        val = d[2]
                        sem = dsems[d[1]]
                    if known.get(key, 0) >= val:
                        continue
                    known[key] = val
                    e.wait_ge(sem, val)
                    self.nwaits += 1
                if rec["fn"] is None:
                    continue
                ins = rec["fn"](e)
                if rec["dma"] is not None:
                    ins.then_inc(dsems[rec["dma"]], 16)
                elif rec["sig"]:
                    ins.then_inc(sems[eng], 1)

        with nc.Block() as block:
            @block.sync
            def _(e):
                run("sync", e)

            @block.tensor
            def _(e):
                run("tensor", e)

            @block.vector
            def _(e):
                run("vector", e)

            @block.scalar
            def _(e):
                run("scalar", e)

            @block.gpsimd
            def _(e):
                run("gpsimd", e)
        self.es.close()


class Carver:
    def __init__(self, regions):
        self.regions = regions
        self.off = [0] * len(regions)

    def take(self, nelem, dt):
        nb = nelem * (4 if dt == F32 else 2)
        nb = (nb + 63) // 64 * 64
        for i, r in enumerate(self.regions):
            cap = r.shape[1] * 2
            if self.off[i] + nb <= cap:
                o = self.off[i]
                self.off[i] += nb
                ap = r[:, o // 2:(o + nb) // 2]
                if dt == F32:
                    ap = ap.bitcast(F32)
                return ap[:, 0:nelem]
        raise MemoryError(f"carver out of space for {nelem} {dt}")


NPP_L = 48 + 8 * 4 + 32 + 8 + 96 + 16 + 8 + 8 + 128


def pp_off(L):
    o = {}
    p = 8
    for name, n in (("b_ada", 48), ("ln1_g", 8), ("ln1_b", 8), ("ln2_g", 8), ("ln2_b", 8), ("b_ff1", 32),
                    ("b_ff2", 8), ("convw", 96), ("sinks", 16), ("a_log", 8), ("dt_bias", 8), ("normw", 128)):
        o[name] = (p, n)
        p += n * L
    return o, p


def build(L=DEPTH, stop=None, dbg=False):
    nc = bass.Bass("TRN2", target_bir_lowering=False)
    P = Prog(nc, same_engine_sync=not os.environ.get("K_NOSAME"))
    PO, NPP = pp_off(L)

    xT_d = nc.dram_tensor("xT", [128, 8 * SEQ], F32, kind="ExternalInput").ap()
    pp_d = nc.dram_tensor("pp", [128, NPP], F32, kind="ExternalInput").ap()
    cst_d = nc.dram_tensor("cst", [128, 6 * 128], F32, kind="ExternalInput").ap()
    wada_d = nc.dram_tensor("wada", [L, 8, 128, 6144], F32, kind="ExternalInput").ap()
    wA_d = nc.dram_tensor("wA", [L, 4, 128, 8 * 448], F32, kind="ExternalInput").ap()
    wD_d = nc.dram_tensor("wD", [L, 8, 128, 8 * 512], F32, kind="ExternalInput").ap()
    wBA_d = nc.dram_tensor("wBA", [L, 128, 128], F32, kind="ExternalInput").ap()
    wM_d = nc.dram_tensor("wM", [L, 5, 128, 8192], F32, kind="ExternalInput").ap()
    wF_d = nc.dram_tensor("wF", [L, 8, 128, 8192], F32, kind="ExternalInput").ap()
    yT_d = nc.dram_tensor("yT", [128, 8 * SEQ], F32, kind="ExternalOutput").ap()
    if dbg:
        dbg_d = nc.dram_tensor("dbg", [128, 8192], BF16, kind="ExternalOutput").ap()

    xT = P.sb("xTs", [128, 8 * SEQ], F32)
    xB = [[Buf(f"x{c}_{t}") for t in range(4)] for c in range(8)]
    uT = P.sb("uTs", [128, 8192], BF16)
    uB = [[Buf(f"u{c}_{t}") for t in range(2)] for c in range(8)]
    oT = P.sb("oTs", [128, 8192], BF16)
    oB = [[Buf(f"o{c}_{t}") for t in range(2)] for c in range(8)]
    wB = [P.sb(f"wB{i}", [128, 8192], BF16) for i in range(3)]
    wBB = [Buf(f"wB{i}") for i in range(3)]
    wBsem = [P.dsem(f"wB{i}") for i in range(3)]
    wAs = P.sb("wAs", [128, 8192], BF16)
    wAB = [Buf("wA0"), Buf("wA1")]
    wAsem = [P.dsem("wA0"), P.dsem("wA1")]
    wba = P.sb("wba", [128, 128], BF16)
    wbaB = Buf("wba")
    wbasem = P.dsem("wba")
    ppt = P.sb("ppt", [128, NPP], F32)
    ppB = Buf("pp")
    cst = P.sb("cst", [128, 768], F32)
    cstB = Buf("cst")
    cbf = P.sb("cbf", [128, 4 * 128], BF16)
    cbfB = Buf("cbf")
    negones = P.sb("negones", [128, 128], F32)
    mod = P.sb("mod", [128, L * 48], F32)
    sb2 = P.sb("sb2", [128, L * 8], F32)
    fus = P.sb("fus", [128, L * 24], F32)
    epsT = P.sb("epsT", [128, 4], F32)
    cact = P.sb("cact", [128, 8], F32)
    cactb = P.sb("cactb", [128, 8], BF16)
    cactB = Buf("cact")
    kcar = P.sb("kcar", [128, 4 * 128], BF16)
    vcar = P.sb("vcar", [128, 4 * 128], BF16)
    carB = [Buf(f"car{g}") for g in range(4)]
    ccar = P.sb("ccar", [128, 8 * 9], F32)
    ccarB = [Buf(f"cc{h}") for h in range(8)]
    S_f = P.sb("S_f", [128, 8 * 128], F32)
    S_fB = [Buf(f"S{h}") for h in range(8)]
    bet = P.sb("bet", [128, 64], F32)
    nbet = P.sb("nbet", [128, 64], F32)
    gtk = P.sb("gtk", [128, 64], F32)
    gtmp = P.sb("gtmp", [128, 64], F32)
    nega = P.sb("nega", [128, 8], F32)
    sexp = P.sb("sexp", [128, 16], F32)
    bgB = Buf("bg")
    lyrB = Buf("lyr")
    free_ar = P.sb("free_ar", [128, 14336], BF16)

    ident_f = cst[:, 0:128]
    ones_f = cst[:, 128:256]
    triinc = cst[:, 256:384]
    mls = cst[:, 384:512]
    mupi = cst[:, 512:640]
    ident_b = cbf[:, 0:128]
    ones_b = cbf[:, 128:256]
    mcur_b = cbf[:, 256:384]
    mprev_b = cbf[:, 384:512]

    pbank = [P.ps(f"pb{i}", [128, 512], F32) for i in range(8)]
    pbankB = [Buf(f"pb{i}", excl=True) for i in range(8)]
    st = dict(i=0)

    def big():
        lo, hi = st.get("rng", (0, 8))
        key = ("i", lo, hi)
        i = st.get(key, lo)
        st[key] = lo + (i + 1 - lo) % (hi - lo)
        return pbank[i], pbankB[i]

    def quarter():
        t, b = big()
        return t[:, 0:128], b

    def tslot():
        t, b = big()
        return t[:, :].bitcast(BF16)[:, 0:128], b

    def tslot4():
        t, b = big()
        return t[:, :].bitcast(BF16)[:, 0:512], [b]

    def T(fn, r, w):
        return P.op("tensor", fn, r, w)

    def V(fn, r, w):
        return P.op("vector", fn, r, w)

    def A(fn, r, w):
        return P.op("scalar", fn, r, w)

    def G(fn, r, w):
        return P.op("gpsimd", fn, r, w)

    def mm(out, lhsT, rhs, start, stop, r, w):
        return T(lambda e: e.matmul(out, lhsT, rhs, start=start, stop=stop), r, w)

    def act(out, in_, func, r, w, **kw):
        return A(lambda e: e.activation(out=out, in_=in_, func=func, **kw), r, w)

    def u_(k, a, b):
        return uT[:, k * 1024 + a:k * 1024 + b]

    def x_(c, a, b):
        return xT[:, c * SEQ + a:c * SEQ + b]

    def ppc(name, l, j=0, n=1):
        o, per = PO[name]
        return ppt[:, o + l * per + j:o + l * per + j + n]

    def modc(l, w, c=0, n=8):
        o = l * 48 + w * 8 + c
        return mod[:, o:o + n]

    P.dma("sync", lambda e: e.dma_start(out=ppt[:], in_=pp_d), P.dsem("pp"), writes=[ppB])
    P.dma("sync", lambda e: e.dma_start(out=cst[:], in_=cst_d), P.dsem("cst"), writes=[cstB])
    xsem = P.dsem("x")
    for c in range(8):
        P.dma("sync", lambda e, c=c: e.dma_start(out=xT[:, c * SEQ:(c + 1) * SEQ], in_=xT_d[:, c * SEQ:(c + 1) * SEQ]),
              xsem, writes=xB[c])
    for c in range(8):
        for t_ in range(4):
            xB[c][t_].w = ("d", xsem, 128)
    V(lambda e: e.tensor_copy(out=cbf[:, 0:384], in_=cst[:, 0:384]), [cstB], [cbfB])
    V(lambda e: e.tensor_copy(out=cbf[:, 384:512], in_=cst[:, 640:768]), [cstB], [cbfB])
    V(lambda e: e.memset(negones[:], -1.0), [], [cbfB])
    V(lambda e: e.memset(epsT[:, 0:1], LN_EPS / (ALPHA * ALPHA)), [], [cbfB])
    V(lambda e: e.memset(epsT[:, 1:2], RMS_EPS), [], [cbfB])
    V(lambda e: e.memset(epsT[:, 2:3], 1.0), [], [cbfB])
    V(lambda e: e.memset(epsT[:, 3:4], 0.0), [], [cbfB])
    eps_ln = epsT[:, 0:1]
    eps_rms = epsT[:, 1:2]
    one_c = epsT[:, 2:3]
    act(cact[:], ppt[:, 0:8], AF.Silu, [ppB], [cactB])
    V(lambda e: e.tensor_copy(out=cactb[:], in_=cact[:]), [cactB], [cactB])
    modBs = [Buf(f"mod{l}") for l in range(L)]
    modst = dict(cnt=0)
    o_b, _ = PO["b_ada"]
    o_f2, _ = PO["b_ff2"]

    def mod_dma(l, piece):
        s_ = (l * 8 + piece) % 3
        P.dma("gpsimd", lambda e: e.dma_start(out=wB[s_][:, 0:6144], in_=wada_d[l, piece]), wBsem[s_], writes=[wBB[s_]])

    def mod_piece(l, piece):
        s_ = (l * 8 + piece) % 3
        psm, psmB = big()
        for m in range(6):
            for k in range(8):
                mm(psm[:, m:m + 1], wB[s_][:, k * 768 + m * 128:k * 768 + (m + 1) * 128], cactb[:, k:k + 1],
                   k == 0, k == 7, [wBB[s_], cactB], [psmB])
        c0_ = l * 48 + piece * 6
        V(lambda e: e.tensor_tensor(out=mod[:, c0_:c0_ + 6], in0=psm[:, 0:6], in1=ppt[:, o_b + c0_:o_b + c0_ + 6], op=ALU.add),
          [psmB, ppB], [modBs[l]])

    def mod_final(l):
        for w_ in (1, 4):
            V(lambda e, w_=w_: e.tensor_scalar_add(out=modc(l, w_), in0=modc(l, w_), scalar1=1.0), [modBs[l]], [modBs[l]])
        for w_ in (2, 5):
            V(lambda e, w_=w_: e.tensor_scalar(out=modc(l, w_), in0=modc(l, w_), scalar1=1.0, scalar2=1.0 / ALPHA,
                                               op0=ALU.add, op1=ALU.mult), [modBs[l]], [modBs[l]])
        V(lambda e: e.tensor_tensor(out=sb2[:, l * 8:(l + 1) * 8], in0=modc(l, 5), in1=ppt[:, o_f2 + l * 8:o_f2 + (l + 1) * 8], op=ALU.mult),
          [modBs[l], ppB], [modBs[l]])
        f0 = l * 24
        V(lambda e: e.tensor_tensor(out=fus[:, f0:f0 + 8], in0=ppc("ln1_g", l, 0, 8), in1=modc(l, 4), op=ALU.mult), [modBs[l], ppB], [modBs[l]])
        V(lambda e: e.tensor_tensor(out=fus[:, f0 + 8:f0 + 16], in0=ppc("ln1_b", l, 0, 8), in1=modc(l, 4), op=ALU.mult), [modBs[l], ppB], [modBs[l]])
        V(lambda e: e.tensor_tensor(out=fus[:, f0 + 8:f0 + 16], in0=fus[:, f0 + 8:f0 + 16], in1=modc(l, 3), op=ALU.add), [modBs[l]], [modBs[l]])
        V(lambda e: e.tensor_tensor(out=fus[:, f0 + 16:f0 + 24], in0=ppc("ln1_b", l, 0, 8), in1=sb2[:, l * 8:(l + 1) * 8], op=ALU.add),
          [modBs[l], ppB], [modBs[l]])

    for piece in range(8):
        mod_dma(0, piece)
        mod_piece(0, piece)
    mod_final(0)
    allW = wBB + wAB
    P.barrier(allW)

    def phase_u(l, hf, which):
        scw, shw = (1, 0) if which == 0 else (4, 3)
        i = 0
        for c in range(8):
            for t in range(2):
                T0 = hf * 1024 + t * 512
                if i % 2 == 0:
                    act(u_(c, t * 512, (t + 1) * 512), x_(c, T0, T0 + 512), AF.Identity, [xB[c][hf * 2 + t], modBs[l]],
                        [uB[c][t]], scale=modc(l, scw, c, 1), bias=modc(l, shw, c, 1))
                else:
                    V(lambda e, c=c, t=t, T0=T0: e.tensor_scalar(out=u_(c, t * 512, (t + 1) * 512), in0=x_(c, T0, T0 + 512),
                                                                 scalar1=modc(l, scw, c, 1), scalar2=modc(l, shw, c, 1),
                                                                 op0=ALU.mult, op1=ALU.add),
                      [xB[c][hf * 2 + t], modBs[l]], [uB[c][t]])
                i += 1

    def load_w(eng, dst_ap, src_ap, sem, buf):
        P.dma(eng, lambda e: e.dma_start(out=dst_ap, in_=src_ap), sem, writes=[buf])

    def prefetch_merge(l, X):
        load_w("gpsimd", wB[0][:], wM_d[l, 2 * X], wBsem[0], wBB[0])
        load_w("gpsimd", wB[1][:], wM_d[l, 2 * X + 1], wBsem[1], wBB[1])
        load_w("gpsimd", wB[2][:], wM_d[l, 4], wBsem[2], wBB[2])

    def phase_attn(l, hf):
        cv = Carver([free_ar[:]])
        do_mod = (hf == 0 and l + 1 < L)
        qT = [cv.take(1024, BF16) for _ in range(4)]
        kT2 = cv.take(1152, BF16)
        vaug = cv.take(9 * 128, BF16)
        Et = [[cv.take(512, BF16) for _ in range(2)] for _ in range(2)]
        rec = cv.take(512, F32)
        qB = [Buf(), Buf(), Buf(), Buf()]
        kB = Buf()
        vB = Buf()
        EB = [[Buf(), Buf()], [Buf(), Buf()]]
        recB = Buf()
        act(sexp[:], ppc("sinks", l, 0, 16), AF.Exp, [ppB], [lyrB])
        V(lambda e: e.memset(vaug[:].rearrange("p (b c) -> p b c", c=128)[:, :, 64:128], 1.0), [], [vB])
        for h in range(4):
            V(lambda e, h=h: e.memset(qT[h][:, :], 0.0), [], [qB[h]])
        load_w("gpsimd", wAs[:, 0:3584], wA_d[l, 0], wAsem[0], wAB[0])
        for g in range(4):
            s = g % 2
            if g + 1 < 4:
                load_w("gpsimd", wAs[:, (1 - s) * 4096:(1 - s) * 4096 + 3584], wA_d[l, g + 1], wAsem[1 - s], wAB[1 - s])
            if do_mod and g == 0:
                for pc in range(3):
                    mod_dma(l + 1, pc)
            Wb = s * 4096

            def W(k, a, b):
                return wAs[:, Wb + k * 448 + a:Wb + k * 448 + b]
            for jp in range(2):
                for t in range(2):
                    ps, pb = big()
                    for k in range(8):
                        mm(ps[:, :], W(k, jp * 128, (jp + 1) * 128), u_(k, t * 512, (t + 1) * 512), k == 0, k == 7,
                           [wAB[s], uB[k][t]], [pb])
                    act(qT[2 * jp][0:64, t * 512:(t + 1) * 512], ps[0:64, :], AF.Copy, [pb], [qB[2 * jp]], scale=0.125)
                    act(qT[2 * jp + 1][64:128, t * 512:(t + 1) * 512], ps[64:128, :], AF.Copy, [pb], [qB[2 * jp + 1]], scale=0.125)
            for t in range(2):
                ps, pb = big()
                for k in range(8):
                    mm(ps[:, :], W(k, 256, 384), u_(k, t * 512, (t + 1) * 512), k == 0, k == 7, [wAB[s], uB[k][t]], [pb])
                V(lambda e, ps=ps, t=t: e.tensor_copy(out=kT2[:, 128 + t * 512:128 + (t + 1) * 512], in_=ps[:, :]), [pb], [kB])
            ps, pb = big()
            for blk in range(8):
                for k in range(8):
                    mm(ps[:, blk * 64:(blk + 1) * 64], u_(k, blk * 128, (blk + 1) * 128), W(k, 384, 448), k == 0, k == 7,
                       [wAB[s], uB[k][blk // 4]], [pb])
            V(lambda e, ps=ps: e.tensor_copy(out=vaug[:].rearrange("p (b c) -> p b c", c=128)[:, 1:9, 0:64],
                                             in_=ps[:, :].rearrange("p (b c) -> p b c", c=64)), [pb], [vB])
            if CUT <= 1:
                return
            if hf == 1:
                V(lambda e, g=g: e.tensor_copy(out=kT2[:, 0:128], in_=kcar[:, g * 128:(g + 1) * 128]), [carB[g]], [kB])
                V(lambda e, g=g: e.tensor_copy(out=vaug[:, 0:64], in_=vcar[:, g * 128:g * 128 + 64]), [carB[g]], [vB])
            def stage1(n, g=g):
                N = hf * 8 + n
                js = [0, 1] if N > 0 else [1]
                par = n % 2
                for j in js:
                    ps, pb = big()
                    for h in range(4):
                        mm(ps[:, h * 128:(h + 1) * 128], kT2[:, (n + j) * 128:(n + j + 1) * 128],
                           qT[h][:, n * 128:(n + 1) * 128], True, True, [kB, qB[h]], [pb])
                    E = Et[j][par]
                    act(E[:, :], ps[:, :], AF.Exp, [pb], [EB[j][par]])
                    mk = mcur_b if j == 1 else mprev_b
                    V(lambda e, E=E, mk=mk: e.tensor_tensor(out=E.rearrange("p (h q) -> p h q", h=4),
                                                            in0=E.rearrange("p (h q) -> p h q", h=4),
                                                            in1=mk.unsqueeze(1).to_broadcast([128, 4, 128]), op=ALU.mult),
                      [EB[j][par], cbfB], [EB[j][par]])
                return js

            def stage2(n, js, g=g):
                par = n % 2
                ps, pb = big()
                for idx, j in enumerate(js):
                    mm(ps[:, :], vaug[:, (n + j) * 128:(n + j + 1) * 128], Et[j][par][:, :], idx == 0, idx == len(js) - 1,
                       [vB, EB[j][par]], [pb])
                V(lambda e, ps=ps: e.tensor_tensor(out=rec[0:64, :].rearrange("p (h q) -> p h q", h=4),
                                                   in0=ps[64:128, :].rearrange("p (h q) -> p h q", h=4),
                                                   in1=sexp[64:128, g * 4:(g + 1) * 4].unsqueeze(2).to_broadcast([64, 4, 128]),
                                                   op=ALU.add), [pb, lyrB], [recB])
                act(rec[0:64, :], rec[0:64, :], AF.Ln, [recB], [recB])
                act(rec[0:64, :], rec[0:64, :], AF.Exp, [recB], [recB], scale=-1.0)
                psv = ps[0:64, :].rearrange("p (a b q) -> p a b q", a=2, b=2)
                rcv = rec[0:64, :].rearrange("p (a b q) -> p a b q", a=2, b=2)
                for odd in range(2):
                    c0 = 2 * g
                    dst = oT[odd * 64:odd * 64 + 64, :].rearrange("p (c t) -> p c t", c=8)[:, c0:c0 + 2, n * 128:(n + 1) * 128]
                    V(lambda e, dst=dst, odd=odd, psv=psv, rcv=rcv: e.tensor_tensor(out=dst, in0=psv[:, :, odd, :], in1=rcv[:, :, odd, :],
                                                                                    op=ALU.mult),
                      [pb, recB], [oB[c0][n // 4], oB[c0 + 1][n // 4]])
            js_cur = stage1(0)
            for n in range(8):
                js_nxt = stage1(n + 1) if n + 1 < 8 else None
                stage2(n, js_cur)
                js_cur = js_nxt
            if hf == 0:
                V(lambda e, g=g: e.tensor_copy(out=kcar[:, g * 128:(g + 1) * 128], in_=kT2[:, 1024:1152]), [kB], [carB[g]])
                V(lambda e, g=g: e.tensor_copy(out=vcar[:, g * 128:g * 128 + 64], in_=vaug[:, 8 * 128:8 * 128 + 64]), [vB], [carB[g]])
            if do_mod:
                for pc in (2 * g, 2 * g + 1):
                    mod_piece(l + 1, pc)
                    if pc + 3 < 8:
                        mod_dma(l + 1, pc + 3)
        if do_mod:
            mod_final(l + 1)
        prefetch_merge(l, 0)

    def ln_alloc(cv):
        return dict(sq=cv.take(512, F32), sacc=cv.take(512, F32), qacc=cv.take(512, F32), mu=cv.take(512, F32),
                    rstd=cv.take(512, F32), tmp=[cv.take(512, F32) for _ in range(2)],
                    B=[Buf() for _ in range(5)], tB=[Buf(), Buf()])

    def layer_norm(l, Tg_, gname, bname, tm, fuse=False):
        sq, sacc, qacc, mu, rstd, tmp = tm["sq"], tm["sacc"], tm["qacc"], tm["mu"], tm["rstd"], tm["tmp"]
        sqB, saB, qaB, muB, rsB = tm["B"]
        tB = tm["tB"]
        a0 = Tg_ * 512
        for c in range(8):
            xs = x_(c, a0, a0 + 512)
            if c == 0:
                V(lambda e, xs=xs: e.tensor_copy(out=sacc, in_=xs), [xB[c][Tg_]], [saB])
                act(qacc, xs, AF.Square, [xB[c][Tg_]], [qaB])
            else:
                V(lambda e, xs=xs: e.tensor_tensor(out=sacc, in0=sacc, in1=xs, op=ALU.add), [xB[c][Tg_], saB], [saB])
                act(sq, xs, AF.Square, [xB[c][Tg_]], [sqB])
                V(lambda e: e.tensor_tensor(out=qacc, in0=qacc, in1=sq, op=ALU.add), [sqB, qaB], [qaB])
        p1, p1B = big()
        mm(p1[:, :], ones_f, sacc, True, True, [saB, cstB], [p1B])
        p2, p2B = big()
        mm(p2[:, :], ones_f, qacc, True, True, [qaB, cstB], [p2B])
        act(mu, p1[:, :], AF.Copy, [p1B], [muB], scale=1.0 / D)
        V(lambda e: e.tensor_tensor(out=sq, in0=mu, in1=mu, op=ALU.mult), [muB, sqB], [sqB])
        V(lambda e: e.scalar_tensor_tensor(out=rstd, in0=p2[:, :], scalar=1.0 / D, in1=sq, op0=ALU.mult, op1=ALU.subtract),
          [p2B, sqB], [rsB])
        act(rstd, rstd, AF.Ln, [rsB], [rsB], bias=eps_ln, scale=1.0)
        act(rstd, rstd, AF.Exp, [rsB], [rsB], scale=-0.5)
        for c in range(8):
            xs = x_(c, a0, a0 + 512)
            tt = tmp[c % 2]
            V(lambda e, xs=xs, tt=tt: e.tensor_tensor(out=tt, in0=xs, in1=mu, op=ALU.subtract), [xB[c][Tg_], muB], [tB[c % 2]])
            V(lambda e, tt=tt: e.tensor_tensor(out=tt, in0=tt, in1=rstd, op=ALU.mult), [tB[c % 2], rsB], [tB[c % 2]])
            if fuse:
                f0 = l * 24
                tl = Tg_ % 2
                act(xs, tt, AF.Identity, [tB[c % 2], ppB, modBs[l]], [xB[c][Tg_]], scale=ppc(gname, l, c, 1), bias=fus[:, f0 + 16 + c:f0 + 17 + c])
                V(lambda e, tt=tt, c=c, tl=tl, f0=f0: e.tensor_scalar(out=u_(c, tl * 512, (tl + 1) * 512), in0=tt, scalar1=fus[:, f0 + c:f0 + c + 1],
                                                                      scalar2=fus[:, f0 + 8 + c:f0 + 9 + c], op0=ALU.mult, op1=ALU.add),
                  [tB[c % 2], modBs[l]], [uB[c][tl]])
            else:
                act(xs, tt, AF.Identity, [tB[c % 2], ppB], [xB[c][Tg_]], scale=ppc(gname, l, c, 1), bias=ppc(bname, l, c, 1))

    def phase_merge(l, hf, X, do_ln):
        if X == 1:
            prefetch_merge(l, X)
        cv = Carver([free_ar[:], wAs[:]])
        sg = [cv.take(512, F32) for _ in range(2)]
        mg = cv.take(8 * 512, BF16)
        sgB = [Buf(), Buf()]
        mgB = [Buf() for _ in range(8)]
        lnt = ln_alloc(cv) if do_ln else None
        for t in range(2):
            Tg_ = hf * 2 + t
            for c in range(8):
                py, pyB = big()
                for k in range(8):
                    mm(py[:, :], wB[0][:, k * 1024 + c * 128:k * 1024 + (c + 1) * 128], oT[:, k * 1024 + t * 512:k * 1024 + (t + 1) * 512],
                       k == 0, k == 7, [wBB[0], oB[k][t]], [pyB])
                pg, pgB = big()
                for k in range(8):
                    mm(pg[:, :], wB[1][:, k * 1024 + c * 128:k * 1024 + (c + 1) * 128], u_(k, t * 512, (t + 1) * 512),
                       k == 0, k == 7, [wBB[1], uB[k][t]], [pgB])
                act(sg[c % 2], pg[:, :], AF.Sigmoid, [pgB], [sgB[c % 2]])
                V(lambda e, c=c, py=py: e.tensor_tensor(out=mg[:, c * 512:(c + 1) * 512], in0=py[:, :], in1=sg[c % 2], op=ALU.mult),
                  [pyB, sgB[c % 2]], [mgB[c]])
            for c2 in range(8):
                pm, pmB = big()
                for c in range(8):
                    mm(pm[:, :], wB[2][:, c * 1024 + c2 * 128:c * 1024 + (c2 + 1) * 128], mg[:, c * 512:(c + 1) * 512],
                       c == 0, c == 7, [wBB[2], mgB[c]], [pmB])
                xs = x_(c2, Tg_ * 512, Tg_ * 512 + 512)
                V(lambda e, pm=pm, xs=xs, c2=c2: e.scalar_tensor_tensor(out=xs, in0=pm[:, :], scalar=modc(l, 2, c2, 1), in1=xs,
                                                                         op0=ALU.mult, op1=ALU.add),
                  [pmB, modBs[l], xB[c2][Tg_]], [xB[c2][Tg_]])
            if do_ln and t == 1:
                load_w("gpsimd", wB[0][:], wF_d[l, 0], wBsem[0], wBB[0])
                load_w("gpsimd", wB[1][:], wF_d[l, 1], wBsem[1], wBB[1])
            if do_ln:
                layer_norm(l, Tg_, "ln1_g", "ln1_b", lnt, fuse=True)

    def phase_ffn(l, hf):
        cv = Carver([free_ar[:], wAs[:]])
        rl = [cv.take(512, F32) for _ in range(2)]
        hT = [cv.take(4 * 512, BF16) for _ in range(2)]
        rlB = [Buf(), Buf()]
        hB = [[Buf() for _ in range(4)] for _ in range(2)]
        lnt = ln_alloc(cv)
        it = 0
        for F in range(8):
            s = F % 3
            if F + 2 < 8:
                s2 = (F + 2) % 3
                load_w("gpsimd", wB[s2][:], wF_d[l, F + 2], wBsem[s2], wBB[s2])
            for t in range(2):
                Tg_ = hf * 2 + t
                hp = it % 2
                it += 1
                for f in range(4):
                    ph, phB = big()
                    for k in range(8):
                        mm(ph[:, :], wB[s][:, k * 512 + f * 128:k * 512 + (f + 1) * 128], u_(k, t * 512, (t + 1) * 512),
                           k == 0, k == 7, [wBB[s], uB[k][t]], [phB])
                    r_ = rl[f % 2]
                    act(r_, ph[:, :], AF.Relu, [phB, ppB], [rlB[f % 2]], bias=ppc("b_ff1", l, F * 4 + f, 1), scale=1.0)
                    V(lambda e, r_=r_, hp=hp, f=f: e.tensor_tensor(out=hT[hp][:, f * 512:(f + 1) * 512], in0=r_, in1=r_, op=ALU.mult),
                      [rlB[f % 2]], [hB[hp][f]])
                for c2 in range(8):
                    po_, poB = big()
                    for f in range(4):
                        mm(po_[:, :], wB[s][:, 4096 + f * 1024 + c2 * 128:4096 + f * 1024 + (c2 + 1) * 128], hT[hp][:, f * 512:(f + 1) * 512],
                           f == 0, f == 3, [wBB[s], hB[hp][f]], [poB])
                    xs = x_(c2, Tg_ * 512, Tg_ * 512 + 512)
                    V(lambda e, po_=po_, xs=xs, c2=c2: e.scalar_tensor_tensor(out=xs, in0=po_[:, :], scalar=modc(l, 5, c2, 1), in1=xs,
                                                                               op0=ALU.mult, op1=ALU.add),
                      [poB, modBs[l], xB[c2][Tg_]], [xB[c2][Tg_]])
        for t in range(2):
            layer_norm(l, hf * 2 + t, "ln2_g", "ln2_b", lnt)

    def interleave(*gens, rngs=None, reps=None):
        gens = [(g, (rngs[i] if rngs else (0, 8)), (reps[i] if reps else 1)) for i, g in enumerate(gens) if g is not None]
        while gens:
            for it in list(gens):
                st["rng"] = it[1]
                try:
                    for _ in range(it[2]):
                        next(it[0])
                except StopIteration:
                    gens.remove(it)
        st["rng"] = (0, 8)

    def chain(*gens):
        for g in gens:
            if g is not None:
                yield from g

    def phase_dn(l, hf):
        cv = Carver([free_ar[:], wB[0][:], wB[1][:], wB[2][:]])
        W5 = 512
        pre = cv.take(1028, BF16)
        dg = cv.take(512, BF16)
        dgB = Buf()
        acc = cv.take(1024, F32)
        sqb = cv.take(512, BF16)
        rn = cv.take(512, F32)
        qT = cv.take(1024, BF16)
        kT = cv.take(1024, BF16)
        vT = cv.take(1024, BF16)
        kk = cv.take(1024, BF16)
        vv = cv.take(1024, BF16)
        sz = [cv.take(1024, BF16) for _ in range(2)]
        Tg4 = cv.take(W5, F32)
        D4 = cv.take(W5, F32)
        DT4 = cv.take(W5, F32)
        eg4 = cv.take(W5, F32)
        Pab = [cv.take(W5, F32) for _ in range(2)]
        PTab = [cv.take(W5, F32) for _ in range(2)]
        PabB = [Buf(), Buf()]
        PTabB = [Buf(), Buf()]
        Za = cv.take(W5, F32)
        Zb4 = cv.take(W5, BF16)
        sm4 = cv.take(32, F32)
        ab = [dict(P0=cv.take(W5, F32), Rw=cv.take(W5, BF16), vb=cv.take(W5, BF16), B=Buf()) for _ in range(2)]
        bun = [dict(qdT=cv.take(W5, BF16), kdec=cv.take(W5, BF16), intraT=cv.take(W5, BF16), negWT=cv.take(W5, BF16),
                    U=cv.take(W5, F32), last=cv.take(4, F32), B=Buf(), B2=Buf()) for _ in range(3)]
        vnew = cv.take(128, BF16)
        S_b = cv.take(128, BF16)
        tt = cv.take(128, F32)
        junk = cv.take(128, F32)
        obt = cv.take(128, BF16)
        smc = cv.take(8, F32)
        preB, accB, sqbB, rnB, qTB, kTB, vTB, kkB, vvB = (Buf() for _ in range(9))
        szB = [Buf(), Buf()]
        TgB, DB, DTB, egB, PaB, PTaB, ZaB, ZbB, RwB, vbB, smB = (Buf() for _ in range(11))
        vnB, SbB, ttB, jkB, obB, scB = (Buf() for _ in range(6))
        gl4, t4, egp4, ekd4, bege4 = (sm4[:, i * 4:(i + 1) * 4] for i in range(5))
        ssq, rms = smc[:, 0:1], smc[:, 1:2]

        def v4(ap):
            return ap.rearrange("p (i f) -> p i f", i=4)

        def bc1(ap):
            return ap.unsqueeze(1).to_broadcast([128, 4, 128])

        def bc2(ap):
            return ap.unsqueeze(2).to_broadcast([128, 4, 128])

        load_w("gpsimd", wba[:], wBA_d[l], wbasem, wbaB)
        ps, pb = big()
        for blk in range(8):
            for k in range(8):
                mm(ps[:, blk * 16:(blk + 1) * 16], u_(k, blk * 128, (blk + 1) * 128), wba[:, k * 16:(k + 1) * 16], k == 0, k == 7,
                   [wbaB, uB[k][blk // 4]], [pb])
        psv = ps[:, 0:128].rearrange("p (b j) -> p b j", j=16)
        b3 = bet[:].rearrange("p (b j) -> p b j", j=8)
        nb3 = nbet[:].rearrange("p (b j) -> p b j", j=8)
        g3 = gtk[:].rearrange("p (b j) -> p b j", j=8)
        gt3 = gtmp[:].rearrange("p (b j) -> p b j", j=8)
        act(b3, psv[:, :, 0:8], AF.Sigmoid, [pb], [bgB])
        V(lambda e: e.tensor_scalar(out=nbet[:], in0=bet[:], scalar1=-1.0, scalar2=None, op0=ALU.mult), [bgB], [bgB])
        V(lambda e: e.tensor_tensor(out=gt3, in0=psv[:, :, 8:16], in1=ppc("dt_bias", l, 0, 8).unsqueeze(1).to_broadcast([128, 8, 8]),
                                    op=ALU.add), [pb, ppB], [bgB])
        act(gtmp[:], gtmp[:], AF.Exp, [bgB], [bgB])
        act(gtmp[:], gtmp[:], AF.Ln, [bgB], [bgB], bias=one_c, scale=1.0)
        act(nega[:], ppc("a_log", l, 0, 8), AF.Exp, [ppB], [bgB])
        V(lambda e: e.scalar_tensor_tensor(out=g3, in0=gt3, scalar=-1.0, in1=nega[:].unsqueeze(1).to_broadcast([128, 8, 8]),
                                           op0=ALU.mult, op1=ALU.mult), [bgB], [bgB])
        normw = ppc("normw", l, 0, 128)
        cwo, _ = PO["convw"]

        def preamble(h):
            s = h % 2
            if h == 0:
                load_w("gpsimd", wAs[:, 0:4096], wD_d[l, 0], wAsem[0], wAB[0])
            if h + 1 < 8:
                load_w("gpsimd", wAs[:, (1 - s) * 4096:(2 - s) * 4096], wD_d[l, h + 1], wAsem[1 - s], wAB[1 - s])
            Wb = s * 4096

            def W(k, a, b):
                return wAs[:, Wb + k * 512 + a:Wb + k * 512 + b]
            for X in range(3):
                cc = ccar[:, (h * 3 + X) * 3:(h * 3 + X) * 3 + 3]
                if hf == 0:
                    V(lambda e: e.memset(pre[:, 0:3], 0.0), [], [preB])
                else:
                    V(lambda e, cc=cc: e.tensor_copy(out=pre[:, 0:3], in_=cc), [ccarB[h]], [preB])

                def cw(j, X=X, h=h):
                    o = cwo + l * 96 + j * 24 + X * 8 + h
                    return ppt[:, o:o + 1]
                for j in range(4):
                    V(lambda e, j=j, cw=cw: e.tensor_scalar(out=dg[:, j * 128:(j + 1) * 128], in0=ident_b, scalar1=cw(j), scalar2=None, op0=ALU.mult),
                      [cbfB, ppB], [dgB])
                for t in range(2):
                    ps, pb = big()
                    for k in range(8):
                        mm(ps[:, :], W(k, X * 128, (X + 1) * 128), u_(k, t * 512, (t + 1) * 512), k == 0, k == 7, [wAB[s], uB[k][t]], [pb])
                    act(pre[:, 3 + t * 512:3 + (t + 1) * 512], ps[:, :], AF.Copy, [pb], [preB])
                    yield
                if hf == 0:
                    V(lambda e, cc=cc: e.tensor_copy(out=cc, in_=pre[:, 1024:1027]), [preB], [ccarB[h]])
                for t in range(2):
                    ps, pb = big()
                    for j in range(4):
                        mm(ps[:, :], dg[:, j * 128:(j + 1) * 128], pre[:, t * 512 + j:t * 512 + j + 512], j == 0, j == 3, [dgB, preB], [pb])
                    act(acc[:, t * 512:(t + 1) * 512], ps[:, :], AF.Silu, [pb], [accB])
                    yield
                if X < 2:
                    for t in range(2):
                        at_ = acc[:, t * 512:(t + 1) * 512]
                        act(sqb, at_, AF.Square, [accB], [sqbB])
                        ps, pb = big()
                        mm(ps[:, :], ones_b, sqb, True, True, [sqbB, cbfB], [pb])
                        act(rn, ps[:, :], AF.Ln, [pb], [rnB], bias=eps_rms, scale=1.0)
                        act(rn, rn, AF.Exp, [rnB], [rnB], scale=-0.5)
                        if X == 0:
                            V(lambda e, at_=at_, t=t: e.scalar_tensor_tensor(out=qT[:, t * 512:(t + 1) * 512], in0=at_, scalar=128 ** -0.5, in1=rn,
                                                                             op0=ALU.mult, op1=ALU.mult), [accB, rnB], [qTB])
                        else:
                            V(lambda e, at_=at_, t=t: e.tensor_tensor(out=kT[:, t * 512:(t + 1) * 512], in0=at_, in1=rn, op=ALU.mult),
                              [accB, rnB], [kTB])
                        yield
                else:
                    for t in range(2):
                        G(lambda e, t=t: e.tensor_copy(out=vT[:, t * 512:(t + 1) * 512], in_=acc[:, t * 512:(t + 1) * 512]), [accB], [vTB])
                    yield
            for h2 in range(2):
                ps, pb = big()
                for b4 in range(4):
                    blk = h2 * 4 + b4
                    for k in range(8):
                        mm(ps[:, b4 * 128:(b4 + 1) * 128], u_(k, blk * 128, (blk + 1) * 128), W(k, 384, 512), k == 0, k == 7,
                           [wAB[s], uB[k][blk // 4]], [pb])
                act(sz[s][:, h2 * 512:(h2 + 1) * 512], ps[:, :], AF.Silu, [pb], [szB[s]])
                yield
            for src, srcB, dst, dstB in ((kT, kTB, kk, kkB), (vT, vTB, vv, vvB)):
                for h2 in range(2):
                    pt4, pt4B = tslot4()
                    for b4 in range(4):
                        blk = h2 * 4 + b4
                        T(lambda e, pt4=pt4, b4=b4, src=src, blk=blk: e.transpose(pt4[:, b4 * 128:(b4 + 1) * 128],
                                                                                  src[:, blk * 128:(blk + 1) * 128], ident_b),
                          [srcB, cbfB], pt4B)
                    act(dst[:, h2 * 512:(h2 + 1) * 512], pt4, AF.Copy, pt4B, [dstB])
                    yield

        def prescanA(h, t, gi):
            n0 = 4 * t
            a0 = t * 512
            bu = bun[gi % 3]
            abu = ab[gi % 2]
            abB = abu["B"]
            P0, Rw4, vb4 = abu["P0"], abu["Rw"], abu["vb"]
            gsel = g3[:, n0:n0 + 4, h]
            bsel = b3[:, n0:n0 + 4, h]
            nbsel = nb3[:, n0:n0 + 4, h]
            V(lambda e: e.tensor_tensor(out=v4(Tg4), in0=bc1(triinc), in1=bc2(gsel), op=ALU.mult), [cstB, bgB], [TgB])
            pgr, pgrB = big()
            mm(pgr[:, :], ones_f, Tg4, True, True, [cstB, TgB], [pgrB])
            pgd, pgdB = big()
            mm(pgd[:, :], negones[:], Tg4, True, False, [cbfB, TgB], [pgdB])
            for i in range(4):
                mm(pgd[:, i * 128:(i + 1) * 128], Tg4[:, i * 128:(i + 1) * 128], ones_f, False, i == 3, [cstB, TgB], [pgdB])
            yield
            V(lambda e: e.tensor_tensor(out=v4(D4), in0=v4(pgd[:, :]), in1=bc1(mls), op=ALU.add), [pgdB, cstB], [DB])
            act(D4, D4, AF.Exp, [DB], [DB])
            V(lambda e: e.tensor_tensor(out=v4(DT4), in0=bc1(mupi), in1=v4(pgd[:, :]), op=ALU.subtract), [pgdB, cstB], [DTB])
            act(DT4, DT4, AF.Exp, [DTB], [DTB])
            act(eg4, pgr[:, :], AF.Exp, [pgrB], [egB])
            yield
            pgr_l = v4(pgr[:, :])[:, :, 127]
            pgd_l = v4(pgd[:, :])[:, :, 127]
            act(gl4, pgr_l, AF.Copy, [pgrB], [smB])
            V(lambda e: e.tensor_tensor(out=t4, in0=pgd_l, in1=gl4, op=ALU.add), [pgdB, smB], [smB])
            act(egp4, t4, AF.Exp, [smB], [smB])
            act(ekd4, pgd_l, AF.Exp, [pgdB, smB], [smB], scale=-1.0)
            V(lambda e: e.tensor_tensor(out=bege4, in0=egp4, in1=bsel, op=ALU.mult), [smB, bgB], [smB])
            act(bu["last"], v4(eg4)[:, :, 127], AF.Copy, [egB], [bu["B"]])
            yield
            pkk, pkkB = big()
            for i in range(4):
                ks = kT[:, a0 + i * 128:a0 + (i + 1) * 128]
                mm(pkk[:, i * 128:(i + 1) * 128], ks, ks, True, True, [kTB], [pkkB])
            pqk, pqkB = big()
            for i in range(4):
                ks = kT[:, a0 + i * 128:a0 + (i + 1) * 128]
                mm(pqk[:, i * 128:(i + 1) * 128], ks, qT[:, a0 + i * 128:a0 + (i + 1) * 128], True, True, [kTB, qTB], [pqkB])
            yield
            V(lambda e: e.tensor_tensor(out=P0, in0=pkk[:, :], in1=D4, op=ALU.mult), [pkkB, DB], [abB])
            V(lambda e: e.tensor_tensor(out=v4(P0), in0=v4(P0), in1=bc2(nbsel), op=ALU.mult), [abB, bgB], [abB])
            V(lambda e: e.tensor_tensor(out=bu["intraT"], in0=pqk[:, :], in1=DT4, op=ALU.mult), [pqkB, DTB], [bu["B"]])
            yield
            G(lambda e: e.tensor_tensor(out=bu["qdT"], in0=qT[:, a0:a0 + 512], in1=eg4, op=ALU.mult), [qTB, egB], [bu["B"]])
            G(lambda e: e.tensor_tensor(out=v4(bu["kdec"]), in0=v4(kk[:, a0:a0 + 512]), in1=bc2(ekd4), op=ALU.mult), [kkB, smB], [bu["B"]])
            G(lambda e: e.tensor_tensor(out=v4(Rw4), in0=v4(kk[:, a0:a0 + 512]), in1=bc2(bege4), op=ALU.mult), [kkB, smB], [abB])
            G(lambda e: e.tensor_tensor(out=v4(vb4), in0=v4(vv[:, a0:a0 + 512]), in1=bc2(bsel), op=ALU.mult), [vvB, bgB], [abB])
            yield

        def prescanB(h, t, gi):
            bu = bun[gi % 3]
            abu = ab[gi % 2]
            abB = abu["B"]
            P0, Rw4, vb4 = abu["P0"], abu["Rw"], abu["vb"]
            ptp, ptpB = big()
            for i in range(4):
                T(lambda e, i=i: e.transpose(ptp[:, i * 128:(i + 1) * 128], P0[:, i * 128:(i + 1) * 128], ident_f), [abB, cstB], [ptpB])
            act(PTab[0], ptp[:, :], AF.Copy, [ptpB], [PTabB[0]])
            V(lambda e: e.tensor_tensor(out=v4(Za), in0=v4(ptp[:, :]), in1=bc1(ident_f), op=ALU.add), [ptpB, cstB], [ZaB])
            yield
            pend = None
            for s_ in range(1, 7):
                cur, prv = s_ % 2, (s_ - 1) % 2
                Pin, PinB = (P0, abB) if s_ == 1 else (Pab[prv], PabB[prv])
                PTin, PTinB = PTab[prv], PTabB[prv]
                pp_, ppB_ = big()
                for i in range(4):
                    sl = slice(i * 128, (i + 1) * 128)
                    mm(pp_[:, sl], PTin[:, sl], Pin[:, sl], True, True, [PTinB, PinB], [ppB_])
                if os.environ.get("K_FINE", "0") == "1":
                    yield
                if s_ < 6:
                    pq_, pqB_ = big()
                    for i in range(4):
                        sl = slice(i * 128, (i + 1) * 128)
                        mm(pq_[:, sl], Pin[:, sl], PTin[:, sl], True, True, [PTinB, PinB], [pqB_])
                if os.environ.get("K_FINE", "0") == "1":
                    yield
                if pend is not None:
                    pend()
                    if os.environ.get("K_FINE", "0") == "1":
                        yield
                act(Pab[cur], pp_[:, :], AF.Copy, [ppB_], [PabB[cur]])
                if s_ < 6:
                    V(lambda e, pq_=pq_, cur=cur: e.tensor_copy(out=PTab[cur], in_=pq_[:, :]), [pqB_], [PTabB[cur]])
                yield

                def zupd(cur=cur):
                    pz_, pzB_ = big()
                    for i in range(4):
                        sl = slice(i * 128, (i + 1) * 128)
                        mm(pz_[:, sl], Pab[cur][:, sl], Za[:, sl], True, True, [PabB[cur], ZaB], [pzB_])
                    V(lambda e, pz_=pz_: e.tensor_tensor(out=Za, in0=pz_[:, :], in1=Za, op=ALU.add), [pzB_, ZaB], [ZaB])
                pend = zupd
            pend()
            yield
            V(lambda e: e.tensor_copy(out=Zb4, in_=Za), [ZaB], [ZbB])
            pu, puB = big()
            for i in range(4):
                sl = slice(i * 128, (i + 1) * 128)
                mm(pu[:, sl], Zb4[:, sl], vb4[:, sl], True, True, [ZbB, abB], [puB])
            act(bu["U"], pu[:, :], AF.Copy, [puB], [bu["B2"]])
            pw, pwB = big()
            for i in range(4):
                sl = slice(i * 128, (i + 1) * 128)
                mm(pw[:, sl], Rw4[:, sl], Zb4[:, sl], True, True, [ZbB, abB], [pwB])
            act(bu["negWT"], pw[:, :], AF.Copy, [pwB], [bu["B2"]], scale=-1.0)
            yield

        def scan(h, t, gi):
            bu = bun[gi % 3]
            bB = bu["B"]
            b2 = bu["B2"]
            s = h % 2
            Sh = S_f[:, h * 128:(h + 1) * 128]
            if t == 0:
                if hf == 0:
                    V(lambda e: e.memset(Sh, 0.0), [], [S_fB[h]])
                V(lambda e: e.tensor_copy(out=S_b, in_=Sh), [S_fB[h]], [SbB])
            for i in range(4):
                n = 4 * t + i
                c0 = n * 128
                sl = slice(i * 128, (i + 1) * 128)
                p1, p1B = quarter()
                mm(p1, bu["negWT"][:, sl], S_b, True, True, [b2, SbB], [p1B])
                V(lambda e, p1=p1, sl=sl: e.tensor_tensor(out=vnew, in0=p1, in1=bu["U"][:, sl], op=ALU.add), [p1B, b2], [vnB])
                yield
                po_, poB = quarter()
                mm(po_, bu["qdT"][:, sl], S_b, True, False, [bB, SbB], [poB])
                mm(po_, bu["intraT"][:, sl], vnew, False, True, [bB, vnB], [poB])
                pS, pSB = quarter()
                mm(pS, bu["kdec"][:, sl], vnew, True, True, [bB, vnB], [pSB])
                V(lambda e, pS=pS, i=i: e.scalar_tensor_tensor(out=Sh, in0=Sh, scalar=bu["last"][:, i:i + 1], in1=pS,
                                                               op0=ALU.mult, op1=ALU.add), [S_fB[h], pSB, bB], [S_fB[h]])
                V(lambda e: e.tensor_copy(out=S_b, in_=Sh), [S_fB[h]], [SbB])
                yield
                V(lambda e: e.memset(ssq, 0.0), [], [scB])
                act(junk, po_, AF.Square, [poB, scB], [jkB, scB], accum_out=ssq)
                act(rms, ssq, AF.Ln, [scB], [scB], bias=eps_rms, scale=1.0 / 128)
                act(rms, rms, AF.Exp, [scB], [scB], scale=-0.5)
                V(lambda e, po_=po_: e.scalar_tensor_tensor(out=tt, in0=po_, scalar=rms, in1=normw, op0=ALU.mult, op1=ALU.mult),
                  [poB, scB, ppB], [ttB])
                yield
                G(lambda e, c0=c0: e.tensor_tensor(out=obt, in0=tt, in1=sz[s][:, c0:c0 + 128], op=ALU.mult), [ttB, szB[s]], [obB])
                pt2, pt2B = tslot()
                T(lambda e, pt2=pt2: e.transpose(pt2, obt, ident_b), [obB, cbfB], [pt2B])
                act(oT[:, h * 1024 + c0:h * 1024 + c0 + 128], pt2, AF.Copy, [pt2B], [oB[h][n // 4]])
                yield

        groups = [(h, t) for h in range(8) for t in range(2)]
        NG = len(groups)

        def a_stream(gi):
            if gi >= NG:
                return None
            h, t = groups[gi]
            if t == 0:
                return chain(preamble(h), prescanA(h, t, gi))
            return prescanA(h, t, gi)
        interleave(a_stream(0))
        SR = int(os.environ.get("K_SR", "1"))
        for gi in range(NG + 1):
            gens, rngs, reps = [], [], []
            if gi >= 1:
                gens.append(scan(groups[gi - 1][0], groups[gi - 1][1], gi - 1))
                rngs.append((6, 8))
                reps.append(SR)
            if gi < NG:
                gens.append(prescanB(groups[gi][0], groups[gi][1], gi))
                rngs.append((3, 6))
                reps.append(int(os.environ.get("K_BR", "1")))
            nxt = a_stream(gi + 1)
            if nxt is not None:
                gens.append(nxt)
                rngs.append((0, 3))
                reps.append(int(os.environ.get("K_AR", "1")))
            interleave(*gens, rngs=rngs, reps=reps)

    for l in range(L):
        if stop == "p0":
            break
        for hf in range(2):
            phase_u(l, hf, 0)
            if stop == "u":
                break
            phase_attn(l, hf)
            P.barrier(allW)
            if stop == "attn":
                break
            phase_merge(l, hf, 0, False)
            P.barrier(allW)
            if stop == "ma":
                break
            phase_dn(l, hf)
            P.barrier(allW)
            if stop == "dn":
                break
            phase_merge(l, hf, 1, True)
            P.barrier(allW)
            if stop == "ln1":
                continue
            phase_ffn(l, hf)
            P.barrier(allW)
        if stop is not None and stop != "ln1":
            break

    osem = P.dsem("out")
    evs = []
    for c in range(8):
        evs.append(P.dma("sync", lambda e, c=c: e.dma_start(out=yT_d[:, c * SEQ:(c + 1) * SEQ], in_=xT[:, c * SEQ:(c + 1) * SEQ]),
                         osem, reads=xB[c]))
    if dbg:
        dsm = P.dsem("dbg")
        evs.append(P.dma("sync", lambda e: e.dma_start(out=dbg_d, in_=oT[:]), dsm, reads=[b for row in oB for b in row]))
    P.wait_events("sync", evs)
    P.emit()
    return nc


def _kp(w):
    n = w.shape[1]
    return np.ascontiguousarray(w.reshape(8, 128, n).transpose(1, 0, 2))


def _colT(v, nch):
    return np.ascontiguousarray(v.reshape(nch, 128).T)


def pack_shared(inp, L):
    PO, NPP = pp_off(L)
    pp = np.zeros((128, NPP), np.float32)

    def put(name, l, arr):
        o, per = PO[name]
        pp[:, o + l * per:o + (l + 1) * per] = arr
    wada = np.zeros((L, 8, 128, 6144), np.float32)
    wA = np.zeros((L, 4, 128, 8 * 448), np.float32)
    wD = np.zeros((L, 8, 128, 8 * 512), np.float32)
    wBA = np.zeros((L, 128, 128), np.float32)
    wM = np.zeros((L, 5, 128, 8192), np.float32)
    wF = np.zeros((L, 8, 128, 8192), np.float32)
    for l in range(L):
        put("b_ada", l, _colT(inp["b_ada"][l], 48))
        for nm in ("ln1_g", "ln1_b", "ln2_g", "ln2_b", "b_ff2"):
            put(nm, l, _colT(inp[nm][l], 8))
        put("b_ff1", l, _colT(inp["b_ff1"][l], 32))
        cw = inp["conv_w"][l]
        put("convw", l, cw.reshape(4, 24, 128).transpose(2, 0, 1).reshape(128, 96))
        put("sinks", l, np.broadcast_to(inp["sinks"][l][None, :], (128, 16)))
        put("a_log", l, np.broadcast_to(inp["a_log"][l][None, :], (128, 8)))
        put("dt_bias", l, np.broadcast_to(inp["dt_bias"][l][None, :], (128, 8)))
        put("normw", l, np.broadcast_to(inp["dn_norm_w"][l][None, :], (128, 128)))
        wa = _kp(inp["w_ada"][l])
        for piece in range(8):
            wada[l, piece] = wa[:, :, piece * 768:(piece + 1) * 768].reshape(128, 6144)
        wi = _kp(inp["w_in"][l])
        for g in range(4):
            q = wi[:, :, g * 256:(g + 1) * 256]
            k = wi[:, :, 1024 + g * 64:1024 + (g + 1) * 64]
            v = wi[:, :, 1280 + g * 64:1280 + (g + 1) * 64]
            wA[l, g] = np.concatenate([q, k, k, v], axis=2).reshape(128, 8 * 448)
        for h in range(8):
            parts = [wi[:, :, 1536 + X * 1024 + h * 128:1536 + X * 1024 + (h + 1) * 128] for X in range(4)]
            wD[l, h] = np.concatenate(parts, axis=2).reshape(128, 8 * 512)
        wBA[l] = wi[:, :, 5632:5648].reshape(128, 128)
        wM[l, 0] = _kp(inp["w_oa"][l]).reshape(128, 8192)
        wM[l, 1] = wi[:, :, 5648:6672].reshape(128, 8192)
        wM[l, 2] = _kp(inp["w_ob"][l]).reshape(128, 8192)
        wM[l, 3] = wi[:, :, 6672:7696].reshape(128, 8192)
        wM[l, 4] = _kp(inp["w_out"][l]).reshape(128, 8192)
        w1 = _kp(inp["w_ff1"][l])
        w2 = inp["w_ff2"][l]
        for F in range(8):
            wF[l, F, :, 0:4096] = w1[:, :, F * 512:(F + 1) * 512].reshape(128, 4096)
            blk = w2[F * 512:(F + 1) * 512, :].reshape(4, 128, 1024).transpose(1, 0, 2)
            wF[l, F, :, 4096:8192] = blk.reshape(128, 4096)
    p = np.arange(128)[:, None]
    f = np.arange(128)[None, :]
    cst = np.concatenate([
        (p == f).astype(np.float32), np.ones((128, 128), np.float32), (p <= f).astype(np.float32),
        np.where(p > f, 0.0, NEG).astype(np.float32), np.where(f >= p, 0.0, NEG).astype(np.float32),
        (p > f).astype(np.float32)], axis=1)
    return dict(pp=pp, cst=cst, wada=wada, wA=wA, wD=wD, wBA=wBA, wM=wM, wF=wF)


def make_in_maps(inp, L, ncores):
    shared = pack_shared(inp, L)
    maps = []
    for b in range(ncores):
        m = dict(shared)
        pp = shared["pp"].copy()
        pp[:, 0:8] = _colT(inp["c"][b], 8)
        m["pp"] = pp
        m["xT"] = np.ascontiguousarray(inp["x"][b].T.reshape(8, 128, SEQ).transpose(1, 0, 2)).reshape(128, 8 * SEQ)
        maps.append(m)
    return maps


def unpack_out(yT):
    return np.ascontiguousarray(yT.reshape(128, 8, SEQ).transpose(2, 1, 0).reshape(SEQ, D))


_NC_CACHE = {}


def kernel(**inputs):
    inp = {k: np.asarray(v, dtype=np.float32) for k, v in inputs.items()}
    ncores = 8
    if "nc" not in _NC_CACHE:
        _NC_CACHE["nc"] = build(DEPTH)
    nc = _NC_CACHE["nc"]
    maps = make_in_maps(inp, DEPTH, ncores)
    res = run_bass_kernel_spmd(nc, maps, core_ids=list(range(ncores)))
    out = np.stack([unpack_out(res.results[b]["yT"]) for b in range(ncores)], axis=0)
    return out.astype(np.float32)
```

```python
import contextlib
import os
import numpy as np
CUT = int(os.environ.get('K_CUT', '99'))
import concourse.bass as bass
import concourse.mybir as mybir
from concourse.bass_utils import run_bass_kernel_spmd

F32 = mybir.dt.float32
BF16 = mybir.dt.bfloat16
ALU = mybir.AluOpType
AF = mybir.ActivationFunctionType

ALL_ENG = ("sync", "tensor", "vector", "scalar", "gpsimd")
D = 1024
SEQ = 2048
DEPTH = 4
ALPHA = (2 * DEPTH) ** 0.25
LN_EPS = 1e-5
RMS_EPS = 1e-6
NEG = -30000.0


class Buf:
    __slots__ = ("name", "w", "r", "excl")

    def __init__(self, name="", excl=False):
        self.name = name
        self.w = None
        self.r = []
        self.excl = excl


class Prog:
    def __init__(self, nc, same_engine_sync=True):
        self.nc = nc
        self.ops = {e: [] for e in ALL_ENG}
        self.dsem_count = {}
        self.same = same_engine_sync
        self.es = contextlib.ExitStack()
        self.dsem_names = []

    def sb(self, name, shape, dt):
        return self.es.enter_context(self.nc.sbuf_tensor("sb_" + name, list(shape), dt))

    def ps(self, name, shape, dt):
        return self.es.enter_context(self.nc.psum_tensor("ps_" + name, list(shape), dt))

    def dsem(self, name):
        self.dsem_count[name] = 0
        self.dsem_names.append(name)
        return name

    def _deps(self, eng, reads, writes):
        deps = []
        for b in reads:
            if b.w is not None:
                deps.append(b.w)
        for b in writes:
            if b.w is not None:
                deps.append(b.w)
            deps.extend(b.r)
        out = []
        for d in deps:
            if d[0] == "e" and d[1] == eng and (eng in ("tensor", "sync") or not self.same):
                continue
            if d not in out:
                out.append(d)
        return out

    def op(self, eng, fn, reads=(), writes=()):
        writes = list(writes) + [b for b in reads if b.excl]
        reads = [b for b in reads if not b.excl]
        deps = self._deps(eng, reads, writes)
        idx = len(self.ops[eng])
        self.ops[eng].append(dict(fn=fn, deps=deps, sig=False, dma=None))
        ev = ("e", eng, idx)
        for b in reads:
            b.r.append(ev)
        for b in writes:
            b.w = ev
            b.r = []
        return ev

    def dma(self, eng, fn, dsem, reads=(), writes=()):
        deps = self._deps(eng, reads, writes)
        self.dsem_count[dsem] += 16
        ev = ("d", dsem, self.dsem_count[dsem])
        self.ops[eng].append(dict(fn=fn, deps=deps, sig=False, dma=dsem))
        for b in reads:
            b.r.append(ev)
        for b in writes:
            b.w = ev
            b.r = []
        return ev

    def wait_events(self, eng, evs):
        self.ops[eng].append(dict(fn=None, deps=list(evs), sig=False, dma=None))

    def barrier(self, bufs=()):
        evs = []
        for eng in ("tensor", "vector", "scalar", "gpsimd"):
            n = len(self.ops[eng])
            i = n - 1
            while i >= 0 and (self.ops[eng][i]["fn"] is None or self.ops[eng][i]["dma"] is not None):
                i -= 1
            if i >= 0:
                evs.append(("e", eng, i))
        for eng in ("tensor", "vector", "scalar", "gpsimd"):
            self.wait_events(eng, [e for e in evs if e[1] != eng])
        for b in bufs:
            b.r = list(b.r) + evs
        return evs

    def emit(self):
        nc = self.nc
        for eng in ALL_ENG:
            for rec in self.ops[eng]:
                for d in rec["deps"]:
                    if d[0] == "e":
                        self.ops[d[1]][d[2]]["sig"] = True
        cnt = {}
        for eng in ALL_ENG:
            c = 0
            for i, rec in enumerate(self.ops[eng]):
                if rec["sig"]:
                    c += 1
                    cnt[(eng, i)] = c
        sems = {}
        for eng in ALL_ENG:
            sems[eng] = self.es.enter_context(nc.semaphore("s_" + eng))
        dsems = {}
        for n in self.dsem_names:
            dsems[n] = self.es.enter_context(nc.semaphore("d_" + n))
        self.nwaits = 0

        def run(eng, e):
            known = {}
            for rec in self.ops[eng]:
                for d in rec["deps"]:
                    if d[0] == "e":
                        key = ("e", d[1])
                        val = cnt[(d[1], d[2])]
                        sem = sems[d[1]]
                    else:
                        key = ("d", d[1])
                        val = d[2]
                        sem = dsems[d[1]]
                    if known.get(key, 0) >= val:
                        continue
                    known[key] = val
                    e.wait_ge(sem, val)
                    self.nwaits += 1
                if rec["fn"] is None:
                    continue
                ins = rec["fn"](e)
                if rec["dma"] is not None:
                    ins.then_inc(dsems[rec["dma"]], 16)
                elif rec["sig"]:
                    ins.then_inc(sems[eng], 1)

        with nc.Block() as block:
            @block.sync
            def _(e):
                run("sync", e)

            @block.tensor
            def _(e):
                run("tensor", e)

            @block.vector
            def _(e):
                run("vector", e)

            @block.scalar
            def _(e):
                run("scalar", e)

            @block.gpsimd
            def _(e):
                run("gpsimd", e)
        self.es.close()


class Carver:
    def __init__(self, regions):
        self.regions = regions
        self.off = [0] * len(regions)

    def take(self, nelem, dt):
        nb = nelem * (4 if dt == F32 else 2)
        nb = (nb + 63) // 64 * 64
        for i, r in enumerate(self.regions):
            cap = r.shape[1] * 2
            if self.off[i] + nb <= cap:
                o = self.off[i]
                self.off[i] += nb
                ap = r[:, o // 2:(o + nb) // 2]
                if dt == F32:
                    ap = ap.bitcast(F32)
                return ap[:, 0:nelem]
        raise MemoryError(f"carver out of space for {nelem} {dt}")


NPP_L = 48 + 8 * 4 + 32 + 8 + 96 + 16 + 8 + 8 + 128


def pp_off(L):
    o = {}
    p = 8
    for name, n in (("b_ada", 48), ("ln1_g", 8), ("ln1_b", 8), ("ln2_g", 8), ("ln2_b", 8), ("b_ff1", 32),
                    ("b_ff2", 8), ("convw", 96), ("sinks", 16), ("a_log", 8), ("dt_bias", 8), ("normw", 128)):
        o[name] = (p, n)
        p += n * L
    return o, p


def build(L=DEPTH, stop=None, dbg=False):
    nc = bass.Bass("TRN2", target_bir_lowering=False)
    P = Prog(nc, same_engine_sync=not os.environ.get("K_NOSAME"))
    PO, NPP = pp_off(L)

    xT_d = nc.dram_tensor("xT", [128, 8 * SEQ], F32, kind="ExternalInput").ap()
    pp_d = nc.dram_tensor("pp", [128, NPP], F32, kind="ExternalInput").ap()
    cst_d = nc.dram_tensor("cst", [128, 6 * 128], F32, kind="ExternalInput").ap()
    wada_d = nc.dram_tensor("wada", [L, 8, 128, 6144], F32, kind="ExternalInput").ap()
    wA_d = nc.dram_tensor("wA", [L, 4, 128, 8 * 448], F32, kind="ExternalInput").ap()
    wD_d = nc.dram_tensor("wD", [L, 8, 128, 8 * 512], F32, kind="ExternalInput").ap()
    wBA_d = nc.dram_tensor("wBA", [L, 128, 128], F32, kind="ExternalInput").ap()
    wM_d = nc.dram_tensor("wM", [L, 5, 128, 8192], F32, kind="ExternalInput").ap()
    wF_d = nc.dram_tensor("wF", [L, 8, 128, 8192], F32, kind="ExternalInput").ap()
    yT_d = nc.dram_tensor("yT", [128, 8 * SEQ], F32, kind="ExternalOutput").ap()
    if dbg:
        dbg_d = nc.dram_tensor("dbg", [128, 8192], BF16, kind="ExternalOutput").ap()

    xT = P.sb("xTs", [128, 8 * SEQ], F32)
    xB = [[Buf(f"x{c}_{t}") for t in range(4)] for c in range(8)]
    uT = P.sb("uTs", [128, 8192], BF16)
    uB = [[Buf(f"u{c}_{t}") for t in range(2)] for c in range(8)]
    oT = P.sb("oTs", [128, 8192], BF16)
    oB = [[Buf(f"o{c}_{t}") for t in range(2)] for c in range(8)]
    wB = [P.sb(f"wB{i}", [128, 8192], BF16) for i in range(3)]
    wBB = [Buf(f"wB{i}") for i in range(3)]
    wBsem = [P.dsem(f"wB{i}") for i in range(3)]
    wAs = P.sb("wAs", [128, 8192], BF16)
    wAB = [Buf("wA0"), Buf("wA1")]
    wAsem = [P.dsem("wA0"), P.dsem("wA1")]
    wba = P.sb("wba", [128, 128], BF16)
    wbaB = Buf("wba")
    wbasem = P.dsem("wba")
    ppt = P.sb("ppt", [128, NPP], F32)
    ppB = Buf("pp")
    cst = P.sb("cst", [128, 768], F32)
    cstB = Buf("cst")
    cbf = P.sb("cbf", [128, 4 * 128], BF16)
    cbfB = Buf("cbf")
    negones = P.sb("negones", [128, 128], F32)
    mod = P.sb("mod", [128, L * 48], F32)
    sb2 = P.sb("sb2", [128, L * 8], F32)
    fus = P.sb("fus", [128, L * 24], F32)
    epsT = P.sb("epsT", [128, 4], F32)
    cact = P.sb("cact", [128, 8], F32)
    cactb = P.sb("cactb", [128, 8], BF16)
    cactB = Buf("cact")
    kcar = P.sb("kcar", [128, 4 * 128], BF16)
    vcar = P.sb("vcar", [128, 4 * 128], BF16)
    carB = [Buf(f"car{g}") for g in range(4)]
    ccar = P.sb("ccar", [128, 8 * 9], F32)
    ccarB = [Buf(f"cc{h}") for h in range(8)]
    S_f = P.sb("S_f", [128, 8 * 128], F32)
    S_fB = [Buf(f"S{h}") for h in range(8)]
    bet = P.sb("bet", [128, 64], F32)
    nbet = P.sb("nbet", [128, 64], F32)
    gtk = P.sb("gtk", [128, 64], F32)
    gtmp = P.sb("gtmp", [128, 64], F32)
    nega = P.sb("nega", [128, 8], F32)
    sexp = P.sb("sexp", [128, 16], F32)
    bgB = Buf("bg")
    lyrB = Buf("lyr")
    free_ar = P.sb("free_ar", [128, 14336], BF16)

    ident_f = cst[:, 0:128]
    ones_f = cst[:, 128:256]
    triinc = cst[:, 256:384]
    mls = cst[:, 384:512]
    mupi = cst[:, 512:640]
    ident_b = cbf[:, 0:128]
    ones_b = cbf[:, 128:256]
    mcur_b = cbf[:, 256:384]
    mprev_b = cbf[:, 384:512]

    pbank = [P.ps(f"pb{i}", [128, 512], F32) for i in range(8)]
    pbankB = [Buf(f"pb{i}", excl=True) for i in range(8)]
    st = dict(i=0)

    def big():
        lo, hi = st.get("rng", (0, 8))
        key = ("i", lo, hi)
        i = st.get(key, lo)
        st[key] = lo + (i + 1 - lo) % (hi - lo)
        return pbank[i], pbankB[i]

    def quarter():
        t, b = big()
        return t[:, 0:128], b

    def tslot():
        t, b = big()
        return t[:, :].bitcast(BF16)[:, 0:128], b

    def tslot4():
        t, b = big()
        return t[:, :].bitcast(BF16)[:, 0:512], [b]

    def T(fn, r, w):
        return P.op("tensor", fn, r, w)

    def V(fn, r, w):
        return P.op("vector", fn, r, w)

    def A(fn, r, w):
        return P.op("scalar", fn, r, w)

    def G(fn, r, w):
        return P.op("gpsimd", fn, r, w)

    def mm(out, lhsT, rhs, start, stop, r, w):
        return T(lambda e: e.matmul(out, lhsT, rhs, start=start, stop=stop), r, w)

    def act(out, in_, func, r, w, **kw):
        return A(lambda e: e.activation(out=out, in_=in_, func=func, **kw), r, w)

    def u_(k, a, b):
        return uT[:, k * 1024 + a:k * 1024 + b]

    def x_(c, a, b):
        return xT[:, c * SEQ + a:c * SEQ + b]

    def ppc(name, l, j=0, n=1):
        o, per = PO[name]
        return ppt[:, o + l * per + j:o + l * per + j + n]

    def modc(l, w, c=0, n=8):
        o = l * 48 + w * 8 + c
        return mod[:, o:o + n]

    P.dma("sync", lambda e: e.dma_start(out=ppt[:], in_=pp_d), P.dsem("pp"), writes=[ppB])
    P.dma("sync", lambda e: e.dma_start(out=cst[:], in_=cst_d), P.dsem("cst"), writes=[cstB])
    xsem = P.dsem("x")
    for c in range(8):
        P.dma("sync", lambda e, c=c: e.dma_start(out=xT[:, c * SEQ:(c + 1) * SEQ], in_=xT_d[:, c * SEQ:(c + 1) * SEQ]),
              xsem, writes=xB[c])
    for c in range(8):
        for t_ in range(4):
            xB[c][t_].w = ("d", xsem, 128)
    V(lambda e: e.tensor_copy(out=cbf[:, 0:384], in_=cst[:, 0:384]), [cstB], [cbfB])
    V(lambda e: e.tensor_copy(out=cbf[:, 384:512], in_=cst[:, 640:768]), [cstB], [cbfB])
    V(lambda e: e.memset(negones[:], -1.0), [], [cbfB])
    V(lambda e: e.memset(epsT[:, 0:1], LN_EPS / (ALPHA * ALPHA)), [], [cbfB])
    V(lambda e: e.memset(epsT[:, 1:2], RMS_EPS), [], [cbfB])
    V(lambda e: e.memset(epsT[:, 2:3], 1.0), [], [cbfB])
    V(lambda e: e.memset(epsT[:, 3:4], 0.0), [], [cbfB])
    eps_ln = epsT[:, 0:1]
    eps_rms = epsT[:, 1:2]
    one_c = epsT[:, 2:3]
    act(cact[:], ppt[:, 0:8], AF.Silu, [ppB], [cactB])
    V(lambda e: e.tensor_copy(out=cactb[:], in_=cact[:]), [cactB], [cactB])
    modBs = [Buf(f"mod{l}") for l in range(L)]
    modst = dict(cnt=0)
    o_b, _ = PO["b_ada"]
    o_f2, _ = PO["b_ff2"]

    def mod_dma(l, piece):
        s_ = (l * 8 + piece) % 3
        P.dma("gpsimd", lambda e: e.dma_start(out=wB[s_][:, 0:6144], in_=wada_d[l, piece]), wBsem[s_], writes=[wBB[s_]])

    def mod_piece(l, piece):
        s_ = (l * 8 + piece) % 3
        psm, psmB = big()
        for m in range(6):
            for k in range(8):
                mm(psm[:, m:m + 1], wB[s_][:, k * 768 + m * 128:k * 768 + (m + 1) * 128], cactb[:, k:k + 1],
                   k == 0, k == 7, [wBB[s_], cactB], [psmB])
        c0_ = l * 48 + piece * 6
        V(lambda e: e.tensor_tensor(out=mod[:, c0_:c0_ + 6], in0=psm[:, 0:6], in1=ppt[:, o_b + c0_:o_b + c0_ + 6], op=ALU.add),
          [psmB, ppB], [modBs[l]])

    def mod_final(l):
        for w_ in (1, 4):
            V(lambda e, w_=w_: e.tensor_scalar_add(out=modc(l, w_), in0=modc(l, w_), scalar1=1.0), [modBs[l]], [modBs[l]])
        for w_ in (2, 5):
            V(lambda e, w_=w_: e.tensor_scalar(out=modc(l, w_), in0=modc(l, w_), scalar1=1.0, scalar2=1.0 / ALPHA,
                                               op0=ALU.add, op1=ALU.mult), [modBs[l]], [modBs[l]])
        V(lambda e: e.tensor_tensor(out=sb2[:, l * 8:(l + 1) * 8], in0=modc(l, 5), in1=ppt[:, o_f2 + l * 8:o_f2 + (l + 1) * 8], op=ALU.mult),
          [modBs[l], ppB], [modBs[l]])
        f0 = l * 24
        V(lambda e: e.tensor_tensor(out=fus[:, f0:f0 + 8], in0=ppc("ln1_g", l, 0, 8), in1=modc(l, 4), op=ALU.mult), [modBs[l], ppB], [modBs[l]])
        V(lambda e: e.tensor_tensor(out=fus[:, f0 + 8:f0 + 16], in0=ppc("ln1_b", l, 0, 8), in1=modc(l, 4), op=ALU.mult), [modBs[l], ppB], [modBs[l]])
        V(lambda e: e.tensor_tensor(out=fus[:, f0 + 8:f0 + 16], in0=fus[:, f0 + 8:f0 + 16], in1=modc(l, 3), op=ALU.add), [modBs[l]], [modBs[l]])
        V(lambda e: e.tensor_tensor(out=fus[:, f0 + 16:f0 + 24], in0=ppc("ln1_b", l, 0, 8), in1=sb2[:, l * 8:(l + 1) * 8], op=ALU.add),
          [modBs[l], ppB], [modBs[l]])

    for piece in range(8):
        mod_dma(0, piece)
        mod_piece(0, piece)
    mod_final(0)
    allW = wBB + wAB
    P.barrier(allW)

    def phase_u(l, hf, which):
        scw, shw = (1, 0) if which == 0 else (4, 3)
        i = 0
        for c in range(8):
            for t in range(2):
                T0 = hf * 1024 + t * 512
                if i % 2 == 0:
                    act(u_(c, t * 512, (t + 1) * 512), x_(c, T0, T0 + 512), AF.Identity, [xB[c][hf * 2 + t], modBs[l]],
                        [uB[c][t]], scale=modc(l, scw, c, 1), bias=modc(l, shw, c, 1))
                else:
                    V(lambda e, c=c, t=t, T0=T0: e.tensor_scalar(out=u_(c, t * 512, (t + 1) * 512), in0=x_(c, T0, T0 + 512),
                                                                 scalar1=modc(l, scw, c, 1), scalar2=modc(l, shw, c, 1),
                                                                 op0=ALU.mult, op1=ALU.add),
                      [xB[c][hf * 2 + t], modBs[l]], [uB[c][t]])
                i += 1

    def load_w(eng, dst_ap, src_ap, sem, buf):
        P.dma(eng, lambda e: e.dma_start(out=dst_ap, in_=src_ap), sem, writes=[buf])

    def prefetch_merge(l, X):
        load_w("gpsimd", wB[0][:], wM_d[l, 2 * X], wBsem[0], wBB[0])
        load_w("gpsimd", wB[1][:], wM_d[l, 2 * X + 1], wBsem[1], wBB[1])
        load_w("gpsimd", wB[2][:], wM_d[l, 4], wBsem[2], wBB[2])

    def phase_attn(l, hf):
        cv = Carver([free_ar[:]])
        do_mod = (hf == 0 and l + 1 < L)
        qT = [cv.take(1024, BF16) for _ in range(4)]
        kT2 = cv.take(1152, BF16)
        vaug = cv.take(9 * 128, BF16)
        Et = [[cv.take(512, BF16) for _ in range(2)] for _ in range(2)]
        rec = cv.take(512, F32)
        qB = [Buf(), Buf(), Buf(), Buf()]
        kB = Buf()
        vB = Buf()
        EB = [[Buf(), Buf()], [Buf(), Buf()]]
        recB = Buf()
        act(sexp[:], ppc("sinks", l, 0, 16), AF.Exp, [ppB], [lyrB])
        V(lambda e: e.memset(vaug[:].rearrange("p (b c) -> p b c", c=128)[:, :, 64:128], 1.0), [], [vB])
        for h in range(4):
            V(lambda e, h=h: e.memset(qT[h][:, :], 0.0), [], [qB[h]])
        load_w("gpsimd", wAs[:, 0:3584], wA_d[l, 0], wAsem[0], wAB[0])
        for g in range(4):
            s = g % 2
            if g + 1 < 4:
                load_w("gpsimd", wAs[:, (1 - s) * 4096:(1 - s) * 4096 + 3584], wA_d[l, g + 1], wAsem[1 - s], wAB[1 - s])
            if do_mod and g == 0:
                for pc in range(3):
                    mod_dma(l + 1, pc)
            Wb = s * 4096

            def W(k, a, b):
                return wAs[:, Wb + k * 448 + a:Wb + k * 448 + b]
            for jp in range(2):
                for t in range(2):
                    ps, pb = big()
                    for k in range(8):
                        mm(ps[:, :], W(k, jp * 128, (jp + 1) * 128), u_(k, t * 512, (t + 1) * 512), k == 0, k == 7,
                           [wAB[s], uB[k][t]], [pb])
                    act(qT[2 * jp][0:64, t * 512:(t + 1) * 512], ps[0:64, :], AF.Copy, [pb], [qB[2 * jp]], scale=0.125)
                    act(qT[2 * jp + 1][64:128, t * 512:(t + 1) * 512], ps[64:128, :], AF.Copy, [pb], [qB[2 * jp + 1]], scale=0.125)
            for t in range(2):
                ps, pb = big()
                for k in range(8):
                    mm(ps[:, :], W(k, 256, 384), u_(k, t * 512, (t + 1) * 512), k == 0, k == 7, [wAB[s], uB[k][t]], [pb])
                V(lambda e, ps=ps, t=t: e.tensor_copy(out=kT2[:, 128 + t * 512:128 + (t + 1) * 512], in_=ps[:, :]), [pb], [kB])
            ps, pb = big()
            for blk in range(8):
                for k in range(8):
                    mm(ps[:, blk * 64:(blk + 1) * 64], u_(k, blk * 128, (blk + 1) * 128), W(k, 384, 448), k == 0, k == 7,
                       [wAB[s], uB[k][blk // 4]], [pb])
            V(lambda e, ps=ps: e.tensor_copy(out=vaug[:].rearrange("p (b c) -> p b c", c=128)[:, 1:9, 0:64],
                                             in_=ps[:, :].rearrange("p (b c) -> p b c", c=64)), [pb], [vB])
            if CUT <= 1:
                return
            if hf == 1:
                V(lambda e, g=g: e.tensor_copy(out=kT2[:, 0:128], in_=kcar[:, g * 128:(g + 1) * 128]), [carB[g]], [kB])
                V(lambda e, g=g: e.tensor_copy(out=vaug[:, 0:64], in_=vcar[:, g * 128:g * 128 + 64]), [carB[g]], [vB])
            def stage1(n, g=g):
                N = hf * 8 + n
                js = [0, 1] if N > 0 else [1]
                par = n % 2
                for j in js:
                    ps, pb = big()
                    for h in range(4):
                        mm(ps[:, h * 128:(h + 1) * 128], kT2[:, (n + j) * 128:(n + j + 1) * 128],
                           qT[h][:, n * 128:(n + 1) * 128], True, True, [kB, qB[h]], [pb])
                    E = Et[j][par]
                    act(E[:, :], ps[:, :], AF.Exp, [pb], [EB[j][par]])
                    mk = mcur_b if j == 1 else mprev_b
                    V(lambda e, E=E, mk=mk: e.tensor_tensor(out=E.rearrange("p (h q) -> p h q", h=4),
                                                            in0=E.rearrange("p (h q) -> p h q", h=4),
                                                            in1=mk.unsqueeze(1).to_broadcast([128, 4, 128]), op=ALU.mult),
                      [EB[j][par], cbfB], [EB[j][par]])
                return js

            def stage2(n, js, g=g):
                par = n % 2
                ps, pb = big()
                for idx, j in enumerate(js):
                    mm(ps[:, :], vaug[:, (n + j) * 128:(n + j + 1) * 128], Et[j][par][:, :], idx == 0, idx == len(js) - 1,
                       [vB, EB[j][par]], [pb])
                V(lambda e, ps=ps: e.tensor_tensor(out=rec[0:64, :].rearrange("p (h q) -> p h q", h=4),
                                                   in0=ps[64:128, :].rearrange("p (h q) -> p h q", h=4),
                                                   in1=sexp[64:128, g * 4:(g + 1) * 4].unsqueeze(2).to_broadcast([64, 4, 128]),
                                                   op=ALU.add), [pb, lyrB], [recB])
                act(rec[0:64, :], rec[0:64, :], AF.Ln, [recB], [recB])
                act(rec[0:64, :], rec[0:64, :], AF.Exp, [recB], [recB], scale=-1.0)
                psv = ps[0:64, :].rearrange("p (a b q) -> p a b q", a=2, b=2)
                rcv = rec[0:64, :].rearrange("p (a b q) -> p a b q", a=2, b=2)
                for odd in range(2):
                    c0 = 2 * g
                    dst = oT[odd * 64:odd * 64 + 64, :].rearrange("p (c t) -> p c t", c=8)[:, c0:c0 + 2, n * 128:(n + 1) * 128]
                    V(lambda e, dst=dst, odd=odd, psv=psv, rcv=rcv: e.tensor_tensor(out=dst, in0=psv[:, :, odd, :], in1=rcv[:, :, odd, :],
                                                                                    op=ALU.mult),
                      [pb, recB], [oB[c0][n // 4], oB[c0 + 1][n // 4]])
            js_cur = stage1(0)
            for n in range(8):
                js_nxt = stage1(n + 1) if n + 1 < 8 else None
                stage2(n, js_cur)
                js_cur = js_nxt
            if hf == 0:
                V(lambda e, g=g: e.tensor_copy(out=kcar[:, g * 128:(g + 1) * 128], in_=kT2[:, 1024:1152]), [kB], [carB[g]])
                V(lambda e, g=g: e.tensor_copy(out=vcar[:, g * 128:g * 128 + 64], in_=vaug[:, 8 * 128:8 * 128 + 64]), [vB], [carB[g]])
            if do_mod:
                for pc in (2 * g, 2 * g + 1):
                    mod_piece(l + 1, pc)
                    if pc + 3 < 8:
                        mod_dma(l + 1, pc + 3)
        if do_mod:
            mod_final(l + 1)
        prefetch_merge(l, 0)

    def ln_alloc(cv):
        return dict(sq=cv.take(512, F32), sacc=cv.take(512, F32), qacc=cv.take(512, F32), mu=cv.take(512, F32),
                    rstd=cv.take(512, F32), tmp=[cv.take(512, F32) for _ in range(2)],
                    B=[Buf() for _ in range(5)], tB=[Buf(), Buf()])

    def layer_norm(l, Tg_, gname, bname, tm, fuse=False):
        sq, sacc, qacc, mu, rstd, tmp = tm["sq"], tm["sacc"], tm["qacc"], tm["mu"], tm["rstd"], tm["tmp"]
        sqB, saB, qaB, muB, rsB = tm["B"]
        tB = tm["tB"]
        a0 = Tg_ * 512
        for c in range(8):
            xs = x_(c, a0, a0 + 512)
            if c == 0:
                V(lambda e, xs=xs: e.tensor_copy(out=sacc, in_=xs), [xB[c][Tg_]], [saB])
                act(qacc, xs, AF.Square, [xB[c][Tg_]], [qaB])
            else:
                V(lambda e, xs=xs: e.tensor_tensor(out=sacc, in0=sacc, in1=xs, op=ALU.add), [xB[c][Tg_], saB], [saB])
                act(sq, xs, AF.Square, [xB[c][Tg_]], [sqB])
                V(lambda e: e.tensor_tensor(out=qacc, in0=qacc, in1=sq, op=ALU.add), [sqB, qaB], [qaB])
            yield
        p1, p1B = big()
        mm(p1[:, :], ones_f, sacc, True, True, [saB, cstB], [p1B])
        p2, p2B = big()
        mm(p2[:, :], ones_f, qacc, True, True, [qaB, cstB], [p2B])
        act(mu, p1[:, :], AF.Copy, [p1B], [muB], scale=1.0 / D)
        V(lambda e: e.tensor_tensor(out=sq, in0=mu, in1=mu, op=ALU.mult), [muB, sqB], [sqB])
        V(lambda e: e.scalar_tensor_tensor(out=rstd, in0=p2[:, :], scalar=1.0 / D, in1=sq, op0=ALU.mult, op1=ALU.subtract),
          [p2B, sqB], [rsB])
        act(rstd, rstd, AF.Ln, [rsB], [rsB], bias=eps_ln, scale=1.0)
        act(rstd, rstd, AF.Exp, [rsB], [rsB], scale=-0.5)
        yield
        for c in range(8):
            xs = x_(c, a0, a0 + 512)
            tt = tmp[c % 2]
            V(lambda e, xs=xs, tt=tt: e.tensor_tensor(out=tt, in0=xs, in1=mu, op=ALU.subtract), [xB[c][Tg_], muB], [tB[c % 2]])
            V(lambda e, tt=tt: e.tensor_tensor(out=tt, in0=tt, in1=rstd, op=ALU.mult), [tB[c % 2], rsB], [tB[c % 2]])
            if fuse:
                f0 = l * 24
                tl = Tg_ % 2
                act(xs, tt, AF.Identity, [tB[c % 2], ppB, modBs[l]], [xB[c][Tg_]], scale=ppc(gname, l, c, 1), bias=fus[:, f0 + 16 + c:f0 + 17 + c])
                V(lambda e, tt=tt, c=c, tl=tl, f0=f0: e.tensor_scalar(out=u_(c, tl * 512, (tl + 1) * 512), in0=tt, scalar1=fus[:, f0 + c:f0 + c + 1],
                                                                      scalar2=fus[:, f0 + 8 + c:f0 + 9 + c], op0=ALU.mult, op1=ALU.add),
                  [tB[c % 2], modBs[l]], [uB[c][tl]])
            else:
                act(xs, tt, AF.Identity, [tB[c % 2], ppB], [xB[c][Tg_]], scale=ppc(gname, l, c, 1), bias=ppc(bname, l, c, 1))
            yield

    def phase_merge(l, hf, X, do_ln):
        if X == 1:
            prefetch_merge(l, X)
        cv = Carver([free_ar[:], wAs[:]])
        sg = [cv.take(512, F32) for _ in range(2)]
        mg = cv.take(8 * 512, BF16)
        sgB = [Buf(), Buf()]
        mgB = [Buf() for _ in range(8)]
        lnt = ln_alloc(cv) if do_ln else None
        for t in range(2):
            Tg_ = hf * 2 + t
            for c in range(8):
                py, pyB = big()
                for k in range(8):
                    mm(py[:, :], wB[0][:, k * 1024 + c * 128:k * 1024 + (c + 1) * 128], oT[:, k * 1024 + t * 512:k * 1024 + (t + 1) * 512],
                       k == 0, k == 7, [wBB[0], oB[k][t]], [pyB])
                pg, pgB = big()
                for k in range(8):
                    mm(pg[:, :], wB[1][:, k * 1024 + c * 128:k * 1024 + (c + 1) * 128], u_(k, t * 512, (t + 1) * 512),
                       k == 0, k == 7, [wBB[1], uB[k][t]], [pgB])
                act(sg[c % 2], pg[:, :], AF.Sigmoid, [pgB], [sgB[c % 2]])
                V(lambda e, c=c, py=py: e.tensor_tensor(out=mg[:, c * 512:(c + 1) * 512], in0=py[:, :], in1=sg[c % 2], op=ALU.mult),
                  [pyB, sgB[c % 2]], [mgB[c]])
            for c2 in range(8):
                pm, pmB = big()
                for c in range(8):
                    mm(pm[:, :], wB[2][:, c * 1024 + c2 * 128:c * 1024 + (c2 + 1) * 128], mg[:, c * 512:(c + 1) * 512],
                       c == 0, c == 7, [wBB[2], mgB[c]], [pmB])
                xs = x_(c2, Tg_ * 512, Tg_ * 512 + 512)
                V(lambda e, pm=pm, xs=xs, c2=c2: e.scalar_tensor_tensor(out=xs, in0=pm[:, :], scalar=modc(l, 2, c2, 1), in1=xs,
                                                                         op0=ALU.mult, op1=ALU.add),
                  [pmB, modBs[l], xB[c2][Tg_]], [xB[c2][Tg_]])
            if do_ln and t == 1:
                load_w("gpsimd", wB[0][:], wF_d[l, 0], wBsem[0], wBB[0])
                load_w("gpsimd", wB[1][:], wF_d[l, 1], wBsem[1], wBB[1])
            if do_ln:
                for _ in layer_norm(l, Tg_, "ln1_g", "ln1_b", lnt, fuse=True):
                    pass

    def phase_ffn(l, hf):
        cv = Carver([free_ar[:], wAs[:]])
        rl = [cv.take(512, F32) for _ in range(2)]
        hT = [cv.take(4 * 512, BF16) for _ in range(2)]
        rlB = [Buf(), Buf()]
        hB = [[Buf() for _ in range(4)] for _ in range(2)]
        lnt = ln_alloc(cv)
        lnt2 = ln_alloc(cv)
        it = 0
        for F in range(8):
            s = F % 3
            if F + 2 < 8:
                s2 = (F + 2) % 3
                load_w("gpsimd", wB[s2][:], wF_d[l, F + 2], wBsem[s2], wBB[s2])
            for t in range(2):
                Tg_ = hf * 2 + t
                hp = it % 2
                it += 1
                for f in range(4):
                    ph, phB = big()
                    for k in range(8):
                        mm(ph[:, :], wB[s][:, k * 512 + f * 128:k * 512 + (f + 1) * 128], u_(k, t * 512, (t + 1) * 512),
                           k == 0, k == 7, [wBB[s], uB[k][t]], [phB])
                    r_ = rl[f % 2]
                    act(r_, ph[:, :], AF.Relu, [phB, ppB], [rlB[f % 2]], bias=ppc("b_ff1", l, F * 4 + f, 1), scale=1.0)
                    V(lambda e, r_=r_, hp=hp, f=f: e.tensor_tensor(out=hT[hp][:, f * 512:(f + 1) * 512], in0=r_, in1=r_, op=ALU.mult),
                      [rlB[f % 2]], [hB[hp][f]])
                for c2 in range(8):
                    po_, poB = big()
                    for f in range(4):
                        mm(po_[:, :], wB[s][:, 4096 + f * 1024 + c2 * 128:4096 + f * 1024 + (c2 + 1) * 128], hT[hp][:, f * 512:(f + 1) * 512],
                           f == 0, f == 3, [wBB[s], hB[hp][f]], [poB])
                    xs = x_(c2, Tg_ * 512, Tg_ * 512 + 512)
                    V(lambda e, po_=po_, xs=xs, c2=c2: e.scalar_tensor_tensor(out=xs, in0=po_[:, :], scalar=modc(l, 5, c2, 1), in1=xs,
                                                                               op0=ALU.mult, op1=ALU.add),
                      [poB, modBs[l], xB[c2][Tg_]], [xB[c2][Tg_]])
        gA = layer_norm(l, hf * 2, "ln2_g", "ln2_b", lnt)
        gB = layer_norm(l, hf * 2 + 1, "ln2_g", "ln2_b", lnt2)
        done = 0
        while done < 2:
            done = 0
            for g_ in (gA, gB):
                try:
                    next(g_)
                except StopIteration:
                    done += 1

    def interleave(*gens, rngs=None, reps=None):
        gens = [(g, (rngs[i] if rngs else (0, 8)), (reps[i] if reps else 1)) for i, g in enumerate(gens) if g is not None]
        while gens:
            for it in list(gens):
                st["rng"] = it[1]
                try:
                    for _ in range(it[2]):
                        next(it[0])
                except StopIteration:
                    gens.remove(it)
        st["rng"] = (0, 8)

    def chain(*gens):
        for g in gens:
            if g is not None:
                yield from g

    def phase_dn(l, hf):
        cv = Carver([free_ar[:], wB[0][:], wB[1][:], wB[2][:]])
        W5 = 512
        pre = cv.take(1028, BF16)
        dg = cv.take(512, BF16)
        dgB = Buf()
        acc = cv.take(1024, F32)
        sqb = cv.take(512, BF16)
        rn = cv.take(512, F32)
        qT = cv.take(1024, BF16)
        kT = cv.take(1024, BF16)
        vT = cv.take(1024, BF16)
        kk = cv.take(1024, BF16)
        vv = cv.take(1024, BF16)
        sz = [cv.take(1024, BF16) for _ in range(2)]
        Tg4 = cv.take(W5, F32)
        D4 = cv.take(W5, F32)
        DT4 = cv.take(W5, F32)
        eg4 = cv.take(W5, F32)
        Pab = [cv.take(W5, F32) for _ in range(2)]
        PTab = [cv.take(W5, F32) for _ in range(2)]
        PabB = [Buf(), Buf()]
        PTabB = [Buf(), Buf()]
        Za = cv.take(W5, F32)
        Zb4 = cv.take(W5, BF16)
        sm4 = cv.take(32, F32)
        ab = [dict(P0=cv.take(W5, F32), Rw=cv.take(W5, BF16), vb=cv.take(W5, BF16), B=Buf()) for _ in range(2)]
        bun = [dict(qdT=cv.take(W5, BF16), kdec=cv.take(W5, BF16), intraT=cv.take(W5, BF16), negWT=cv.take(W5, BF16),
                    U=cv.take(W5, F32), last=cv.take(4, F32), B=Buf(), B2=Buf()) for _ in range(3)]
        vnew = cv.take(128, BF16)
        S_b = cv.take(128, BF16)
        tt = cv.take(128, F32)
        junk = cv.take(128, F32)
        obt = cv.take(128, BF16)
        smc = cv.take(8, F32)
        preB, accB, sqbB, rnB, qTB, kTB, vTB, kkB, vvB = (Buf() for _ in range(9))
        szB = [Buf(), Buf()]
        TgB, DB, DTB, egB, PaB, PTaB, ZaB, ZbB, RwB, vbB, smB = (Buf() for _ in range(11))
        vnB, SbB, ttB, jkB, obB, scB = (Buf() for _ in range(6))
        gl4, t4, egp4, ekd4, bege4 = (sm4[:, i * 4:(i + 1) * 4] for i in range(5))
        ssq, rms = smc[:, 0:1], smc[:, 1:2]

        def v4(ap):
            return ap.rearrange("p (i f) -> p i f", i=4)

        def bc1(ap):
            return ap.unsqueeze(1).to_broadcast([128, 4, 128])

        def bc2(ap):
            return ap.unsqueeze(2).to_broadcast([128, 4, 128])

        load_w("gpsimd", wba[:], wBA_d[l], wbasem, wbaB)
        ps, pb = big()
        for blk in range(8):
            for k in range(8):
                mm(ps[:, blk * 16:(blk + 1) * 16], u_(k, blk * 128, (blk + 1) * 128), wba[:, k * 16:(k + 1) * 16], k == 0, k == 7,
                   [wbaB, uB[k][blk // 4]], [pb])
        psv = ps[:, 0:128].rearrange("p (b j) -> p b j", j=16)
        b3 = bet[:].rearrange("p (b j) -> p b j", j=8)
        nb3 = nbet[:].rearrange("p (b j) -> p b j", j=8)
        g3 = gtk[:].rearrange("p (b j) -> p b j", j=8)
        gt3 = gtmp[:].rearrange("p (b j) -> p b j", j=8)
        act(b3, psv[:, :, 0:8], AF.Sigmoid, [pb], [bgB])
        V(lambda e: e.tensor_scalar(out=nbet[:], in0=bet[:], scalar1=-1.0, scalar2=None, op0=ALU.mult), [bgB], [bgB])
        V(lambda e: e.tensor_tensor(out=gt3, in0=psv[:, :, 8:16], in1=ppc("dt_bias", l, 0, 8).unsqueeze(1).to_broadcast([128, 8, 8]),
                                    op=ALU.add), [pb, ppB], [bgB])
        act(gtmp[:], gtmp[:], AF.Exp, [bgB], [bgB])
        act(gtmp[:], gtmp[:], AF.Ln, [bgB], [bgB], bias=one_c, scale=1.0)
        act(nega[:], ppc("a_log", l, 0, 8), AF.Exp, [ppB], [bgB])
        V(lambda e: e.scalar_tensor_tensor(out=g3, in0=gt3, scalar=-1.0, in1=nega[:].unsqueeze(1).to_broadcast([128, 8, 8]),
                                           op0=ALU.mult, op1=ALU.mult), [bgB], [bgB])
        normw = ppc("normw", l, 0, 128)
        cwo, _ = PO["convw"]

        def preamble(h):
            s = h % 2
            if h == 0:
                load_w("gpsimd", wAs[:, 0:4096], wD_d[l, 0], wAsem[0], wAB[0])
            if h + 1 < 8:
                load_w("gpsimd", wAs[:, (1 - s) * 4096:(2 - s) * 4096], wD_d[l, h + 1], wAsem[1 - s], wAB[1 - s])
            Wb = s * 4096

            def W(k, a, b):
                return wAs[:, Wb + k * 512 + a:Wb + k * 512 + b]
            for X in range(3):
                cc = ccar[:, (h * 3 + X) * 3:(h * 3 + X) * 3 + 3]
                if hf == 0:
                    V(lambda e: e.memset(pre[:, 0:3], 0.0), [], [preB])
                else:
                    V(lambda e, cc=cc: e.tensor_copy(out=pre[:, 0:3], in_=cc), [ccarB[h]], [preB])

                def cw(j, X=X, h=h):
                    o = cwo + l * 96 + j * 24 + X * 8 + h
                    return ppt[:, o:o + 1]
                for j in range(4):
                    V(lambda e, j=j, cw=cw: e.tensor_scalar(out=dg[:, j * 128:(j + 1) * 128], in0=ident_b, scalar1=cw(j), scalar2=None, op0=ALU.mult),
                      [cbfB, ppB], [dgB])
                for t in range(2):
                    ps, pb = big()
                    for k in range(8):
                        mm(ps[:, :], W(k, X * 128, (X + 1) * 128), u_(k, t * 512, (t + 1) * 512), k == 0, k == 7, [wAB[s], uB[k][t]], [pb])
                    act(pre[:, 3 + t * 512:3 + (t + 1) * 512], ps[:, :], AF.Copy, [pb], [preB])
                    yield
                if hf == 0:
                    V(lambda e, cc=cc: e.tensor_copy(out=cc, in_=pre[:, 1024:1027]), [preB], [ccarB[h]])
                for t in range(2):
                    ps, pb = big()
                    for j in range(4):
                        mm(ps[:, :], dg[:, j * 128:(j + 1) * 128], pre[:, t * 512 + j:t * 512 + j + 512], j == 0, j == 3, [dgB, preB], [pb])
                    act(acc[:, t * 512:(t + 1) * 512], ps[:, :], AF.Silu, [pb], [accB])
                    yield
                if X < 2:
                    for t in range(2):
                        at_ = acc[:, t * 512:(t + 1) * 512]
                        act(sqb, at_, AF.Square, [accB], [sqbB])
                        ps, pb = big()
                        mm(ps[:, :], ones_b, sqb, True, True, [sqbB, cbfB], [pb])
                        act(rn, ps[:, :], AF.Ln, [pb], [rnB], bias=eps_rms, scale=1.0)
                        act(rn, rn, AF.Exp, [rnB], [rnB], scale=-0.5)
                        if X == 0:
                            V(lambda e, at_=at_, t=t: e.scalar_tensor_tensor(out=qT[:, t * 512:(t + 1) * 512], in0=at_, scalar=128 ** -0.5, in1=rn,
                                                                             op0=ALU.mult, op1=ALU.mult), [accB, rnB], [qTB])
                        else:
                            V(lambda e, at_=at_, t=t: e.tensor_tensor(out=kT[:, t * 512:(t + 1) * 512], in0=at_, in1=rn, op=ALU.mult),
                              [accB, rnB], [kTB])
                        yield
                else:
                    for t in range(2):
                        G(lambda e, t=t: e.tensor_copy(out=vT[:, t * 512:(t + 1) * 512], in_=acc[:, t * 512:(t + 1) * 512]), [accB], [vTB])
                    yield
            for h2 in range(2):
                ps, pb = big()
                for b4 in range(4):
                    blk = h2 * 4 + b4
                    for k in range(8):
                        mm(ps[:, b4 * 128:(b4 + 1) * 128], u_(k, blk * 128, (blk + 1) * 128), W(k, 384, 512), k == 0, k == 7,
                           [wAB[s], uB[k][blk // 4]], [pb])
                act(sz[s][:, h2 * 512:(h2 + 1) * 512], ps[:, :], AF.Silu, [pb], [szB[s]])
                yield
            for src, srcB, dst, dstB in ((kT, kTB, kk, kkB), (vT, vTB, vv, vvB)):
                for h2 in range(2):
                    pt4, pt4B = tslot4()
                    for b4 in range(4):
                        blk = h2 * 4 + b4
                        T(lambda e, pt4=pt4, b4=b4, src=src, blk=blk: e.transpose(pt4[:, b4 * 128:(b4 + 1) * 128],
                                                                                  src[:, blk * 128:(blk + 1) * 128], ident_b),
                          [srcB, cbfB], pt4B)
                    act(dst[:, h2 * 512:(h2 + 1) * 512], pt4, AF.Copy, pt4B, [dstB])
                    yield

        def prescanA(h, t, gi):
            n0 = 4 * t
            a0 = t * 512
            bu = bun[gi % 3]
            abu = ab[gi % 2]
            abB = abu["B"]
            P0, Rw4, vb4 = abu["P0"], abu["Rw"], abu["vb"]
            gsel = g3[:, n0:n0 + 4, h]
            bsel = b3[:, n0:n0 + 4, h]
            nbsel = nb3[:, n0:n0 + 4, h]
            V(lambda e: e.tensor_tensor(out=v4(Tg4), in0=bc1(triinc), in1=bc2(gsel), op=ALU.mult), [cstB, bgB], [TgB])
            pgr, pgrB = big()
            mm(pgr[:, :], ones_f, Tg4, True, True, [cstB, TgB], [pgrB])
            pgd, pgdB = big()
            mm(pgd[:, :], negones[:], Tg4, True, False, [cbfB, TgB], [pgdB])
            for i in range(4):
                mm(pgd[:, i * 128:(i + 1) * 128], Tg4[:, i * 128:(i + 1) * 128], ones_f, False, i == 3, [cstB, TgB], [pgdB])
            yield
            V(lambda e: e.tensor_tensor(out=v4(D4), in0=v4(pgd[:, :]), in1=bc1(mls), op=ALU.add), [pgdB, cstB], [DB])
            act(D4, D4, AF.Exp, [DB], [DB])
            V(lambda e: e.tensor_tensor(out=v4(DT4), in0=bc1(mupi), in1=v4(pgd[:, :]), op=ALU.subtract), [pgdB, cstB], [DTB])
            act(DT4, DT4, AF.Exp, [DTB], [DTB])
            act(eg4, pgr[:, :], AF.Exp, [pgrB], [egB])
            yield
            pgr_l = v4(pgr[:, :])[:, :, 127]
            pgd_l = v4(pgd[:, :])[:, :, 127]
            act(gl4, pgr_l, AF.Copy, [pgrB], [smB])
            V(lambda e: e.tensor_tensor(out=t4, in0=pgd_l, in1=gl4, op=ALU.add), [pgdB, smB], [smB])
            act(egp4, t4, AF.Exp, [smB], [smB])
            act(ekd4, pgd_l, AF.Exp, [pgdB, smB], [smB], scale=-1.0)
            V(lambda e: e.tensor_tensor(out=bege4, in0=egp4, in1=bsel, op=ALU.mult), [smB, bgB], [smB])
            act(bu["last"], v4(eg4)[:, :, 127], AF.Copy, [egB], [bu["B"]])
            yield
            pkk, pkkB = big()
            for i in range(4):
                ks = kT[:, a0 + i * 128:a0 + (i + 1) * 128]
                mm(pkk[:, i * 128:(i + 1) * 128], ks, ks, True, True, [kTB], [pkkB])
            pqk, pqkB = big()
            for i in range(4):
                ks = kT[:, a0 + i * 128:a0 + (i + 1) * 128]
                mm(pqk[:, i * 128:(i + 1) * 128], ks, qT[:, a0 + i * 128:a0 + (i + 1) * 128], True, True, [kTB, qTB], [pqkB])
            yield
            V(lambda e: e.tensor_tensor(out=P0, in0=pkk[:, :], in1=D4, op=ALU.mult), [pkkB, DB], [abB])
            V(lambda e: e.tensor_tensor(out=v4(P0), in0=v4(P0), in1=bc2(nbsel), op=ALU.mult), [abB, bgB], [abB])
            V(lambda e: e.tensor_tensor(out=bu["intraT"], in0=pqk[:, :], in1=DT4, op=ALU.mult), [pqkB, DTB], [bu["B"]])
            yield
            G(lambda e: e.tensor_tensor(out=bu["qdT"], in0=qT[:, a0:a0 + 512], in1=eg4, op=ALU.mult), [qTB, egB], [bu["B"]])
            G(lambda e: e.tensor_tensor(out=v4(bu["kdec"]), in0=v4(kk[:, a0:a0 + 512]), in1=bc2(ekd4), op=ALU.mult), [kkB, smB], [bu["B"]])
            G(lambda e: e.tensor_tensor(out=v4(Rw4), in0=v4(kk[:, a0:a0 + 512]), in1=bc2(bege4), op=ALU.mult), [kkB, smB], [abB])
            G(lambda e: e.tensor_tensor(out=v4(vb4), in0=v4(vv[:, a0:a0 + 512]), in1=bc2(bsel), op=ALU.mult), [vvB, bgB], [abB])
            yield

        def prescanB(h, t, gi):
            bu = bun[gi % 3]
            abu = ab[gi % 2]
            abB = abu["B"]
            P0, Rw4, vb4 = abu["P0"], abu["Rw"], abu["vb"]
            ptp, ptpB = big()
            for i in range(4):
                T(lambda e, i=i: e.transpose(ptp[:, i * 128:(i + 1) * 128], P0[:, i * 128:(i + 1) * 128], ident_f), [abB, cstB], [ptpB])
            act(PTab[0], ptp[:, :], AF.Copy, [ptpB], [PTabB[0]])
            V(lambda e: e.tensor_tensor(out=v4(Za), in0=v4(ptp[:, :]), in1=bc1(ident_f), op=ALU.add), [ptpB, cstB], [ZaB])
            yield
            pend = None
            for s_ in range(1, 7):
                cur, prv = s_ % 2, (s_ - 1) % 2
                Pin, PinB = (P0, abB) if s_ == 1 else (Pab[prv], PabB[prv])
                PTin, PTinB = PTab[prv], PTabB[prv]
                pp_, ppB_ = big()
                for i in range(4):
                    sl = slice(i * 128, (i + 1) * 128)
                    mm(pp_[:, sl], PTin[:, sl], Pin[:, sl], True, True, [PTinB, PinB], [ppB_])
                if os.environ.get("K_FINE", "0") == "1":
                    yield
                if s_ < 6:
                    pq_, pqB_ = big()
                    for i in range(4):
                        sl = slice(i * 128, (i + 1) * 128)
                        mm(pq_[:, sl], Pin[:, sl], PTin[:, sl], True, True, [PTinB, PinB], [pqB_])
                if os.environ.get("K_FINE", "0") == "1":
                    yield
                if pend is not None:
                    pend()
                    if os.environ.get("K_FINE", "0") == "1":
                        yield
                act(Pab[cur], pp_[:, :], AF.Copy, [ppB_], [PabB[cur]])
                if s_ < 6:
                    V(lambda e, pq_=pq_, cur=cur: e.tensor_copy(out=PTab[cur], in_=pq_[:, :]), [pqB_], [PTabB[cur]])
                yield

                def zupd(cur=cur):
                    pz_, pzB_ = big()
                    for i in range(4):
                        sl = slice(i * 128, (i + 1) * 128)
                        mm(pz_[:, sl], Pab[cur][:, sl], Za[:, sl], True, True, [PabB[cur], ZaB], [pzB_])
                    V(lambda e, pz_=pz_: e.tensor_tensor(out=Za, in0=pz_[:, :], in1=Za, op=ALU.add), [pzB_, ZaB], [ZaB])
                pend = zupd
            pend()
            yield
            V(lambda e: e.tensor_copy(out=Zb4, in_=Za), [ZaB], [ZbB])
            pu, puB = big()
            for i in range(4):
                sl = slice(i * 128, (i + 1) * 128)
                mm(pu[:, sl], Zb4[:, sl], vb4[:, sl], True, True, [ZbB, abB], [puB])
            act(bu["U"], pu[:, :], AF.Copy, [puB], [bu["B2"]])
            pw, pwB = big()
            for i in range(4):
                sl = slice(i * 128, (i + 1) * 128)
                mm(pw[:, sl], Rw4[:, sl], Zb4[:, sl], True, True, [ZbB, abB], [pwB])
            act(bu["negWT"], pw[:, :], AF.Copy, [pwB], [bu["B2"]], scale=-1.0)
            yield

        def scan(h, t, gi):
            bu = bun[gi % 3]
            bB = bu["B"]
            b2 = bu["B2"]
            s = h % 2
            Sh = S_f[:, h * 128:(h + 1) * 128]
            if t == 0:
                if hf == 0:
                    V(lambda e: e.memset(Sh, 0.0), [], [S_fB[h]])
                V(lambda e: e.tensor_copy(out=S_b, in_=Sh), [S_fB[h]], [SbB])
            for i in range(4):
                n = 4 * t + i
                c0 = n * 128
                sl = slice(i * 128, (i + 1) * 128)
                p1, p1B = quarter()
                mm(p1, bu["negWT"][:, sl], S_b, True, True, [b2, SbB], [p1B])
                V(lambda e, p1=p1, sl=sl: e.tensor_tensor(out=vnew, in0=p1, in1=bu["U"][:, sl], op=ALU.add), [p1B, b2], [vnB])
                yield
                po_, poB = quarter()
                mm(po_, bu["qdT"][:, sl], S_b, True, False, [bB, SbB], [poB])
                mm(po_, bu["intraT"][:, sl], vnew, False, True, [bB, vnB], [poB])
                pS, pSB = quarter()
                mm(pS, bu["kdec"][:, sl], vnew, True, True, [bB, vnB], [pSB])
                V(lambda e, pS=pS, i=i: e.scalar_tensor_tensor(out=Sh, in0=Sh, scalar=bu["last"][:, i:i + 1], in1=pS,
                                                               op0=ALU.mult, op1=ALU.add), [S_fB[h], pSB, bB], [S_fB[h]])
                V(lambda e: e.tensor_copy(out=S_b, in_=Sh), [S_fB[h]], [SbB])
                yield
                V(lambda e: e.memset(ssq, 0.0), [], [scB])
                act(junk, po_, AF.Square, [poB, scB], [jkB, scB], accum_out=ssq)
                act(rms, ssq, AF.Ln, [scB], [scB], bias=eps_rms, scale=1.0 / 128)
                act(rms, rms, AF.Exp, [scB], [scB], scale=-0.5)
                V(lambda e, po_=po_: e.scalar_tensor_tensor(out=tt, in0=po_, scalar=rms, in1=normw, op0=ALU.mult, op1=ALU.mult),
                  [poB, scB, ppB], [ttB])
                yield
                G(lambda e, c0=c0: e.tensor_tensor(out=obt, in0=tt, in1=sz[s][:, c0:c0 + 128], op=ALU.mult), [ttB, szB[s]], [obB])
                pt2, pt2B = tslot()
                T(lambda e, pt2=pt2: e.transpose(pt2, obt, ident_b), [obB, cbfB], [pt2B])
                act(oT[:, h * 1024 + c0:h * 1024 + c0 + 128], pt2, AF.Copy, [pt2B], [oB[h][n // 4]])
                yield

        groups = [(h, t) for h in range(8) for t in range(2)]
        NG = len(groups)

        def a_stream(gi):
            if gi >= NG:
                return None
            h, t = groups[gi]
            if t == 0:
                return chain(preamble(h), prescanA(h, t, gi))
            return prescanA(h, t, gi)
        interleave(a_stream(0))
        SR = int(os.environ.get("K_SR", "1"))
        for gi in range(NG + 1):
            gens, rngs, reps = [], [], []
            if gi >= 1:
                gens.append(scan(groups[gi - 1][0], groups[gi - 1][1], gi - 1))
                rngs.append((6, 8))
                reps.append(SR)
            if gi < NG:
                gens.append(prescanB(groups[gi][0], groups[gi][1], gi))
                rngs.append((3, 6))
                reps.append(int(os.environ.get("K_BR", "1")))
            nxt = a_stream(gi + 1)
            if nxt is not None:
                gens.append(nxt)
                rngs.append((0, 3))
                reps.append(int(os.environ.get("K_AR", "1")))
            interleave(*gens, rngs=rngs, reps=reps)

    for l in range(L):
        if stop == "p0":
            break
        for hf in range(2):
            phase_u(l, hf, 0)
            if stop == "u":
                break
            phase_attn(l, hf)
            P.barrier(allW)
            if stop == "attn":
                break
            phase_merge(l, hf, 0, False)
            P.barrier(allW)
            if stop == "ma":
                break
            phase_dn(l, hf)
            P.barrier(allW)
            if stop == "dn":
                break
            phase_merge(l, hf, 1, True)
            P.barrier(allW)
            if stop == "ln1":
                continue
            phase_ffn(l, hf)
            P.barrier(allW)
        if stop is not None and stop != "ln1":
            break

    osem = P.dsem("out")
    evs = []
    for c in range(8):
        evs.append(P.dma("sync", lambda e, c=c: e.dma_start(out=yT_d[:, c * SEQ:(c + 1) * SEQ], in_=xT[:, c * SEQ:(c + 1) * SEQ]),
                         osem, reads=xB[c]))
    if dbg:
        dsm = P.dsem("dbg")
        evs.append(P.dma("sync", lambda e: e.dma_start(out=dbg_d, in_=oT[:]), dsm, reads=[b for row in oB for b in row]))
    P.wait_events("sync", evs)
    P.emit()
    return nc


def _kp(w):
    n = w.shape[1]
    return np.ascontiguousarray(w.reshape(8, 128, n).transpose(1, 0, 2))


def _colT(v, nch):
    return np.ascontiguousarray(v.reshape(nch, 128).T)


def pack_shared(inp, L):
    PO, NPP = pp_off(L)
    pp = np.zeros((128, NPP), np.float32)

    def put(name, l, arr):
        o, per = PO[name]
        pp[:, o + l * per:o + (l + 1) * per] = arr
    wada = np.zeros((L, 8, 128, 6144), np.float32)
    wA = np.zeros((L, 4, 128, 8 * 448), np.float32)
    wD = np.zeros((L, 8, 128, 8 * 512), np.float32)
    wBA = np.zeros((L, 128, 128), np.float32)
    wM = np.zeros((L, 5, 128, 8192), np.float32)
    wF = np.zeros((L, 8, 128, 8192), np.float32)
    for l in range(L):
        put("b_ada", l, _colT(inp["b_ada"][l], 48))
        for nm in ("ln1_g", "ln1_b", "ln2_g", "ln2_b", "b_ff2"):
            put(nm, l, _colT(inp[nm][l], 8))
        put("b_ff1", l, _colT(inp["b_ff1"][l], 32))
        cw = inp["conv_w"][l]
        put("convw", l, cw.reshape(4, 24, 128).transpose(2, 0, 1).reshape(128, 96))
        put("sinks", l, np.broadcast_to(inp["sinks"][l][None, :], (128, 16)))
        put("a_log", l, np.broadcast_to(inp["a_log"][l][None, :], (128, 8)))
        put("dt_bias", l, np.broadcast_to(inp["dt_bias"][l][None, :], (128, 8)))
        put("normw", l, np.broadcast_to(inp["dn_norm_w"][l][None, :], (128, 128)))
        wa = _kp(inp["w_ada"][l])
        for piece in range(8):
            wada[l, piece] = wa[:, :, piece * 768:(piece + 1) * 768].reshape(128, 6144)
        wi = _kp(inp["w_in"][l])
        for g in range(4):
            q = wi[:, :, g * 256:(g + 1) * 256]
            k = wi[:, :, 1024 + g * 64:1024 + (g + 1) * 64]
            v = wi[:, :, 1280 + g * 64:1280 + (g + 1) * 64]
            wA[l, g] = np.concatenate([q, k, k, v], axis=2).reshape(128, 8 * 448)
        for h in range(8):
            parts = [wi[:, :, 1536 + X * 1024 + h * 128:1536 + X * 1024 + (h + 1) * 128] for X in range(4)]
            wD[l, h] = np.concatenate(parts, axis=2).reshape(128, 8 * 512)
        wBA[l] = wi[:, :, 5632:5648].reshape(128, 128)
        wM[l, 0] = _kp(inp["w_oa"][l]).reshape(128, 8192)
        wM[l, 1] = wi[:, :, 5648:6672].reshape(128, 8192)
        wM[l, 2] = _kp(inp["w_ob"][l]).reshape(128, 8192)
        wM[l, 3] = wi[:, :, 6672:7696].reshape(128, 8192)
        wM[l, 4] = _kp(inp["w_out"][l]).reshape(128, 8192)
        w1 = _kp(inp["w_ff1"][l])
        w2 = inp["w_ff2"][l]
        for F in range(8):
            wF[l, F, :, 0:4096] = w1[:, :, F * 512:(F + 1) * 512].reshape(128, 4096)
            blk = w2[F * 512:(F + 1) * 512, :].reshape(4, 128, 1024).transpose(1, 0, 2)
            wF[l, F, :, 4096:8192] = blk.reshape(128, 4096)
    p = np.arange(128)[:, None]
    f = np.arange(128)[None, :]
    cst = np.concatenate([
        (p == f).astype(np.float32), np.ones((128, 128), np.float32), (p <= f).astype(np.float32),
        np.where(p > f, 0.0, NEG).astype(np.float32), np.where(f >= p, 0.0, NEG).astype(np.float32),
        (p > f).astype(np.float32)], axis=1)
    return dict(pp=pp, cst=cst, wada=wada, wA=wA, wD=wD, wBA=wBA, wM=wM, wF=wF)


def make_in_maps(inp, L, ncores):
    shared = pack_shared(inp, L)
    maps = []
    for b in range(ncores):
        m = dict(shared)
        pp = shared["pp"].copy()
        pp[:, 0:8] = _colT(inp["c"][b], 8)
        m["pp"] = pp
        m["xT"] = np.ascontiguousarray(inp["x"][b].T.reshape(8, 128, SEQ).transpose(1, 0, 2)).reshape(128, 8 * SEQ)
        maps.append(m)
    return maps


def unpack_out(yT):
    return np.ascontiguousarray(yT.reshape(128, 8, SEQ).transpose(2, 1, 0).reshape(SEQ, D))


_NC_CACHE = {}


def kernel(**inputs):
    inp = {k: np.asarray(v, dtype=np.float32) for k, v in inputs.items()}
    ncores = 8
    if "nc" not in _NC_CACHE:
        _NC_CACHE["nc"] = build(DEPTH)
    nc = _NC_CACHE["nc"]
    maps = make_in_maps(inp, DEPTH, ncores)
    res = run_bass_kernel_spmd(nc, maps, core_ids=list(range(ncores)))
    out = np.stack([unpack_out(res.results[b]["yT"]) for b in range(ncores)], axis=0)
    return out.astype(np.float32)
```
